# Optimizing a Trainium2 kernel written in Bass

```python
import math
import jax, jax.numpy as jnp
from jax import lax
import numpy as np

D_MODEL = 1024
BATCH = 8
SEQ = 4096
DEPTH = 2

N_EVEN = (DEPTH + 1) // 2
N_ODD = DEPTH // 2
CHUNK = 64
CONV_W = 4
A_HEADS = 4
A_DK = 128
A_DV = D_MODEL // 2 // A_HEADS
A_KW = A_HEADS * A_DK
A_VW = A_HEADS * A_DV
B_HEADS = 4
B_DH = D_MODEL // 2 // B_HEADS
B_WIDTH = B_HEADS * B_DH
QK_BLOCK = 4
AB_SIZES = (A_KW, A_KW, A_VW, A_VW, B_WIDTH, B_WIDTH, B_WIDTH, 2 * B_HEADS)
AB_IN = 2 * A_KW + 2 * A_VW + 3 * B_WIDTH + 2 * B_HEADS
AB_MIX = A_VW + B_WIDTH
D_RNN = D_MODEL
C_BLOCKS = 8
C_BLOCK = D_RNN // C_BLOCKS
RG_C = 8.0
D_FF = 256 * math.ceil(8 * D_MODEL / 3 / 256)
FFN_RES_W = 0.5
ALPHA = (2 * DEPTH) ** 0.25
BETA = (8 * DEPTH) ** -0.25
N_ADA = 9

kernel_name = 'hybrid_hgrn2_mlstm_rglru_macaron_deepnorm'


def layer_norm(x, g, b, eps=1e-5):
    xf = x.astype(jnp.float32)
    mu = jnp.mean(xf, -1, keepdims=True)
    var = jnp.mean(jnp.square(xf - mu), -1, keepdims=True)
    return ((xf - mu) * lax.rsqrt(var + eps) * g + b).astype(x.dtype)


def head_norm(x, g, n_heads, center, eps=1e-6):
    bsz, s, w = x.shape
    xf = x.astype(jnp.float32).reshape(bsz, s, n_heads, w // n_heads)
    if center:
        xf = xf - jnp.mean(xf, -1, keepdims=True)
    xf = xf * lax.rsqrt(jnp.mean(jnp.square(xf), -1, keepdims=True) + eps)
    return (xf.reshape(bsz, s, w) * g).astype(x.dtype)


def causal_conv(x, w, b):
    ch = x.shape[-1]
    y = lax.conv_general_dilated(x, w[:, None, :], window_strides=(1,), padding=[(CONV_W - 1, 0)],
                                 dimension_numbers=('NWC', 'WIO', 'NWC'), feature_group_count=ch)
    return y + b


def to_chunks(x, n_heads):
    bsz, s, w = x.shape
    return x.reshape(bsz, s // CHUNK, CHUNK, n_heads, w // n_heads).transpose(1, 0, 3, 2, 4)


def gate_to_chunks(g):
    bsz, s, h = g.shape
    return g.reshape(bsz, s // CHUNK, CHUNK, h).transpose(1, 0, 3, 2)


def from_chunks(y):
    n, bsz, h, cl, d = y.shape
    return y.transpose(1, 0, 3, 2, 4).reshape(bsz, n * cl, h * d)


def hgrn2_scan(q, k, v, logf):
    _, bsz, h, cl, dk = q.shape
    dv = v.shape[-1]
    mask = jnp.tril(jnp.ones((cl, cl), bool))[:, :, None]

    def step(state, inp):
        q_c, k_c, v_c, lf = inp
        b = jnp.cumsum(lf, axis=2)
        inter = jnp.einsum('bhtk,bhkv->bhtv', q_c * jnp.exp(b), state)
        decay = jnp.exp(jnp.where(mask, b[:, :, :, None, :] - b[:, :, None, :, :], -jnp.inf))
        att = jnp.einsum('bhtk,bhsk,bhtsk->bhts', q_c, k_c, decay)
        intra = jnp.einsum('bhts,bhsv->bhtv', att, v_c)
        b_end = b[:, :, -1:, :]
        new_state = jnp.exp(b_end[:, :, 0])[..., None] * state + jnp.einsum(
            'bhsk,bhsv->bhkv', k_c * jnp.exp(b_end - b), v_c)
        return new_state, inter + intra

    s0 = jnp.zeros((bsz, h, dk, dv), jnp.float32)
    _, o = lax.scan(step, s0, (q, k, v, logf))
    return o


def mlstm_scan(q, k, v, logi, logf):
    _, bsz, h, cl, dk = q.shape
    dv = v.shape[-1]
    mask = jnp.tril(jnp.ones((cl, cl), bool))

    def step(carry, inp):
        c_st, n_st, m_st = carry
        q_c, k_c, v_c, li, lf = inp
        b = jnp.cumsum(lf, axis=-1)
        d_mat = jnp.where(mask, b[..., :, None] - b[..., None, :] + li[..., None, :], -jnp.inf)
        g_inter = b + m_st[..., None]
        m_t = jnp.maximum(g_inter, jnp.max(d_mat, -1))
        w_inter = jnp.exp(g_inter - m_t)
        aw = jnp.exp(d_mat - m_t[..., None]) * jnp.einsum('bhtd,bhsd->bhts', q_c, k_c)
        num = w_inter[..., None] * jnp.einsum('bhtd,bhde->bhte', q_c, c_st) + jnp.einsum('bhts,bhse->bhte', aw, v_c)
        den = w_inter * jnp.einsum('bhtd,bhd->bht', q_c, n_st) + jnp.sum(aw, -1)
        h_out = num / jnp.maximum(jnp.abs(den), jnp.exp(-m_t))[..., None]
        g_state = b[..., -1] + m_st
        s_w = b[..., -1:] - b + li
        m_new = jnp.maximum(g_state, jnp.max(s_w, -1))
        w_s = jnp.exp(s_w - m_new[..., None])
        dec = jnp.exp(g_state - m_new)
        c_new = dec[..., None, None] * c_st + jnp.einsum('bhs,bhsd,bhse->bhde', w_s, k_c, v_c)
        n_new = dec[..., None] * n_st + jnp.einsum('bhs,bhsd->bhd', w_s, k_c)
        return (c_new, n_new, m_new), h_out

    init = (jnp.zeros((bsz, h, dk, dv), jnp.float32), jnp.zeros((bsz, h, dk), jnp.float32),
            jnp.zeros((bsz, h), jnp.float32))
    _, hs = lax.scan(step, init, (q, k, v, logi, logf))
    return hs


def hgrn2_mlstm_mixer(t, lb, w_in, w_out, hgrn_g, conv_w, conv_b, wq, wk, gate_b, skip, mnorm_g):
    bsz, s, _ = t.shape
    f32 = jnp.float32
    idx = list(np.cumsum(AB_SIZES)[:-1])
    a_q, a_f, a_i, a_g, b_x, b_v, b_z, b_gate = jnp.split(t @ w_in, idx, axis=-1)
    f = lb + (1.0 - lb) * jax.nn.sigmoid(a_f.astype(f32))
    q_a = jax.nn.silu(a_q.astype(f32))
    o_a = from_chunks(hgrn2_scan(to_chunks(q_a, A_HEADS), to_chunks(1.0 - f, A_HEADS),
                                 to_chunks(a_i.astype(f32), A_HEADS), to_chunks(jnp.log(f), A_HEADS)))
    y_a = head_norm(o_a.astype(t.dtype), hgrn_g, A_HEADS, center=False) * jax.nn.silu(a_g)
    xc = jax.nn.silu(causal_conv(b_x, conv_w, conv_b))
    xcb = xc.reshape(bsz, s, B_WIDTH // QK_BLOCK, QK_BLOCK)
    q_b = jnp.einsum('bsnj,nij->bsni', xcb, wq).reshape(bsz, s, B_WIDTH).astype(f32)
    k_b = (jnp.einsum('bsnj,nij->bsni', xcb, wk).reshape(bsz, s, B_WIDTH) * B_DH ** -0.5).astype(f32)
    gates = b_gate.astype(f32) + gate_b
    logi = gates[..., :B_HEADS]
    logf = jax.nn.log_sigmoid(gates[..., B_HEADS:])
    h_b = from_chunks(mlstm_scan(to_chunks(q_b, B_HEADS), to_chunks(k_b, B_HEADS),
                                 to_chunks(b_v.astype(f32), B_HEADS), gate_to_chunks(logi), gate_to_chunks(logf)))
    y_b = (head_norm(h_b.astype(t.dtype), mnorm_g, B_HEADS, center=True) + skip * xc) * jax.nn.silu(b_z)
    return jnp.concatenate([y_a, y_b], axis=-1) @ w_out


def rglru_mixer(t, w_in, conv_w, conv_b, wa, ba, wx, bx, lam, w_out):
    bsz, s, _ = t.shape
    f32 = jnp.float32
    y_br, x_br = jnp.split(t @ w_in, 2, axis=-1)
    gate = jax.nn.gelu(y_br)
    xr = causal_conv(x_br, conv_w, conv_b)
    xb = xr.reshape(bsz, s, C_BLOCKS, C_BLOCK)
    r = jax.nn.sigmoid((jnp.einsum('bsnj,nij->bsni', xb, wa).reshape(bsz, s, D_RNN) + ba).astype(f32))
    i = jax.nn.sigmoid((jnp.einsum('bsnj,nij->bsni', xb, wx).reshape(bsz, s, D_RNN) + bx).astype(f32))
    log_a = -RG_C * r * jax.nn.softplus(-lam.astype(f32))
    a = jnp.exp(log_a)
    u = jnp.sqrt(-jnp.expm1(2.0 * log_a)) * i * xr.astype(f32)

    def combine(left, right):
        a1, b1 = left
        a2, b2 = right
        return a1 * a2, a2 * b1 + b2

    _, hs = lax.associative_scan(combine, (a, u), axis=1)
    return (hs.astype(t.dtype) * gate) @ w_out


def swiglu(t, w1, w3, w2):
    return (jax.nn.silu(t @ w1) * (t @ w3)) @ w2


def residual_sublayer(x, mod, fn, weight, g, b):
    shift, scale, gate = mod[:, 0, None, :], mod[:, 1, None, :], mod[:, 2, None, :]
    y = fn(x * (1.0 + scale) + shift)
    return layer_norm(ALPHA * x + weight * (1.0 + gate) * y, g, b)


def setup_inputs(seed: int = 0) -> dict:
    key = jax.random.key(seed)
    ks = jax.random.split(key, 32)
    f32 = jnp.float32

    def nrm(k, shape, scale):
        return jax.random.normal(k, shape, f32) * scale

    d = D_MODEL
    u = jax.random.uniform(ks[28], (N_ODD, D_RNN), f32, minval=0.9, maxval=0.999)
    p = u ** (1.0 / RG_C)
    return {
        'x': nrm(ks[0], (BATCH, SEQ, d), 1.0),
        'c': nrm(ks[1], (BATCH, d), 1.0),
        'ada_w': nrm(ks[2], (DEPTH, d, N_ADA * d), 0.1 * d ** -0.5),
        'ada_b': nrm(ks[3], (DEPTH, N_ADA * d), 0.02),
        'ln_g': 1.0 + nrm(ks[4], (DEPTH, 3, d), 0.02),
        'ln_b': nrm(ks[5], (DEPTH, 3, d), 0.02),
        'ffn_w1': nrm(ks[6], (DEPTH, 2, d, D_FF), d ** -0.5),
        'ffn_w3': nrm(ks[7], (DEPTH, 2, d, D_FF), d ** -0.5),
        'ffn_w2': nrm(ks[8], (DEPTH, 2, D_FF, d), BETA * D_FF ** -0.5),
        'hgrn_lb_logits': nrm(ks[9], (DEPTH + 1, A_KW), 0.1),
        'ab_w_in': nrm(ks[10], (N_EVEN, d, AB_IN), d ** -0.5),
        'ab_w_out': nrm(ks[11], (N_EVEN, AB_MIX, d), BETA * AB_MIX ** -0.5),
        'hgrn_norm_g': 1.0 + nrm(ks[12], (N_EVEN, A_VW), 0.02),
        'mlstm_conv_w': nrm(ks[13], (N_EVEN, CONV_W, B_WIDTH), CONV_W ** -0.5),
        'mlstm_conv_b': nrm(ks[14], (N_EVEN, B_WIDTH), 0.02),
        'mlstm_wq': nrm(ks[15], (N_EVEN, B_WIDTH // QK_BLOCK, QK_BLOCK, QK_BLOCK), QK_BLOCK ** -0.5),
        'mlstm_wk': nrm(ks[16], (N_EVEN, B_WIDTH // QK_BLOCK, QK_BLOCK, QK_BLOCK), QK_BLOCK ** -0.5),
        'mlstm_gate_b': jnp.concatenate([nrm(ks[17], (N_EVEN, B_HEADS), 0.1),
                                         jnp.linspace(3.0, 6.0, B_HEADS, dtype=f32)[None, :]
                                         + nrm(ks[18], (N_EVEN, B_HEADS), 0.1)], axis=-1),
        'mlstm_skip': 1.0 + nrm(ks[19], (N_EVEN, B_WIDTH), 0.02),
        'mlstm_norm_g': 1.0 + nrm(ks[20], (N_EVEN, B_WIDTH), 0.02),
        'rglru_w_in': nrm(ks[21], (N_ODD, d, 2 * D_RNN), d ** -0.5),
        'rglru_conv_w': nrm(ks[22], (N_ODD, CONV_W, D_RNN), CONV_W ** -0.5),
        'rglru_conv_b': nrm(ks[23], (N_ODD, D_RNN), 0.02),
        'rglru_wa': nrm(ks[24], (N_ODD, C_BLOCKS, C_BLOCK, C_BLOCK), C_BLOCK ** -0.5),
        'rglru_ba': nrm(ks[25], (N_ODD, D_RNN), 0.02),
        'rglru_wx': nrm(ks[26], (N_ODD, C_BLOCKS, C_BLOCK, C_BLOCK), C_BLOCK ** -0.5),
        'rglru_bx': nrm(ks[27], (N_ODD, D_RNN), 0.02),
        'rglru_lambda': jnp.log(p) - jnp.log1p(-p),
        'rglru_w_out': nrm(ks[29], (N_ODD, D_RNN, d), BETA * D_RNN ** -0.5),
    }


def reference(x, c, ada_w, ada_b, ln_g, ln_b, ffn_w1, ffn_w3, ffn_w2, hgrn_lb_logits,
              ab_w_in, ab_w_out, hgrn_norm_g, mlstm_conv_w, mlstm_conv_b, mlstm_wq, mlstm_wk,
              mlstm_gate_b, mlstm_skip, mlstm_norm_g, rglru_w_in, rglru_conv_w, rglru_conv_b,
              rglru_wa, rglru_ba, rglru_wx, rglru_bx, rglru_lambda, rglru_w_out):
    bsz = x.shape[0]
    lb_all = jnp.cumsum(jax.nn.softmax(hgrn_lb_logits.astype(jnp.float32), axis=0), axis=0)
    c_act = jax.nn.silu(c)
    for layer in range(DEPTH):
        ada = (c_act @ ada_w[layer] + ada_b[layer]).reshape(bsz, 3, 3, D_MODEL)
        x = residual_sublayer(x, ada[:, 0], lambda t: swiglu(t, ffn_w1[layer, 0], ffn_w3[layer, 0], ffn_w2[layer, 0]),
                              FFN_RES_W, ln_g[layer, 0], ln_b[layer, 0])
        e = layer // 2
        if layer % 2 == 0:
            mixer = lambda t: hgrn2_mlstm_mixer(t, lb_all[layer], ab_w_in[e], ab_w_out[e], hgrn_norm_g[e],
                                                mlstm_conv_w[e], mlstm_conv_b[e], mlstm_wq[e], mlstm_wk[e],
                                                mlstm_gate_b[e], mlstm_skip[e], mlstm_norm_g[e])
        else:
            mixer = lambda t: rglru_mixer(t, rglru_w_in[e], rglru_conv_w[e], rglru_conv_b[e], rglru_wa[e],
                                          rglru_ba[e], rglru_wx[e], rglru_bx[e], rglru_lambda[e], rglru_w_out[e])
        x = residual_sublayer(x, ada[:, 1], mixer, 1.0, ln_g[layer, 1], ln_b[layer, 1])
        x = residual_sublayer(x, ada[:, 2], lambda t: swiglu(t, ffn_w1[layer, 1], ffn_w3[layer, 1], ffn_w2[layer, 1]),
                              FFN_RES_W, ln_g[layer, 2], ln_b[layer, 2])
    return x
```

```python
from contextlib import ExitStack
import numpy as np
import concourse.bass as bass
import concourse.mybir as mybir
from concourse.bass_utils import run_bass_kernel_spmd

F32, BF16 = mybir.dt.float32, mybir.dt.bfloat16
AF = mybir.ActivationFunctionType
ALU = mybir.AluOpType

D = 1024
S = 4096
NB = 8
DFF = 2816
NFC = DFF // 128
NT = S // 128
ALPHA = 4.0 ** 0.25
AB_IN = 3592
LN_EPS = 1e-5


class Tok:
    __slots__ = ("sem", "val", "epoch")

    def __init__(self, sem, val, epoch):
        self.sem, self.val, self.epoch = sem, val, epoch


class DSem:
    def __init__(self, K, sem):
        self.K = K
        self.sem = sem
        self.n = 0

    def tok(self):
        return Tok(self.sem, self.n, self.K.epoch)


class Eng:
    def __init__(self, K, name):
        self.K, self.name = K, name
        self.eng = getattr(K.nc, name)
        self.sem = None
        self.n = 0
        self.waited = {}

    def wait(self, t, force=False):
        if t is None:
            return
        if isinstance(t, (list, tuple)):
            for u in t:
                self.wait(u, force)
            return
        if t.epoch < self.K.epoch and not force:
            return
        if t.val <= 0:
            return
        key = id(t.sem)
        if t.sem is self.sem and not force:
            pass
        if self.waited.get(key, 0) >= t.val:
            return
        self.waited[key] = t.val
        self.eng.wait_ge(t.sem, t.val)

    def last(self):
        return Tok(self.sem, self.n, self.K.epoch)

    def op(self, meth, *a, deps=(), **kw):
        self.wait(deps)
        ins = getattr(self.eng, meth)(*a, **kw)
        self.n += 1
        ins.then_inc(self.sem, 1)
        return Tok(self.sem, self.n, self.K.epoch)

    def dma(self, dsem, out, in_, deps=(), **kw):
        self.wait(deps)
        self.eng.dma_start(out=out, in_=in_, **kw).then_inc(dsem.sem, 16)
        dsem.n += 16
        return dsem.tok()


class Kern:
    def __init__(self, nc, nsems):
        self.nc = nc
        self.nsem = 0
        self.pfx = "s0_"
        self.dsems = {False: [], True: []}
        self.dcur = {False: 0, True: 0}
        self.epoch = 0
        self.engs = {n: Eng(self, n) for n in ("tensor", "vector", "scalar", "gpsimd", "sync")}
        for e in self.engs.values():
            e.sem = self.new_sem()
        self.pe, self.dve, self.act, self.pool_e, self.sp = (self.engs[n] for n in
                                                             ("tensor", "vector", "scalar", "gpsimd", "sync"))

    def new_sem(self):
        self.nsem += 1
        return self.nc.alloc_semaphore("s%d" % self.nsem)

    def dsem(self, sw=False):
        lst = self.dsems[sw]
        if self.dcur[sw] == len(lst):
            lst.append(DSem(self, self.new_sem()))
        d = lst[self.dcur[sw]]
        self.dcur[sw] += 1
        return d

    def barrier(self, extra=()):
        toks = [e.last() for e in self.engs.values() if e.n > 0] + list(extra)
        self.epoch += 1
        self.dcur = {False: 0, True: 0}
        for e in self.engs.values():
            e.sem = self.new_sem()
            e.n = 0
            e.waited = {}
        for e in self.engs.values():
            for t in toks:
                e.wait(t, force=True)


INORDER_NO_SELF_WAIT = ("tensor",)


class _Part:
    __slots__ = ("w", "r", "ds")

    def __init__(self):
        self.w = None
        self.r = {}
        self.ds = None


class _Ref:
    __slots__ = ("parts",)

    def __init__(self, parts):
        self.parts = parts

    @property
    def ds(self):
        return self.parts[0].ds


class Buf:
    def __init__(self, K, t, dma=False, nparts=1):
        self.t = t
        self.parts = [_Part() for _ in range(nparts)]
        if dma:
            for pt in self.parts:
                pt.ds = K.dsem(sw=(dma == "sw"))
        self.ds = self.parts[0].ds

    def p(self, i):
        return _Ref([self.parts[i]])

    def __getitem__(self, k):
        return self.t[k]


def _deps(R, W, skip_sem=None):
    d = []
    for b in R:
        for pt in b.parts:
            if pt.w is not None:
                d.append(pt.w)
    for b in W:
        for pt in b.parts:
            if pt.w is not None:
                d.append(pt.w)
            d.extend(pt.r.values())
    if skip_sem is not None:
        d = [t for t in d if t.sem is not skip_sem]
    return d


def _mark(tok, R, W):
    for b in R:
        for pt in b.parts:
            pt.r[id(tok.sem)] = tok
    for b in W:
        for pt in b.parts:
            pt.w = tok
            pt.r = {}


_REC = [None]


def record(fn, *args):
    prev = _REC[0]
    _REC[0] = lst = []
    try:
        fn(*args)
    finally:
        _REC[0] = prev
    return lst


def emit_merged(a, b):
    na, nb = len(a), len(b)
    ia = ib = 0
    while ia < na or ib < nb:
        if ib >= nb or (ia < na and ia * nb <= ib * na):
            a[ia]()
            ia += 1
        else:
            b[ib]()
            ib += 1


def merge_lists(a, b):
    out, na, nb, ia, ib = [], len(a), len(b), 0, 0
    while ia < na or ib < nb:
        if ib >= nb or (ia < na and ia * nb <= ib * na):
            out.append(a[ia])
            ia += 1
        else:
            out.append(b[ib])
            ib += 1
    return out


def do(e, meth, *a, R=(), W=(), deps=(), **kw):
    if _REC[0] is not None:
        th = lambda: _do_now(e, meth, *a, R=R, W=W, deps=deps, **kw)
        if meth == "matmul" and kw.get("start") is False and _REC[0]:
            prev = _REC[0][-1]
            _REC[0][-1] = lambda prev=prev, th=th: (prev(), th())
        else:
            _REC[0].append(th)
        return None
    return _do_now(e, meth, *a, R=R, W=W, deps=deps, **kw)


def _do_now(e, meth, *a, R=(), W=(), deps=(), **kw):
    skip = e.sem if e.name in INORDER_NO_SELF_WAIT else None
    tok = e.op(meth, *a, deps=_deps(R, W, skip) + list(deps), **kw)
    _mark(tok, R, W)
    return tok


def dodma(e, out, in_, R=(), W=(), ds=None, deps=(), **kw):
    if _REC[0] is not None:
        _REC[0].append(lambda: _dodma_now(e, out, in_, R=R, W=W, ds=ds, deps=deps, **kw))
        return None
    return _dodma_now(e, out, in_, R=R, W=W, ds=ds, deps=deps, **kw)


def _dodma_now(e, out, in_, R=(), W=(), ds=None, deps=(), **kw):
    if ds is None:
        ds = W[0].ds
    tok = e.dma(ds, out, in_, deps=_deps(R, W) + list(deps), **kw)
    _mark(tok, R, W)
    return tok


class Ring:
    def __init__(self, K, es, name, n):
        self.b = [Buf(K, es.enter_context(K.nc.psum_tensor(K.pfx + "%s%d" % (name, i), [128, 512], F32))) for i in range(n)]
        self.i = 0

    def get(self):
        b = self.b[self.i]
        self.i = (self.i + 1) % len(self.b)
        return b


def stage_prep(K, c_ap, ada_w, ada_b, ada_scr):
    nc = K.nc
    pe, dve, act, pool, sp = K.pe, K.dve, K.act, K.pool_e, K.sp
    with ExitStack() as es:
        c_sb = es.enter_context(nc.sbuf_tensor(K.pfx + "p_c", [128, 8], F32))
        c_bf = es.enter_context(nc.sbuf_tensor(K.pfx + "p_cb", [128, 8], BF16))
        w0 = es.enter_context(nc.sbuf_tensor(K.pfx + "p_w0", [128, 8, 512], BF16))
        w1 = es.enter_context(nc.sbuf_tensor(K.pfx + "p_w1", [128, 8, 512], BF16))
        b0 = es.enter_context(nc.sbuf_tensor(K.pfx + "p_b0", [1, 512], F32))
        b1 = es.enter_context(nc.sbuf_tensor(K.pfx + "p_b1", [1, 512], F32))
        r0 = es.enter_context(nc.sbuf_tensor(K.pfx + "p_r0", [1, 512], F32))
        r1 = es.enter_context(nc.sbuf_tensor(K.pfx + "p_r1", [1, 512], F32))
        ps0 = es.enter_context(nc.psum_tensor(K.pfx + "p_ps0", [1, 512], F32))
        ps1 = es.enter_context(nc.psum_tensor(K.pfx + "p_ps1", [1, 512], F32))
        wb, bb, rb, psb = [w0, w1], [b0, b1], [r0, r1], [ps0, ps1]
        dw = [K.dsem(sw=True), K.dsem(sw=True)]
        db = [K.dsem(), K.dsem()]
        do = [K.dsem(), K.dsem()]
        dc = K.dsem()
        t = sp.dma(dc, c_sb[:], c_ap.rearrange("(p k) -> p k", k=8))
        t = act.op("activation", out=c_bf[:], in_=c_sb[:], func=AF.Silu, deps=[t])
        c_tok = t
        chunks = [(l, n) for l in range(2) for n in range(18)]
        mm_done = [None, None]
        ev_done = [None, None]
        for ci, (l, n) in enumerate(chunks):
            i = ci % 2
            wsrc = ada_w[l].rearrange("(p k) n -> p k n", k=8)[:, :, n * 512:(n + 1) * 512]
            tw = pool.dma(dw[i], wb[i][:], wsrc, deps=[mm_done[i]])
            tb = sp.dma(db[i], bb[i][:], ada_b[l:l + 1, n * 512:(n + 1) * 512], deps=[ev_done[i]])
            tm = None
            for k in range(8):
                tm = pe.op("matmul", psb[i][:], lhsT=c_bf[:, k:k + 1], rhs=wb[i][:, k, :],
                           start=(k == 0), stop=(k == 7),
                           deps=[tw, c_tok, ev_done[i]] if k == 0 else ())
            mm_done[i] = tm
            te = dve.op("tensor_tensor", out=rb[i][:], in0=psb[i][:], in1=bb[i][:], op=ALU.add,
                        deps=[tm, tb, do[i].tok() if do[i].n else None])
            ev_done[i] = te
            sp.dma(do[i], ada_scr[l:l + 1, n * 512:(n + 1) * 512], rb[i][:], deps=[te])
        return [do[0].tok(), do[1].tok()]


def load_mod_vectors(K, ada_scr, l, j, shiftT, scale1T, gate_bc, weight, dsem, lng_bc, lnb_bc, ln_g, ln_b):
    nc = K.nc
    sp, dve = K.sp, K.dve
    base = j * 3 * D
    row = ada_scr[l]
    toks = []
    toks.append(sp.dma(dsem, shiftT[:], row[base:base + D].rearrange("(k p) -> p k", p=128),
                       allow_slow_non_contiguous=True))
    toks.append(sp.dma(dsem, scale1T[:], row[base + D:base + 2 * D].rearrange("(k p) -> p k", p=128),
                       allow_slow_non_contiguous=True))
    toks.append(sp.dma(dsem, gate_bc[:], row[base + 2 * D:base + 3 * D].partition_broadcast(128)))
    toks.append(sp.dma(dsem, lng_bc[:], ln_g[l, j].partition_broadcast(128)))
    toks.append(sp.dma(dsem, lnb_bc[:], ln_b[l, j].partition_broadcast(128)))
    tall = dsem.tok()
    t1 = dve.op("tensor_scalar", out=scale1T[:], in0=scale1T[:], scalar1=1.0, scalar2=None, op0=ALU.add,
                deps=[tall])
    t2 = dve.op("tensor_scalar", out=gate_bc[:], in0=gate_bc[:], scalar1=1.0, scalar2=float(weight),
                op0=ALU.add, op1=ALU.mult, deps=[tall])
    return [t1, t2, tall]


def epilogue_ln(K, ypsA, ypsB, xe, tmp, stats, mv, gate_bc, lng_bc, lnb_bc, deps_y, deps_x, tmp_free, eps_ap):
    dve, act, pool = K.dve, K.act, K.pool_e
    ycons = []
    tl = None
    for hf, yps in enumerate((ypsA, ypsB)):
        sl = slice(hf * 512, (hf + 1) * 512)
        t = dve.op("tensor_tensor", out=tmp[:, sl], in0=yps[:], in1=gate_bc[:, sl], op=ALU.mult,
                   deps=[deps_y[hf], tmp_free])
        ycons.append(t)
        t = dve.op("scalar_tensor_tensor", out=xe[:, sl], in0=xe[:, sl], scalar=float(ALPHA), in1=tmp[:, sl],
                   op0=ALU.mult, op1=ALU.add, deps=[t, deps_x])
        t = dve.op("bn_stats", out=stats[:, hf * 6:(hf + 1) * 6], in_=xe[:, sl], deps=[t])
        tl = t
    t = dve.op("bn_aggr", out=mv[:, 0:2], in_=stats[:, 0:12], deps=[tl])
    t = act.op("activation", out=mv[:, 2:3], in_=mv[:, 1:2], func=AF.Sqrt, bias=eps_ap, scale=1.0, deps=[t])
    t = dve.op("reciprocal", out=mv[:, 2:3], in_=mv[:, 2:3], deps=[t])
    t = dve.op("scalar_tensor_tensor", out=mv[:, 3:4], in0=mv[:, 0:1], scalar=-1.0, in1=mv[:, 2:3],
               op0=ALU.mult, op1=ALU.mult, deps=[t])
    t = act.op("activation", out=xe[:], in_=xe[:], func=AF.Identity, bias=mv[:, 3:4], scale=mv[:, 2:3],
               deps=[t])
    t = pool.op("tensor_tensor", out=xe[:], in0=xe[:], in1=lng_bc[:], op=ALU.mult, deps=[t])
    t = pool.op("tensor_tensor", out=xe[:], in0=xe[:], in1=lnb_bc[:], op=ALU.add, deps=[t])
    return t, ycons


def stage_ffn(K, x_in, x_out, w1, w3, w2, ada_scr, l, j, ln_g, ln_b, ident_dram):
    nc = K.nc
    pe, dve, act, pool, sp = K.pe, K.dve, K.act, K.pool_e, K.sp
    NSUP = S // 512
    with ExitStack() as es:
        w1b = es.enter_context(nc.sbuf_tensor(K.pfx + "f_w1", [128, 8, DFF], BF16))
        w3b = es.enter_context(nc.sbuf_tensor(K.pfx + "f_w3", [128, 8, DFF], BF16))
        w2b = es.enter_context(nc.sbuf_tensor(K.pfx + "f_w2", [128, NFC, D], BF16))
        xa0 = es.enter_context(nc.sbuf_tensor(K.pfx + "f_xa0", [128, D], F32))
        xa1 = es.enter_context(nc.sbuf_tensor(K.pfx + "f_xa1", [128, D], F32))
        xe0 = es.enter_context(nc.sbuf_tensor(K.pfx + "f_xe0", [128, D], F32))
        xe1 = es.enter_context(nc.sbuf_tensor(K.pfx + "f_xe1", [128, D], F32))
        xT = es.enter_context(nc.sbuf_tensor(K.pfx + "f_xT", [128, 8, 512], BF16))
        hb = es.enter_context(nc.sbuf_tensor(K.pfx + "f_h", [128, NFC, 512], BF16))
        sl0 = es.enter_context(nc.sbuf_tensor(K.pfx + "f_sl0", [128, 512], F32))
        sl1 = es.enter_context(nc.sbuf_tensor(K.pfx + "f_sl1", [128, 512], F32))
        tmp = es.enter_context(nc.sbuf_tensor(K.pfx + "f_tmp", [128, D], F32))
        tmp1 = es.enter_context(nc.sbuf_tensor(K.pfx + "f_tmp1", [128, D], F32))
        gate_bc = es.enter_context(nc.sbuf_tensor(K.pfx + "f_gate", [128, D], F32))
        lng_bc = es.enter_context(nc.sbuf_tensor(K.pfx + "f_lng", [128, D], F32))
        lnb_bc = es.enter_context(nc.sbuf_tensor(K.pfx + "f_lnb", [128, D], F32))
        shiftT = es.enter_context(nc.sbuf_tensor(K.pfx + "f_shT", [128, 8], F32))
        scale1T = es.enter_context(nc.sbuf_tensor(K.pfx + "f_scT", [128, 8], F32))
        ident = es.enter_context(nc.sbuf_tensor(K.pfx + "f_id", [128, 128], F32))
        stats = es.enter_context(nc.sbuf_tensor(K.pfx + "f_st", [128, 12], F32))
        stats1 = es.enter_context(nc.sbuf_tensor(K.pfx + "f_st1", [128, 12], F32))
        mv = es.enter_context(nc.sbuf_tensor(K.pfx + "f_mv", [128, 4], F32))
        mv1 = es.enter_context(nc.sbuf_tensor(K.pfx + "f_mv1", [128, 4], F32))
        epsc = es.enter_context(nc.sbuf_tensor(K.pfx + "f_eps", [128, 1], F32))
        p1a = es.enter_context(nc.psum_tensor(K.pfx + "f_p1a", [128, 512], F32))
        p1b = es.enter_context(nc.psum_tensor(K.pfx + "f_p1b", [128, 512], F32))
        p3a = es.enter_context(nc.psum_tensor(K.pfx + "f_p3a", [128, 512], F32))
        p3b = es.enter_context(nc.psum_tensor(K.pfx + "f_p3b", [128, 512], F32))
        yA = es.enter_context(nc.psum_tensor(K.pfx + "f_yA", [128, 512], F32))
        yB = es.enter_context(nc.psum_tensor(K.pfx + "f_yB", [128, 512], F32))
        tp0 = es.enter_context(nc.psum_tensor(K.pfx + "f_tp0", [128, 512], F32))
        tp1 = es.enter_context(nc.psum_tensor(K.pfx + "f_tp1", [128, 512], F32))
        xa, xe, slb = [xa0, xa1], [xe0, xe1], [sl0, sl1]
        p1, p3, tp = [p1a, p1b], [p3a, p3b], [tp0, tp1]
        dmod = K.dsem()
        did = K.dsem()
        dxa = [K.dsem(), K.dsem()]
        dxe = [K.dsem(), K.dsem()]
        dst = [K.dsem(sw=True), K.dsem(sw=True)]
        t_id = sp.dma(did, ident[:], ident_dram)
        t_eps = dve.op("memset", epsc[:], float(LN_EPS))
        mod_toks = load_mod_vectors(K, ada_scr, l, j, shiftT, scale1T, gate_bc, 0.5, dmod,
                                    lng_bc, lnb_bc, ln_g, ln_b)
        w1v = w1.rearrange("(k p) n -> p k n", p=128)
        w3v = w3.rearrange("(k p) n -> p k n", p=128)
        w2v = w2.rearrange("(f p) n -> p f n", p=128)
        CB = 4
        w13_tok = {}
        for g in range(0, NFC, CB):
            hi = min(NFC, g + CB)
            cs = slice(g * 128, hi * 128)
            dg = K.dsem(sw=True)
            pool.dma(dg, w1b[:, :, cs], w1v[:, :, cs])
            tb = pool.dma(dg, w3b[:, :, cs], w3v[:, :, cs])
            for f in range(g, hi):
                w13_tok[f] = tb
        w2_tok = {}
        for g in range(0, NFC, 6):
            hi = min(NFC, g + 6)
            tb = pool.dma(K.dsem(sw=True), w2b[:, g:hi, :], w2v[:, g:hi, :])
            for f in range(g, hi):
                w2_tok[f] = tb

        xin_t = x_in.rearrange("(t p) d -> t p d", p=128)
        xout_t = x_out.rearrange("(t p) d -> t p d", p=128)

        xa_free = [None, None]
        xa_ld = [None, None]
        tp_free = [None, None]
        xT_free = None
        xT_ready = None
        p13_free = [None, None]
        h_ready = None
        h_free = None
        y_free = [None, None]
        xe_free = [None, None]
        tmps, statss, mvs = [tmp, tmp1], [stats, stats1], [mv, mv1]
        tmp_free = [None, None]
        store_toks = []

        def emit_transposes(sup):
            nonlocal xT_ready
            tl = None
            for tt in range(4):
                tile_i = sup * 4 + tt
                b = tile_i % 2
                xa_ld[b] = sp.dma(dxa[b], xa[b][:], xin_t[tile_i], deps=[xa_free[b]])
                tlast = None
                for half in range(2):
                    tm = None
                    for q in range(4):
                        kc = half * 4 + q
                        tm = pe.op("transpose", tp[half][:, q * 128:(q + 1) * 128],
                                   xa[b][:, kc * 128:(kc + 1) * 128], ident[:],
                                   deps=[xa_ld[b], t_id, tp_free[half]] if q == 0 else ())
                    te = None
                    for q in range(4):
                        kc = half * 4 + q
                        te = act.op("activation", out=xT[:, kc, tt * 128:(tt + 1) * 128],
                                    in_=tp[half][:, q * 128:(q + 1) * 128], func=AF.Identity,
                                    bias=shiftT[:, kc:kc + 1], scale=scale1T[:, kc:kc + 1],
                                    deps=[tm, xT_free] + mod_toks if q == 0 else ())
                    tp_free[half] = te
                    tlast = tm
                    tl = te
                xa_free[b] = tlast
            xT_ready = tl

        emit_transposes(0)
        for sup in range(NSUP):
            tmul = None
            for fc in range(NFC):
                i = fc % 2
                cs = slice(fc * 128, (fc + 1) * 128)
                first = [w13_tok[fc], xT_ready, p13_free[i]]
                for k in range(8):
                    pe.op("matmul", p1[i][:], lhsT=w1b[:, k, cs], rhs=xT[:, k, :], start=(k == 0), stop=(k == 7),
                          deps=first if k == 0 else ())
                tm3 = None
                for k in range(8):
                    tm3 = pe.op("matmul", p3[i][:], lhsT=w3b[:, k, cs], rhs=xT[:, k, :], start=(k == 0),
                                stop=(k == 7))
                ts = act.op("activation", out=slb[i][:], in_=p1[i][:], func=AF.Silu, deps=[tm3, p13_free[i]])
                tmul = dve.op("tensor_tensor", out=hb[:, fc, :], in0=slb[i][:], in1=p3[i][:], op=ALU.mult,
                              deps=[ts, tm3, h_free])
                p13_free[i] = tmul
            h_ready = tmul
            xT_free = tm3
            if sup + 1 < NSUP:
                emit_transposes(sup + 1)
            for tt in range(4):
                tile_i = sup * 4 + tt
                b = tile_i % 2
                txe = sp.dma(dxe[b], xe[b][:], xin_t[tile_i], deps=[xe_free[b]])
                ydone = []
                for hf, yps in enumerate((yA, yB)):
                    tm = None
                    for fc in range(NFC):
                        tm = pe.op("matmul", yps[:], lhsT=hb[:, fc, tt * 128:(tt + 1) * 128],
                                   rhs=w2b[:, fc, hf * 512:(hf + 1) * 512], start=(fc == 0), stop=(fc == NFC - 1),
                                   deps=[h_ready, y_free[hf], w2_tok[fc]] if fc == 0 else [w2_tok[fc]])
                    ydone.append(tm)
                h_free = ydone[1]
                tfin, ycons = epilogue_ln(K, yA, yB, xe[b], tmps[b], statss[b], mvs[b], gate_bc, lng_bc, lnb_bc,
                                          ydone, txe, tmp_free[b], epsc[:, 0:1])
                y_free = ycons
                tmp_free[b] = tfin
                ts = pool.dma(dst[b], xout_t[tile_i], xe[b][:], deps=[tfin])
                xe_free[b] = ts
        return [dst[0].tok(), dst[1].tok()]


class MixIO:
    def __init__(self, K, es, ring, x_in, x_out, ada_scr, l, j, ln_g, ln_b, consts, TS, weight):
        nc = K.nc
        self.K, self.ring, self.TS = K, ring, TS
        sb = lambda name, shape, dt, dma=False: Buf(K, es.enter_context(nc.sbuf_tensor(K.pfx + name, shape, dt)), dma)
        self.xa = [sb("m_xa%d" % i, [128, D], F32, True) for i in range(2)]
        self.xe = [sb("m_xe%d" % i, [128, D], F32, True) for i in range(2)]
        self.st = [K.dsem(sw=True), K.dsem(sw=True)]
        self.tmp = sb("m_tmp", [128, D], F32)
        self.gate_bc = sb("m_gate", [128, D], F32, True)
        self.lng = sb("m_lng", [128, D], F32, True)
        self.lnb = sb("m_lnb", [128, D], F32, True)
        self.shT = sb("m_shT", [128, 8], F32, True)
        self.scT = sb("m_scT", [128, 8], F32, True)
        self.cst = sb("m_cst", [128, 896], F32, True)
        self.identb = sb("m_idb", [128, 128], BF16)
        self.stats = sb("m_st", [128, 12], F32)
        self.mv = sb("m_mv", [128, 4], F32)
        self.eps = sb("m_eps", [128, 2], F32)
        self.xTs = [sb("m_xT%d" % i, [128, 8, TS], BF16) for i in range(2)]
        self.xT = self.xTs[0]
        self.xin_t = x_in.rearrange("(t p) d -> t p d", p=128)
        self.xout_t = x_out.rearrange("(t p) d -> t p d", p=128)
        sp, dve = K.sp, K.dve
        dodma(sp, self.cst[:], consts, W=[self.cst])
        do(dve, "tensor_copy", out=self.identb[:], in_=self.cst[:, 0:128], R=[self.cst], W=[self.identb])
        do(dve, "memset", self.eps[:, 0:1], float(LN_EPS), W=[self.eps])
        do(dve, "memset", self.eps[:, 1:2], 1e-6, W=[self.eps])
        base = j * 3 * D
        row = ada_scr[l]
        dodma(sp, self.shT[:], row[base:base + D].rearrange("(k p) -> p k", p=128), W=[self.shT],
              allow_slow_non_contiguous=True)
        dodma(sp, self.scT[:], row[base + D:base + 2 * D].rearrange("(k p) -> p k", p=128), W=[self.scT],
              allow_slow_non_contiguous=True)
        dodma(sp, self.gate_bc[:], row[base + 2 * D:base + 3 * D].partition_broadcast(128), W=[self.gate_bc])
        dodma(sp, self.lng[:], ln_g[l, j].partition_broadcast(128), W=[self.lng])
        dodma(sp, self.lnb[:], ln_b[l, j].partition_broadcast(128), W=[self.lnb])
        do(dve, "tensor_scalar", out=self.scT[:], in0=self.scT[:], scalar1=1.0, scalar2=None, op0=ALU.add,
           R=[self.scT], W=[self.scT])
        do(dve, "tensor_scalar", out=self.gate_bc[:], in0=self.gate_bc[:], scalar1=1.0, scalar2=float(weight),
           op0=ALU.add, op1=ALU.mult, R=[self.gate_bc], W=[self.gate_bc])
        self.ident = self.cst[:, 0:128]
        self.tri = self.cst[:, 128:256]
        self.ones = self.cst[:, 256:384]
        self.rmask = self.cst[:, 384:896]

    def load_xT(self, sup, ring=None):
        K = self.K
        ring = ring if ring is not None else self.ring
        pe, act, sp = K.pe, K.act, K.sp
        xT = self.xTs[sup % 2]
        self.xT = xT
        for tt in range(self.TS // 128):
            ti = sup * (self.TS // 128) + tt
            xa = self.xa[ti % 2]
            dodma(sp, xa[:], self.xin_t[ti], W=[xa])
            for half in range(2):
                bk = ring.get()
                for q in range(4):
                    kc = half * 4 + q
                    do(pe, "transpose", bk[:, q * 128:(q + 1) * 128], xa[:, kc * 128:(kc + 1) * 128], self.ident,
                       R=[xa, self.cst], W=[bk])
                for q in range(4):
                    kc = half * 4 + q
                    do(act, "activation", out=xT[:, kc, tt * 128:(tt + 1) * 128],
                       in_=bk[:, q * 128:(q + 1) * 128], func=AF.Identity, bias=self.shT[:, kc:kc + 1],
                       scale=self.scT[:, kc:kc + 1], R=[bk, self.shT, self.scT], W=[xT])
        return xT

    def out_proj(self, ti, yT, wout, tcols=None):
        K, ring = self.K, self.ring
        pe, act, dve, pool, sp = K.pe, K.act, K.dve, K.pool_e, K.sp
        if tcols is None:
            tcols = slice(0, 128)
        xe = self.xe[ti % 2]
        dodma(sp, xe[:], self.xin_t[ti], W=[xe])
        ybk = [ring.get(), ring.get()]
        for hf in range(2):
            for kc in range(8):
                do(pe, "matmul", ybk[hf][:], lhsT=yT[:, kc, tcols], rhs=wout[:, kc, hf * 512:(hf + 1) * 512],
                   start=(kc == 0), stop=(kc == 7), R=[yT, wout], W=[ybk[hf]])
        tmp, stats, mv = self.tmp, self.stats, self.mv
        for hf in range(2):
            sl = slice(hf * 512, (hf + 1) * 512)
            do(dve, "tensor_tensor", out=tmp[:, sl], in0=ybk[hf][:], in1=self.gate_bc[:, sl], op=ALU.mult,
               R=[ybk[hf], self.gate_bc], W=[tmp])
            do(dve, "scalar_tensor_tensor", out=xe[:, sl], in0=xe[:, sl], scalar=float(ALPHA), in1=tmp[:, sl],
               op0=ALU.mult, op1=ALU.add, R=[tmp, xe], W=[xe])
            do(dve, "bn_stats", out=stats[:, hf * 6:(hf + 1) * 6], in_=xe[:, sl], R=[xe], W=[stats])
        do(dve, "bn_aggr", out=mv[:, 0:2], in_=stats[:, 0:12], R=[stats], W=[mv])
        do(act, "activation", out=mv[:, 2:3], in_=mv[:, 1:2], func=AF.Sqrt, bias=self.eps[:, 0:1], scale=1.0,
           R=[mv, self.eps], W=[mv])
        do(dve, "reciprocal", out=mv[:, 2:3], in_=mv[:, 2:3], R=[mv], W=[mv])
        do(dve, "scalar_tensor_tensor", out=mv[:, 3:4], in0=mv[:, 0:1], scalar=-1.0, in1=mv[:, 2:3],
           op0=ALU.mult, op1=ALU.mult, R=[mv], W=[mv])
        do(act, "activation", out=xe[:], in_=xe[:], func=AF.Identity, bias=mv[:, 3:4], scale=mv[:, 2:3],
           R=[mv, xe], W=[xe])
        do(dve, "tensor_tensor", out=xe[:], in0=xe[:], in1=self.lng[:], op=ALU.mult, R=[xe, self.lng], W=[xe])
        do(dve, "tensor_tensor", out=xe[:], in0=xe[:], in1=self.lnb[:], op=ALU.add, R=[xe, self.lnb], W=[xe])
        dodma(pool, self.xout_t[ti], xe[:], R=[xe], ds=self.st[ti % 2])

    def final_tokens(self):
        return [self.st[0].tok(), self.st[1].tok()]


def stage_rglru(K, x_in, x_out, P, ada_scr, l, ln_g, ln_b, consts):
    nc = K.nc
    pe, dve, act, pool, sp = K.pe, K.dve, K.act, K.pool_e, K.sp
    TS = 256
    NSUP = S // TS
    with ExitStack() as es:
        sb = lambda name, shape, dt, dma=False, nparts=1: Buf(
            K, es.enter_context(nc.sbuf_tensor(K.pfx + name, shape, dt)), dma, nparts)
        ringF = Ring(K, es, "r_psF", 3)
        ringA = Ring(K, es, "r_psA", 2)
        ringB = Ring(K, es, "r_psB", 2)
        ring = ringA
        io = MixIO(K, es, ring, x_in, x_out, ada_scr, l, 1, ln_g, ln_b, consts, TS, 1.0)
        win = sb("r_win", [128, 8, 2048], BF16, "sw", nparts=4)
        wout = sb("r_wout", [128, 8, D], BF16, "sw")
        waT = sb("r_waT", [128, 8, 128], BF16, "sw")
        wxT = sb("r_wxT", [128, 8, 128], BF16, "sw")
        cw = sb("r_cw", [128, 8, 4], F32, True)
        vec = sb("r_vec", [128, 4, 8], F32, True)
        csp = sb("r_csp", [128, 3, 8], F32)
        gates = [sb("r_gate%d" % i, [128, 8, TS], F32, nparts=8) for i in range(2)]
        xbrs = [sb("r_xbr%d" % i, [128, 8, TS + 3], F32, nparts=8) for i in range(2)]
        xr = sb("r_xr", [128, 8, TS], F32, nparts=8)
        xrb = sb("r_xrb", [128, 8, TS], BF16, nparts=8)
        rg = sb("r_rg", [128, 8, TS], F32, nparts=8)
        ig = sb("r_ig", [128, 8, TS], F32, nparts=8)
        aa = sb("r_aa", [128, 8, TS], F32, nparts=8)
        a2 = sb("r_a2", [128, 8, TS], F32, nparts=8)
        hs = sb("r_hs", [128, 8, TS], F32, nparts=8)
        hprev = sb("r_hp", [128, 8], F32, nparts=8)
        one_c = sb("r_one", [128, 1], F32)
        hgT = sb("r_hgT", [128, 8, TS], BF16, nparts=8)
        w_in_v = P["rglru_w_in"].rearrange("(k p) n -> p k n", p=128)
        for g in range(4):
            cs = slice(g * 512, (g + 1) * 512)
            dodma(pool, win[:, :, cs], w_in_v[:, :, cs], W=[win.p(g)])
        dodma(pool, waT[:], P["rglru_waT"].rearrange("n j i -> j n i"), W=[waT])
        dodma(pool, wxT[:], P["rglru_wxT"].rearrange("n j i -> j n i"), W=[wxT])
        dodma(pool, wout[:], P["rglru_w_out"].rearrange("(k p) n -> p k n", p=128), W=[wout])
        for jj in range(4):
            dodma(sp, cw[:, :, jj], P["rglru_conv_w"][jj].rearrange("(n p) -> p n", p=128), W=[cw],
                  allow_slow_non_contiguous=True)
        for i, nm in enumerate(("rglru_conv_b", "rglru_ba", "rglru_bx", "rglru_lambda")):
            dodma(sp, vec[:, i, :], P[nm].rearrange("(n p) -> p n", p=128), W=[vec],
                  allow_slow_non_contiguous=True)
        cb, ba, bx, lam = (vec[:, i, :] for i in range(4))
        e_ = csp[:, 0, :]
        do(act, "activation", out=e_, in_=lam, func=AF.Exp, scale=-1.0, R=[vec], W=[csp])
        do(dve, "tensor_scalar", out=csp[:, 1, :], in0=e_, scalar1=-1.0 / 3.0, scalar2=0.5, op0=ALU.mult,
           op1=ALU.add, R=[csp], W=[csp])
        do(dve, "tensor_tensor", out=csp[:, 1, :], in0=csp[:, 1, :], in1=e_, op=ALU.mult, R=[csp], W=[csp])
        do(dve, "tensor_scalar", out=csp[:, 1, :], in0=csp[:, 1, :], scalar1=-1.0, scalar2=1.0, op0=ALU.mult,
           op1=ALU.add, R=[csp], W=[csp])
        do(dve, "tensor_tensor", out=csp[:, 1, :], in0=csp[:, 1, :], in1=e_, op=ALU.mult, R=[csp], W=[csp])
        do(dve, "tensor_scalar", out=csp[:, 2, :], in0=csp[:, 1, :], scalar1=-16.0, scalar2=None, op0=ALU.mult,
           R=[csp], W=[csp])
        do(dve, "tensor_scalar", out=csp[:, 1, :], in0=csp[:, 1, :], scalar1=-8.0, scalar2=None, op0=ALU.mult,
           R=[csp], W=[csp])
        do(pool, "memset", xbrs[0][:, :, 0:3], 0.0, W=[xbrs[0]])
        do(pool, "memset", hprev[:], 0.0, W=[hprev])
        do(pool, "memset", one_c[:], 1.0, W=[one_c])

        def front(sup):
            xT = io.load_xT(sup, ringF)
            gate, xbr = gates[sup % 2], xbrs[sup % 2]
            for n in range(8):
                bk = ringF.get()
                for kc in range(8):
                    do(pe, "matmul", bk[:, 0:TS], lhsT=win[:, kc, n * 128:(n + 1) * 128], rhs=xT[:, kc, :],
                       start=(kc == 0), stop=(kc == 7), R=[win.p(n // 4), xT], W=[bk])
                do(act, "activation", out=gate[:, n, :], in_=bk[:, 0:TS], func=AF.Gelu_apprx_tanh, R=[bk], W=[gate.p(n)])
            for n in range(8):
                bk = ringF.get()
                for kc in range(8):
                    do(pe, "matmul", bk[:, 0:TS], lhsT=win[:, kc, 1024 + n * 128:1024 + (n + 1) * 128],
                       rhs=xT[:, kc, :], start=(kc == 0), stop=(kc == 7), R=[win.p(2 + n // 4), xT], W=[bk])
                do(act, "activation", out=xbr[:, n, 3:TS + 3], in_=bk[:, 0:TS], func=AF.Identity, R=[bk], W=[xbr.p(n)])

        def back_half(sup, blocks, ring):
            gate, xbr, xbr_next = gates[sup % 2], xbrs[sup % 2], xbrs[(sup + 1) % 2]
            for n in blocks:
                do(pool, "tensor_scalar", out=xr[:, n, :], in0=xbr[:, n, 3:TS + 3], scalar1=cw[:, n, 3:4],
                   scalar2=cb[:, n:n + 1], op0=ALU.mult, op1=ALU.add, R=[xbr.p(n), cw, vec], W=[xr.p(n)])
                for jj in range(3):
                    do(dve, "scalar_tensor_tensor", out=xr[:, n, :], in0=xbr[:, n, jj:jj + TS],
                       scalar=cw[:, n, jj:jj + 1], in1=xr[:, n, :], op0=ALU.mult, op1=ALU.add,
                       R=[xbr.p(n), cw, xr.p(n)], W=[xr.p(n)])
                do(pool, "tensor_copy", out=xrb[:, n, :], in_=xr[:, n, :], R=[xr.p(n)], W=[xrb.p(n)])
            for n in blocks:
                bk = ring.get()
                do(pe, "matmul", bk[:, 0:TS], lhsT=waT[:, n, :], rhs=xrb[:, n, :], start=True, stop=True,
                   R=[waT, xrb.p(n)], W=[bk])
                do(pe, "matmul", bk[:, TS:2 * TS], lhsT=wxT[:, n, :], rhs=xrb[:, n, :], start=True, stop=True,
                   R=[wxT, xrb.p(n)], W=[bk])
                do(act, "activation", out=rg[:, n, :], in_=bk[:, 0:TS], func=AF.Sigmoid, bias=ba[:, n:n + 1],
                   R=[bk, vec], W=[rg.p(n)])
                do(act, "activation", out=ig[:, n, :], in_=bk[:, TS:2 * TS], func=AF.Sigmoid, bias=bx[:, n:n + 1],
                   R=[bk, vec], W=[ig.p(n)])
                do(pool, "tensor_tensor", out=ig[:, n, :], in0=ig[:, n, :], in1=xr[:, n, :], op=ALU.mult,
                   R=[ig.p(n), xr.p(n)], W=[ig.p(n)])
            for n in blocks:
                do(act, "activation", out=aa[:, n, :], in_=rg[:, n, :], func=AF.Exp, scale=csp[:, 1, n:n + 1],
                   R=[rg.p(n), csp], W=[aa.p(n)])
                do(act, "activation", out=a2[:, n, :], in_=rg[:, n, :], func=AF.Exp, scale=csp[:, 2, n:n + 1],
                   R=[rg.p(n), csp], W=[a2.p(n)])
            for n in blocks:
                do(act, "activation", out=a2[:, n, :], in_=a2[:, n, :], func=AF.Sqrt, scale=-1.0, bias=one_c[:, 0:1],
                   R=[a2.p(n), one_c], W=[a2.p(n)])
            for n in blocks:
                do(dve, "tensor_tensor", out=ig[:, n, :], in0=ig[:, n, :], in1=a2[:, n, :], op=ALU.mult,
                   R=[ig.p(n), a2.p(n)], W=[ig.p(n)])
                do(dve, "tensor_tensor_scan", out=hs[:, n, :], data0=aa[:, n, :], data1=ig[:, n, :],
                   initial=hprev[:, n:n + 1], op0=ALU.mult, op1=ALU.add, R=[aa.p(n), ig.p(n), hprev.p(n)],
                   W=[hs.p(n)])
                do(dve, "tensor_copy", out=hprev[:, n:n + 1], in_=hs[:, n, TS - 1:TS], R=[hs.p(n)], W=[hprev.p(n)])
                do(pool, "tensor_tensor", out=hgT[:, n, :], in0=hs[:, n, :], in1=gate[:, n, :], op=ALU.mult,
                   R=[hs.p(n), gate.p(n)], W=[hgT.p(n)])

        def tail(sup):
            xbr, xbr_next = xbrs[sup % 2], xbrs[(sup + 1) % 2]
            do(pool, "tensor_copy", out=xbr_next[:, :, 0:3], in_=xbr[:, :, TS:TS + 3], R=[xbr], W=[xbr_next])
            for tt in range(TS // 128):
                io.out_proj(sup * (TS // 128) + tt, hgT, wout, slice(tt * 128, (tt + 1) * 128))

        front(0)
        for sup in range(NSUP):
            nxt = record(front, sup + 1) if sup + 1 < NSUP else []
            cur = merge_lists(record(back_half, sup, range(0, 4), ringA), record(back_half, sup, range(4, 8), ringB))
            cur += record(tail, sup)
            emit_merged(nxt, cur)
        return io.final_tokens()


def stage_ab(K, x_in, x_out, P, ada_scr, l, ln_g, ln_b, consts):
    nc = K.nc
    pe, dve, act, pool, sp = K.pe, K.dve, K.act, K.pool_e, K.sp
    TS = 256
    NSUP = S // TS
    CQ, CF, CI, CG, CX, CV, CZ, CGT = 0, 512, 1024, 1536, 2048, 2560, 3072, 3584
    with ExitStack() as es:
        sb = lambda name, shape, dt, dma=False, nparts=1: Buf(
            K, es.enter_context(nc.sbuf_tensor(K.pfx + name, shape, dt)), dma, nparts)
        ringF = Ring(K, es, "a_psF", 2)
        ringM = Ring(K, es, "a_psM", 4)
        ringH = Ring(K, es, "a_psH", 2)
        ring = ringM
        io = MixIO(K, es, ring, x_in, x_out, ada_scr, l, 1, ln_g, ln_b, consts, TS, 1.0)
        tri, ones, rmask, identb = io.tri, io.ones, io.rmask, io.identb
        cst = io.cst
        win = sb("a_win", [128, 8, AB_IN], BF16, "sw", nparts=8)
        wout = sb("a_wout", [128, 8, D], BF16, "sw")
        wst = sb("a_wst", [128, 2, 4, 128], F32, True)
        wq_b = sb("a_wq_b", [128, 4, 128], BF16)
        wk_b = sb("a_wk_b", [128, 4, 128], BF16)
        lg = sb("a_lg", [128, 3, 4], F32, True)
        lbv = sb("a_lb", [128, 2, 4], F32)
        cw = sb("a_cw", [128, 4, 4], F32, True)
        cbv = sb("a_cb", [128, 4], F32, True)
        hg_bc = sb("a_hg", [128, 512], F32, True)
        mg_bc = sb("a_mg", [128, 512], F32, True)
        sk_bc = sb("a_sk", [128, 512], F32, True)
        gb_bc = sb("a_gb", [128, 8], F32, True)
        qs = sb("a_qs", [128, 4, TS], F32)
        acc = qs
        fg = sb("a_fg", [128, 4, TS], F32)
        lfb = sb("a_lfb", [128, 4, TS], F32)
        enb = lfb
        bb = sb("a_bb", [128, 4, TS], F32)
        eb = sb("a_eb", [128, 4, TS], F32)
        ebends = [sb("a_ebe%d" % i, [128, 4, TS // 64], F32) for i in range(2)]
        qTs = [sb("a_qT%d" % i, [128, 4, TS], BF16) for i in range(2)]
        kTs = [sb("a_kT%d" % i, [128, 4, TS], BF16) for i in range(2)]
        xbx = sb("a_xbx", [128, 4, TS + 3], F32)
        xcTs = [sb("a_xcT%d" % i, [128, 4, TS], BF16) for i in range(2)]
        qbTs = [sb("a_qbT%d" % i, [128, 4, TS], BF16) for i in range(2)]
        kbTs = [sb("a_kbT%d" % i, [128, 4, TS], BF16) for i in range(2)]
        va = sb("a_va", [64, 512], BF16)
        ga = sb("a_ga", [64, 512], F32)
        vb = sb("a_vb", [128, 4, 132], BF16)
        vhat = sb("a_vhat", [128, 4, 132], BF16)
        zb = sb("a_zb", [128, 512], F32)
        kbtm = sb("a_kbtm", [128, 512], BF16)
        xctm = sb("a_xctm", [128, 512], BF16)
        gt = sb("a_gt", [128, 8], F32)
        gw = sb("a_gw", [128, 16], F32)
        ex = sb("a_ex", [128, 16], F32)
        bcum = sb("a_bcum", [128, 8], F32)
        Sst = sb("a_S", [128, 4, 128], F32)
        Sbf = sb("a_Sbf", [128, 4, 128], BF16)
        Stmp = sb("a_Stmp", [128, 4, 128], F32)
        Cst = sb("a_C", [128, 4, 132], F32)
        Cbf = sb("a_Cbf", [128, 4, 132], BF16)
        attm = sb("a_attm", [64, 4, 64], BF16)
        kTM = sb("a_kTM", [64, 512], BF16)
        osb = sb("a_osb", [64, 4, 128], F32)
        sq = sb("a_sq", [64, 4, 128], F32)
        ssv = sb("a_ss", [64, 8], F32)
        ya = sb("a_ya", [64, 512], BF16)
        AT = sb("a_AT", [128, 4, 128], BF16)
        hsb = sb("a_hsb", [128, 4, 128], F32)
        dn = sb("a_dn", [128, 16], F32)
        hst = sb("a_hst", [128, 4, 6], F32)
        hmv = sb("a_hmv", [128, 4, 2], F32)
        hrs = sb("a_hrs", [128, 8], F32)
        yb = sb("a_yb", [128, 512], BF16)
        tmp2 = sb("a_tmp2", [128, 512], F32)
        yT = sb("a_yT", [128, 8, 128], BF16, nparts=2)

        w_in_v = P["ab_w_in"].rearrange("(k p) n -> p k n", p=128)
        for c0 in (CQ, CF, CX, CI, CG, CV, CZ):
            dodma(pool, win[:, :, c0:c0 + 512], w_in_v[:, :, c0:c0 + 512], W=[win.p(c0 // 512)])
        dodma(pool, win[:, :, CGT:CGT + 8], w_in_v[:, :, CGT:CGT + 8], W=[win.p(CGT // 512)])
        dodma(pool, wout[:], P["ab_w_out"].rearrange("(k p) n -> p k n", p=128), W=[wout])
        dodma(sp, wst[:, 0, :, :], P["mlstm_wq_bd"].rearrange("h p f -> p h f"), W=[wst])
        dodma(sp, wst[:, 1, :, :], P["mlstm_wk_bd"].rearrange("h p f -> p h f"), W=[wst])
        do(dve, "tensor_copy", out=wq_b[:], in_=wst[:, 0, :, :], R=[wst], W=[wq_b])
        do(dve, "tensor_scalar", out=wk_b[:], in0=wst[:, 1, :, :], scalar1=float(128.0 ** -0.5),
           scalar2=None, op0=ALU.mult, R=[wst], W=[wk_b])
        for li_ in range(3):
            dodma(sp, lg[:, li_, :], P["hgrn_lb_logits"][li_].rearrange("(h p) -> p h", p=128), W=[lg],
                  allow_slow_non_contiguous=True)
        for jj in range(4):
            dodma(sp, cw[:, :, jj], P["mlstm_conv_w"][jj].rearrange("(h p) -> p h", p=128), W=[cw],
                  allow_slow_non_contiguous=True)
        dodma(sp, cbv[:], P["mlstm_conv_b"].rearrange("(h p) -> p h", p=128), W=[cbv],
              allow_slow_non_contiguous=True)
        dodma(sp, hg_bc[:], P["hgrn_norm_g"].partition_broadcast(128), W=[hg_bc])
        dodma(sp, mg_bc[:], P["mlstm_norm_g"].partition_broadcast(128), W=[mg_bc])
        dodma(sp, sk_bc[:], P["mlstm_skip"].partition_broadcast(128), W=[sk_bc])
        dodma(sp, gb_bc[:], P["mlstm_gate_b"].partition_broadcast(128), W=[gb_bc])
        do(act, "activation", out=lg[:], in_=lg[:], func=AF.Exp, R=[lg], W=[lg])
        do(dve, "tensor_tensor", out=lbv[:, 1, :], in0=lg[:, 0, :], in1=lg[:, 1, :], op=ALU.add, R=[lg], W=[lbv])
        do(dve, "tensor_tensor", out=lbv[:, 1, :], in0=lbv[:, 1, :], in1=lg[:, 2, :], op=ALU.add, R=[lg, lbv],
           W=[lbv])
        do(dve, "reciprocal", out=lbv[:, 1, :], in_=lbv[:, 1, :], R=[lbv], W=[lbv])
        do(dve, "tensor_tensor", out=lbv[:, 0, :], in0=lg[:, 0, :], in1=lbv[:, 1, :], op=ALU.mult, R=[lg, lbv],
           W=[lbv])
        do(dve, "tensor_scalar", out=lbv[:, 1, :], in0=lbv[:, 0, :], scalar1=-1.0, scalar2=1.0, op0=ALU.mult,
           op1=ALU.add, R=[lbv], W=[lbv])
        do(pool, "memset", Sst[:], 0.0, W=[Sst])
        do(pool, "memset", Sbf[:], 0.0, W=[Sbf])
        do(pool, "memset", Cst[:], 0.0, W=[Cst])
        do(pool, "memset", Cbf[:], 0.0, W=[Cbf])
        do(pool, "memset", vb[:], 1.0, W=[vb])
        do(pool, "memset", xbx[:, :, 0:3], 0.0, W=[xbx])

        def proj_fm(xT, col0, h, bk):
            for kc in range(8):
                do(pe, "matmul", bk[:, 0:TS], lhsT=win[:, kc, col0 + h * 128:col0 + (h + 1) * 128], rhs=xT[:, kc, :],
                   start=(kc == 0), stop=(kc == 7), R=[win.p(col0 // 512), xT], W=[bk])

        def proj_tm(xT, col0, ncol, tcols, bk, m=128):
            for kc in range(8):
                do(pe, "matmul", bk[0:m, 0:ncol], lhsT=xT[:, kc, tcols], rhs=win[:, kc, col0:col0 + ncol],
                   start=(kc == 0), stop=(kc == 7), R=[win.p(col0 // 512), xT], W=[bk])


        def front(sup):
            p_ = sup % 2
            qT, kT, xcT, qbT, kbT, ebe = qTs[p_], kTs[p_], xcTs[p_], qbTs[p_], kbTs[p_], ebends[p_]
            xT = io.load_xT(sup, ringF)
            for h in range(4):
                bk = ringF.get()
                proj_fm(xT, CQ, h, bk)
                do(act, "activation", out=qs[:, h, :], in_=bk[:, 0:TS], func=AF.Silu, R=[bk], W=[qs])
            for h in range(4):
                bk = ringF.get()
                proj_fm(xT, CF, h, bk)
                do(act, "activation", out=fg[:, h, :], in_=bk[:, 0:TS], func=AF.Sigmoid, R=[bk], W=[fg])
            for h in range(4):
                bk = ringF.get()
                proj_fm(xT, CX, h, bk)
                do(act, "activation", out=xbx[:, h, 3:TS + 3], in_=bk[:, 0:TS], func=AF.Identity, R=[bk], W=[xbx])
            for h in range(4):
                do(dve, "tensor_scalar", out=fg[:, h, :], in0=fg[:, h, :], scalar1=lbv[:, 1, h:h + 1],
                   scalar2=lbv[:, 0, h:h + 1], op0=ALU.mult, op1=ALU.add, R=[fg, lbv], W=[fg])
            do(act, "activation", out=lfb[:], in_=fg[:], func=AF.Ln, R=[fg], W=[lfb])
            for h in range(4):
                do(dve, "tensor_tensor_scan", out=bb[:, h, :], data0=rmask[:, 0:TS], data1=lfb[:, h, :], initial=0.0,
                   op0=ALU.mult, op1=ALU.add, R=[cst, lfb], W=[bb])
            do(act, "activation", out=eb[:], in_=bb[:], func=AF.Exp, R=[bb], W=[eb])
            do(act, "activation", out=enb[:], in_=bb[:], func=AF.Exp, scale=-1.0, R=[bb], W=[enb])
            for cq_ in range(TS // 64):
                do(dve, "tensor_copy", out=ebe[:, :, cq_], in_=eb[:, :, cq_ * 64 + 63], R=[eb], W=[ebe])
            do(dve, "tensor_tensor", out=qT[:], in0=qs[:], in1=eb[:], op=ALU.mult, R=[qs, eb], W=[qT])
            do(pool, "tensor_scalar", out=fg[:], in0=fg[:], scalar1=-1.0, scalar2=1.0, op0=ALU.mult, op1=ALU.add,
               R=[fg], W=[fg])
            do(pool, "tensor_tensor", out=kT[:], in0=fg[:], in1=enb[:], op=ALU.mult, R=[fg, enb], W=[kT])
            for h in range(4):
                do(dve, "tensor_scalar", out=acc[:, h, :], in0=xbx[:, h, 3:TS + 3], scalar1=cw[:, h, 3:4],
                   scalar2=None, op0=ALU.mult, R=[xbx, cw], W=[acc])
                for jj in range(3):
                    do(dve, "scalar_tensor_tensor", out=acc[:, h, :], in0=xbx[:, h, jj:jj + TS],
                       scalar=cw[:, h, jj:jj + 1], in1=acc[:, h, :], op0=ALU.mult, op1=ALU.add,
                       R=[xbx, cw, acc], W=[acc])
            for h in range(4):
                do(act, "activation", out=xcT[:, h, :], in_=acc[:, h, :], func=AF.Silu, bias=cbv[:, h:h + 1],
                   R=[acc, cbv], W=[xcT])
            do(pool, "tensor_copy", out=xbx[:, :, 0:3], in_=xbx[:, :, TS:TS + 3], R=[xbx], W=[xbx])
            for h in range(4):
                bk = ringF.get()
                do(pe, "matmul", bk[:, 0:TS], lhsT=wq_b[:, h, :], rhs=xcT[:, h, :], start=True, stop=True,
                   R=[wq_b, xcT], W=[bk])
                do(pe, "matmul", bk[:, TS:2 * TS], lhsT=wk_b[:, h, :], rhs=xcT[:, h, :], start=True, stop=True,
                   R=[wk_b, xcT], W=[bk])
                do(act, "activation", out=qbT[:, h, :], in_=bk[:, 0:TS], func=AF.Identity, R=[bk], W=[qbT])
                do(act, "activation", out=kbT[:, h, :], in_=bk[:, TS:2 * TS], func=AF.Identity, R=[bk], W=[kbT])


        def M(sup, tt):
            p_ = sup % 2
            qT, kT, xcT, qbT, kbT, ebe = qTs[p_], kTs[p_], xcTs[p_], qbTs[p_], kbTs[p_], ebends[p_]
            xT = io.xTs[p_]
            ti = sup * (TS // 128) + tt
            ts = slice(tt * 128, (tt + 1) * 128)
            bk = ringM.get()
            proj_tm(xT, CV, 512, ts, bk)
            do(act, "activation", out=vb[:, :, 0:128], in_=bk[:, 0:512].rearrange("p (h d) -> p h d", h=4),
               func=AF.Identity, R=[bk], W=[vb])
            bk = ringM.get()
            proj_tm(xT, CZ, 512, ts, bk)
            do(act, "activation", out=zb[:], in_=bk[:, 0:512], func=AF.Silu, R=[bk], W=[zb])
            bk = ringM.get()
            proj_tm(xT, CGT, 8, ts, bk)
            do(dve, "tensor_tensor", out=gt[:], in0=bk[:, 0:8], in1=gb_bc[:], op=ALU.add, R=[bk, gb_bc], W=[gt])
            bkb = ringM.get()
            bkb_bf = bkb[:, :].bitcast(BF16)
            for h in range(4):
                do(pe, "transpose", bkb_bf[:, h * 128:(h + 1) * 128], xcT[:, h, ts], identb[:],
                   R=[xcT, identb], W=[bkb])
            do(act, "activation", out=xctm[:], in_=bkb_bf[:, 0:512], func=AF.Identity, R=[bkb], W=[xctm])
            bk = ringM.get()
            for h in range(4):
                do(pe, "matmul", bk[:, h * 128:(h + 1) * 128], lhsT=xcT[:, h, ts], rhs=wk_b[:, h, :],
                   start=True, stop=True, R=[xcT, wk_b], W=[bk])
            do(dve, "tensor_copy", out=kbtm[:], in_=bk[:, 0:512], R=[bk], W=[kbtm])
            do(act, "activation", out=gw[:, 0:4], in_=gt[:, 4:8], func=AF.Exp, scale=-1.0, R=[gt], W=[gw])
            do(act, "activation", out=gw[:, 0:4], in_=gw[:, 0:4], func=AF.Ln, bias=io.eps[:, 1:2] if False else 1.0,
               R=[gw], W=[gw])
            do(dve, "tensor_scalar", out=gw[:, 4:8], in0=gw[:, 0:4], scalar1=-1.0, scalar2=None, op0=ALU.mult,
               R=[gw], W=[gw])
            bk = ringM.get()
            do(pe, "matmul", bk[:, 0:4], lhsT=tri, rhs=gw[:, 4:8], start=True, stop=True, R=[cst, gw], W=[bk])
            do(pe, "matmul", bk[:, 4:8], lhsT=ones, rhs=gw[:, 4:8], start=True, stop=True, R=[cst, gw], W=[bk])
            do(dve, "tensor_copy", out=bcum[:], in_=bk[:, 0:8], R=[bk], W=[bcum])
            do(dve, "tensor_copy", out=gw[:, 8:16], in_=bcum[:], R=[bcum], W=[gw])
            do(dve, "tensor_tensor", out=dn[:, 0:4], in0=gt[:, 0:4], in1=bcum[:, 0:4], op=ALU.subtract,
               R=[gt, bcum], W=[dn])
            do(dve, "tensor_tensor", out=dn[:, 4:8], in0=dn[:, 0:4], in1=bcum[:, 4:8], op=ALU.add,
               R=[dn, bcum], W=[dn])
            do(act, "activation", out=ex[:, 0:8], in_=gw[:, 8:16], func=AF.Exp, R=[gw], W=[ex])
            do(act, "activation", out=ex[:, 8:16], in_=dn[:, 0:8], func=AF.Exp, R=[dn], W=[ex])
            ebt, dec, wsc, usc = (lambda o: (lambda h: ex[:, o + h:o + h + 1]))(0), \
                (lambda h: ex[:, 4 + h:5 + h]), (lambda h: ex[:, 8 + h:9 + h]), (lambda h: ex[:, 12 + h:13 + h])
            bk = ringM.get()
            for h in range(4):
                do(pe, "matmul", bk[:, h * 128:(h + 1) * 128], lhsT=kbT[:, h, ts], rhs=qbT[:, h, ts],
                   start=True, stop=True, R=[kbT, qbT], W=[bk])
            for h in range(4):
                do(dve, "scalar_tensor_tensor", out=AT[:, h, :], in0=bk[:, h * 128:(h + 1) * 128], scalar=wsc(h),
                   in1=tri, op0=ALU.mult, op1=ALU.mult, R=[bk, ex, cst], W=[AT])
                do(dve, "tensor_scalar", out=vhat[:, h, 0:129], in0=vb[:, h, 0:129], scalar1=usc(h),
                   scalar2=None, op0=ALU.mult, R=[vb, ex], W=[vhat])
            rb = [ringM.get(), ringM.get()]
            for h in range(4):
                o = (h % 2) * 129
                do(pe, "matmul", rb[h // 2][:, o:o + 129], lhsT=AT[:, h, :], rhs=vb[:, h, 0:129], start=True,
                   stop=False, R=[AT, vb], W=[rb[h // 2]])
                do(pe, "matmul", rb[h // 2][:, o:o + 129], lhsT=qbT[:, h, ts], rhs=Cbf[:, h, 0:129], start=False,
                   stop=True, R=[qbT, Cbf], W=[rb[h // 2]])
            cbk = [ringM.get(), ringM.get()]
            for h in range(4):
                o = (h % 2) * 129
                do(pe, "matmul", cbk[h // 2][:, o:o + 129], lhsT=kbtm[:, h * 128:(h + 1) * 128],
                   rhs=vhat[:, h, 0:129], start=True, stop=True, R=[kbtm, vhat], W=[cbk[h // 2]])
            for g in range(2):
                do(dve, "tensor_tensor", out=dn[:, 8 + 2 * g:10 + 2 * g],
                   in0=rb[g][:, 0:258].rearrange("p (h d) -> p h d", h=2)[:, :, 128],
                   in1=ex[:, 2 * g:2 * g + 2], op=ALU.mult, R=[rb[g], ex], W=[dn])
            do(dve, "tensor_scalar", out=gw[:, 0:4], in0=dn[:, 8:12], scalar1=-1.0, scalar2=1.0, op0=ALU.mult,
               op1=ALU.max, R=[dn], W=[gw])
            do(dve, "tensor_tensor", out=dn[:, 8:12], in0=dn[:, 8:12], in1=gw[:, 0:4], op=ALU.max, R=[dn, gw],
               W=[dn])
            do(dve, "reciprocal", out=dn[:, 8:12], in_=dn[:, 8:12], R=[dn], W=[dn])
            do(dve, "tensor_tensor", out=dn[:, 12:16], in0=dn[:, 8:12], in1=ex[:, 0:4], op=ALU.mult, R=[dn, ex],
               W=[dn])
            for h in range(4):
                o = (h % 2) * 129
                do(act, "activation", out=hsb[:, h, :], in_=rb[h // 2][:, o:o + 128], func=AF.Copy,
                   scale=dn[:, 12 + h:13 + h], R=[rb[h // 2], dn], W=[hsb])
            for h in range(4):
                o = (h % 2) * 129
                do(dve, "scalar_tensor_tensor", out=Cst[:, h, 0:129], in0=Cst[:, h, 0:129], scalar=dec(h),
                   in1=cbk[h // 2][:, o:o + 129], op0=ALU.mult, op1=ALU.add, R=[Cst, ex, cbk[h // 2]], W=[Cst])
            do(pool, "tensor_copy", out=Cbf[:], in_=Cst[:], R=[Cst], W=[Cbf])
            for h in range(4):
                do(dve, "bn_stats", out=hst[:, h, :], in_=hsb[:, h, :], R=[hsb], W=[hst])
            for h in range(4):
                do(dve, "bn_aggr", out=hmv[:, h, :], in_=hst[:, h, :], R=[hst], W=[hmv])
            do(act, "activation", out=hrs[:, 0:4], in_=hmv[:, :, 1], func=AF.Sqrt, bias=io.eps[:, 1:2], scale=1.0,
               R=[hmv, io.eps], W=[hrs])
            do(dve, "reciprocal", out=hrs[:, 0:4], in_=hrs[:, 0:4], R=[hrs], W=[hrs])
            do(dve, "scalar_tensor_tensor", out=hrs[:, 4:8], in0=hmv[:, :, 0], scalar=-1.0, in1=hrs[:, 0:4],
               op0=ALU.mult, op1=ALU.mult, R=[hmv, hrs], W=[hrs])
            for h in range(4):
                do(act, "activation", out=hsb[:, h, :], in_=hsb[:, h, :], func=AF.Identity,
                   bias=hrs[:, 4 + h:5 + h], scale=hrs[:, h:h + 1], R=[hsb, hrs], W=[hsb])
            hflat = hsb[:, :, :].rearrange("p h d -> p (h d)")
            do(dve, "tensor_tensor", out=hflat, in0=hflat, in1=mg_bc[:], op=ALU.mult, R=[hsb, mg_bc], W=[hsb])
            do(dve, "tensor_tensor", out=tmp2[:], in0=xctm[:], in1=sk_bc[:], op=ALU.mult, R=[xctm, sk_bc],
               W=[tmp2])
            do(dve, "tensor_tensor", out=hflat, in0=hflat, in1=tmp2[:], op=ALU.add, R=[hsb, tmp2], W=[hsb])
            do(dve, "tensor_tensor", out=yb[:], in0=hflat, in1=zb[:], op=ALU.mult, R=[hsb, zb], W=[yb])
            bkb = ringM.get()
            bkb_bf = bkb[:, :].bitcast(BF16)
            for h in range(4):
                do(pe, "transpose", bkb_bf[:, h * 128:(h + 1) * 128], yb[:, h * 128:(h + 1) * 128], identb[:],
                   R=[yb, identb], W=[bkb])
            do(act, "activation", out=yT[:, 4:8, :], in_=bkb_bf[:, 0:512].rearrange("p (h d) -> p h d", h=4),
               func=AF.Identity, R=[bkb], W=[yT.p(1)])

        def H(sup, tt):
            p_ = sup % 2
            qT, kT, xcT, qbT, kbT, ebe = qTs[p_], kTs[p_], xcTs[p_], qbTs[p_], kbTs[p_], ebends[p_]
            xT = io.xTs[p_]
            ti = sup * (TS // 128) + tt
            ts = slice(tt * 128, (tt + 1) * 128)
            for c in range(2):
                cs = slice(tt * 128 + c * 64, tt * 128 + (c + 1) * 64)
                cq = tt * 2 + c
                bk = ringH.get()
                proj_tm(xT, CI, 512, cs, bk, m=64)
                do(act, "activation", out=va[:], in_=bk[0:64, 0:512], func=AF.Identity, R=[bk], W=[va])
                bk = ringH.get()
                proj_tm(xT, CG, 512, cs, bk, m=64)
                do(act, "activation", out=ga[:], in_=bk[0:64, 0:512], func=AF.Silu, R=[bk], W=[ga])
                bk = ringH.get()
                for h in range(4):
                    do(pe, "matmul", bk[0:64, h * 64:(h + 1) * 64], lhsT=kT[:, h, cs], rhs=qT[:, h, cs],
                       start=True, stop=True, R=[kT, qT], W=[bk])
                for h in range(4):
                    do(dve, "tensor_tensor", out=attm[:, h, :], in0=bk[0:64, h * 64:(h + 1) * 64],
                       in1=cst[0:64, 128:192], op=ALU.mult, R=[bk, cst], W=[attm])
                bkb = ringH.get()
                bkb_bf = bkb[:, :].bitcast(BF16)
                for h in range(4):
                    do(pe, "transpose", bkb_bf[0:64, h * 128:(h + 1) * 128], kT[:, h, cs], identb[:],
                       R=[kT, identb], W=[bkb])
                do(act, "activation", out=kTM[:], in_=bkb_bf[0:64, 0:512], func=AF.Identity, R=[bkb], W=[kTM])
                obk = ringH.get()
                for h in range(4):
                    do(pe, "matmul", obk[0:64, h * 128:(h + 1) * 128], lhsT=attm[:, h, :],
                       rhs=va[:, h * 128:(h + 1) * 128], start=True, stop=False, R=[attm, va], W=[obk])
                    do(pe, "matmul", obk[0:64, h * 128:(h + 1) * 128], lhsT=qT[:, h, cs], rhs=Sbf[:, h, :],
                       start=False, stop=True, R=[qT, Sbf], W=[obk])
                sbk = ringH.get()
                for h in range(4):
                    do(pe, "matmul", sbk[:, h * 128:(h + 1) * 128], lhsT=kTM[:, h * 128:(h + 1) * 128],
                       rhs=va[:, h * 128:(h + 1) * 128], start=True, stop=True, R=[kTM, va], W=[sbk])
                do(dve, "tensor_tensor", out=Stmp[:], in0=Sst[:], in1=sbk[:, 0:512].rearrange("p (h d) -> p h d", h=4),
                   op=ALU.add, R=[Sst, sbk], W=[Stmp])
                for h in range(4):
                    do(dve, "tensor_scalar", out=Sst[:, h, :], in0=Stmp[:, h, :], scalar1=ebe[:, h, cq:cq + 1],
                       scalar2=None, op0=ALU.mult, R=[Stmp, ebe], W=[Sst])
                do(pool, "tensor_copy", out=Sbf[:], in_=Sst[:], R=[Sst], W=[Sbf])
                do(act, "activation", out=osb[:], in_=obk[0:64, 0:512].rearrange("p (h d) -> p h d", h=4),
                   func=AF.Identity, R=[obk], W=[osb])
                do(dve, "tensor_tensor", out=sq[:], in0=osb[:], in1=osb[:], op=ALU.mult, R=[osb], W=[sq])
                for h in range(4):
                    do(dve, "tensor_reduce", out=ssv[:, h:h + 1], in_=sq[:, h, :], axis=mybir.AxisListType.X,
                       op=ALU.add, R=[sq], W=[ssv])
                do(act, "activation", out=ssv[:, 4:8], in_=ssv[:, 0:4], func=AF.Sqrt, bias=io.eps[0:64, 1:2],
                   scale=1.0 / 128.0, R=[ssv, io.eps], W=[ssv])
                do(dve, "reciprocal", out=ssv[:, 4:8], in_=ssv[:, 4:8], R=[ssv], W=[ssv])
                for h in range(4):
                    do(dve, "tensor_scalar", out=osb[:, h, :], in0=osb[:, h, :], scalar1=ssv[:, 4 + h:5 + h],
                       scalar2=None, op0=ALU.mult, R=[osb, ssv], W=[osb])
                oflat = osb[:, :, :].rearrange("p h d -> p (h d)")
                do(dve, "tensor_tensor", out=oflat, in0=oflat, in1=hg_bc[0:64, :], op=ALU.mult, R=[osb, hg_bc],
                   W=[osb])
                do(dve, "tensor_tensor", out=ya[:], in0=oflat, in1=ga[:], op=ALU.mult, R=[osb, ga], W=[ya])
                bkb = ringH.get()
                bkb_bf = bkb[:, :].bitcast(BF16)
                for h in range(4):
                    do(pe, "transpose", bkb_bf[:, h * 64:(h + 1) * 64], ya[:, h * 128:(h + 1) * 128],
                       identb[0:64, 0:64], R=[ya, identb], W=[bkb])
                do(act, "activation", out=yT[:, 0:4, c * 64:(c + 1) * 64],
                   in_=bkb_bf[:, 0:256].rearrange("p (h d) -> p h d", h=4), func=AF.Identity, R=[bkb], W=[yT.p(0)])

        def O(sup, tt):
            io.out_proj(sup * (TS // 128) + tt, yT, wout)

        front(0)
        for sup in range(NSUP):
            nxt = record(front, sup + 1) if sup + 1 < NSUP else []
            cur = []
            for tt in range(TS // 128):
                cur += merge_lists(record(M, sup, tt), record(H, sup, tt))
                cur += record(O, sup, tt)
            emit_merged(nxt, cur)
        return io.final_tokens()


ALL_STAGES = ("prep", "ffn00", "ab", "ffn02", "ffn10", "rglru", "ffn12")

PARAM_SHAPES = {
    "ada_w": [2, D, 9 * D], "ada_b": [2, 9 * D], "ln_g": [2, 3, D], "ln_b": [2, 3, D],
    "ffn_w1": [2, 2, D, DFF], "ffn_w3": [2, 2, D, DFF], "ffn_w2": [2, 2, DFF, D],
    "hgrn_lb_logits": [3, 512], "ab_w_in": [D, AB_IN], "ab_w_out": [D, D], "hgrn_norm_g": [512],
    "mlstm_conv_w": [4, 512], "mlstm_conv_b": [512], "mlstm_wq_bd": [4, 128, 128], "mlstm_wk_bd": [4, 128, 128],
    "mlstm_gate_b": [8], "mlstm_skip": [512], "mlstm_norm_g": [512],
    "rglru_w_in": [D, 2 * D], "rglru_conv_w": [4, D], "rglru_conv_b": [D], "rglru_waT": [8, 128, 128],
    "rglru_ba": [D], "rglru_wxT": [8, 128, 128], "rglru_bx": [D], "rglru_lambda": [D], "rglru_w_out": [D, D],
    "consts": [128, 896],
}


def build_program(stages=ALL_STAGES):
    nc = bass.Bass("TRN2", target_bir_lowering=False)
    dt = lambda name, shape, kind="ExternalInput": nc.dram_tensor(name, list(shape), F32, kind=kind).ap()
    x = dt("x", [S, D])
    c = dt("c", [D])
    P = {k: dt(k, shp) for k, shp in PARAM_SHAPES.items()}
    out = dt("out", [S, D], kind="ExternalOutput")
    ada_scr = dt("ada_scr", [2, 9 * D], kind="Internal")
    xs = [dt("xs%d" % i, [S, D], kind="Internal") for i in range(2)]
    ident = P["consts"][:, 0:128]

    K = Kern(nc, 0)
    cur = x
    nxt = 0
    for si, st in enumerate(stages):
        last = si == len(stages) - 1
        K.pfx = "s%d_" % si
        if st == "prep":
            toks = stage_prep(K, c, P["ada_w"], P["ada_b"], ada_scr)
        else:
            dst = out if last else xs[nxt]
            if st.startswith("ffn"):
                l, j = int(st[3]), int(st[4])
                f = 0 if j == 0 else 1
                toks = stage_ffn(K, cur, dst, P["ffn_w1"][l, f], P["ffn_w3"][l, f], P["ffn_w2"][l, f], ada_scr,
                                 l, j, P["ln_g"], P["ln_b"], ident)
            elif st == "ab":
                toks = stage_ab(K, cur, dst, P, ada_scr, 0, P["ln_g"], P["ln_b"], P["consts"])
            elif st == "rglru":
                toks = stage_rglru(K, cur, dst, P, ada_scr, 1, P["ln_g"], P["ln_b"], P["consts"])
            cur = dst
            nxt ^= 1
        K.barrier(toks)
    return nc


def make_consts():
    cst = np.zeros((128, 896), np.float32)
    cst[:, 0:128] = np.eye(128, dtype=np.float32)
    cst[:, 128:256] = np.triu(np.ones((128, 128), np.float32))
    cst[:, 256:384] = 1.0
    rm = np.ones(512, np.float32)
    rm[0::64] = 0.0
    cst[:, 384:896] = rm[None, :]
    return cst


def block_diag_layout(w):
    o = np.zeros((4, 128, 128), np.float32)
    for h in range(4):
        for n in range(32):
            o[h, 4 * n:4 * n + 4, 4 * n:4 * n + 4] = w[h * 32 + n].T
    return o


def shared_inputs(inputs):
    g = lambda k: np.ascontiguousarray(np.asarray(inputs[k], dtype=np.float32))
    m = {k: g(k) for k in ("ada_w", "ada_b", "ln_g", "ln_b", "ffn_w1", "ffn_w3", "ffn_w2", "hgrn_lb_logits")}
    for k in ("ab_w_in", "ab_w_out", "hgrn_norm_g", "mlstm_conv_w", "mlstm_conv_b", "mlstm_gate_b", "mlstm_skip",
              "mlstm_norm_g", "rglru_w_in", "rglru_conv_w", "rglru_conv_b", "rglru_ba", "rglru_bx", "rglru_lambda",
              "rglru_w_out"):
        m[k] = np.ascontiguousarray(g(k)[0])
    m["mlstm_wq_bd"] = block_diag_layout(g("mlstm_wq")[0])
    m["mlstm_wk_bd"] = block_diag_layout(g("mlstm_wk")[0])
    m["rglru_waT"] = np.ascontiguousarray(g("rglru_wa")[0].transpose(0, 2, 1))
    m["rglru_wxT"] = np.ascontiguousarray(g("rglru_wx")[0].transpose(0, 2, 1))
    m["consts"] = make_consts()
    return m


def core_inputs(inputs, b, shared=None, x_override=None):
    m = dict(shared if shared is not None else shared_inputs(inputs))
    m["x"] = np.ascontiguousarray(inputs["x"][b] if x_override is None else x_override, dtype=np.float32)
    m["c"] = np.ascontiguousarray(inputs["c"][b], dtype=np.float32)
    return m


def kernel(**inputs):
    nc = build_program()
    shared = shared_inputs(inputs)
    in_maps = [core_inputs(inputs, b, shared) for b in range(NB)]
    res = run_bass_kernel_spmd(nc, in_maps, core_ids=list(range(NB)))
    return np.stack([r["out"] for r in res.results], axis=0).astype(np.float32)
```

```python
from contextlib import ExitStack
import numpy as np
import concourse.bass as bass
import concourse.mybir as mybir
from concourse.bass_utils import run_bass_kernel_spmd

F32, BF16 = mybir.dt.float32, mybir.dt.bfloat16
AF = mybir.ActivationFunctionType
ALU = mybir.AluOpType

D = 1024
S = 4096
NB = 8
DFF = 2816
NFC = DFF // 128
NT = S // 128
ALPHA = 4.0 ** 0.25
AB_IN = 3592
LN_EPS = 1e-5


class Tok:
    __slots__ = ("sem", "val", "epoch")

    def __init__(self, sem, val, epoch):
        self.sem, self.val, self.epoch = sem, val, epoch


class DSem:
    def __init__(self, K, sem):
        self.K = K
        self.sem = sem
        self.n = 0

    def tok(self):
        return Tok(self.sem, self.n, self.K.epoch)


class Eng:
    def __init__(self, K, name):
        self.K, self.name = K, name
        self.eng = getattr(K.nc, name)
        self.sem = None
        self.n = 0
        self.waited = {}

    def wait(self, t, force=False):
        if t is None:
            return
        if isinstance(t, (list, tuple)):
            for u in t:
                self.wait(u, force)
            return
        if t.epoch < self.K.epoch and not force:
            return
        if t.val <= 0:
            return
        key = id(t.sem)
        if t.sem is self.sem and not force:
            pass
        if self.waited.get(key, 0) >= t.val:
            return
        self.waited[key] = t.val
        self.eng.wait_ge(t.sem, t.val)

    def last(self):
        return Tok(self.sem, self.n, self.K.epoch)

    def op(self, meth, *a, deps=(), **kw):
        self.wait(deps)
        ins = getattr(self.eng, meth)(*a, **kw)
        self.n += 1
        ins.then_inc(self.sem, 1)
        return Tok(self.sem, self.n, self.K.epoch)

    def dma(self, dsem, out, in_, deps=(), **kw):
        self.wait(deps)
        self.eng.dma_start(out=out, in_=in_, **kw).then_inc(dsem.sem, 16)
        dsem.n += 16
        return dsem.tok()


class Kern:
    def __init__(self, nc, nsems):
        self.nc = nc
        self.nsem = 0
        self.pfx = "s0_"
        self.dsems = {False: [], True: []}
        self.dcur = {False: 0, True: 0}
        self.epoch = 0
        self.engs = {n: Eng(self, n) for n in ("tensor", "vector", "scalar", "gpsimd", "sync")}
        for e in self.engs.values():
            e.sem = self.new_sem()
        self.pe, self.dve, self.act, self.pool_e, self.sp = (self.engs[n] for n in
                                                             ("tensor", "vector", "scalar", "gpsimd", "sync"))

    def new_sem(self):
        self.nsem += 1
        return self.nc.alloc_semaphore("s%d" % self.nsem)

    def dsem(self, sw=False):
        lst = self.dsems[sw]
        if self.dcur[sw] == len(lst):
            lst.append(DSem(self, self.new_sem()))
        d = lst[self.dcur[sw]]
        self.dcur[sw] += 1
        return d

    def barrier(self, extra=()):
        toks = [e.last() for e in self.engs.values() if e.n > 0] + list(extra)
        self.epoch += 1
        self.dcur = {False: 0, True: 0}
        for e in self.engs.values():
            e.sem = self.new_sem()
            e.n = 0
            e.waited = {}
        for e in self.engs.values():
            for t in toks:
                e.wait(t, force=True)


INORDER_NO_SELF_WAIT = ("tensor",)


class _Part:
    __slots__ = ("w", "r", "ds")

    def __init__(self):
        self.w = None
        self.r = {}
        self.ds = None


class _Ref:
    __slots__ = ("parts",)

    def __init__(self, parts):
        self.parts = parts

    @property
    def ds(self):
        return self.parts[0].ds


class Buf:
    def __init__(self, K, t, dma=False, nparts=1):
        self.t = t
        self.parts = [_Part() for _ in range(nparts)]
        if dma:
            for pt in self.parts:
                pt.ds = K.dsem(sw=(dma == "sw"))
        self.ds = self.parts[0].ds

    def p(self, i):
        return _Ref([self.parts[i]])

    def __getitem__(self, k):
        return self.t[k]


def _deps(R, W, skip_sem=None):
    d = []
    for b in R:
        for pt in b.parts:
            if pt.w is not None:
                d.append(pt.w)
    for b in W:
        for pt in b.parts:
            if pt.w is not None:
                d.append(pt.w)
            d.extend(pt.r.values())
    if skip_sem is not None:
        d = [t for t in d if t.sem is not skip_sem]
    return d


def _mark(tok, R, W):
    for b in R:
        for pt in b.parts:
            pt.r[id(tok.sem)] = tok
    for b in W:
        for pt in b.parts:
            pt.w = tok
            pt.r = {}


_REC = [None]


def record(fn, *args):
    prev = _REC[0]
    _REC[0] = lst = []
    try:
        fn(*args)
    finally:
        _REC[0] = prev
    return lst


def emit_merged(a, b):
    na, nb = len(a), len(b)
    ia = ib = 0
    while ia < na or ib < nb:
        if ib >= nb or (ia < na and ia * nb <= ib * na):
            a[ia]()
            ia += 1
        else:
            b[ib]()
            ib += 1


def merge_lists(a, b):
    out, na, nb, ia, ib = [], len(a), len(b), 0, 0
    while ia < na or ib < nb:
        if ib >= nb or (ia < na and ia * nb <= ib * na):
            out.append(a[ia])
            ia += 1
        else:
            out.append(b[ib])
            ib += 1
    return out


def do(e, meth, *a, R=(), W=(), deps=(), **kw):
    if _REC[0] is not None:
        th = lambda: _do_now(e, meth, *a, R=R, W=W, deps=deps, **kw)
        if meth == "matmul" and kw.get("start") is False and _REC[0]:
            prev = _REC[0][-1]
            _REC[0][-1] = lambda prev=prev, th=th: (prev(), th())
        else:
            _REC[0].append(th)
        return None
    return _do_now(e, meth, *a, R=R, W=W, deps=deps, **kw)


def _do_now(e, meth, *a, R=(), W=(), deps=(), **kw):
    skip = e.sem if e.name in INORDER_NO_SELF_WAIT else None
    tok = e.op(meth, *a, deps=_deps(R, W, skip) + list(deps), **kw)
    _mark(tok, R, W)
    return tok


def dodma(e, out, in_, R=(), W=(), ds=None, deps=(), **kw):
    if _REC[0] is not None:
        _REC[0].append(lambda: _dodma_now(e, out, in_, R=R, W=W, ds=ds, deps=deps, **kw))
        return None
    return _dodma_now(e, out, in_, R=R, W=W, ds=ds, deps=deps, **kw)


def _dodma_now(e, out, in_, R=(), W=(), ds=None, deps=(), **kw):
    if ds is None:
        ds = W[0].ds
    tok = e.dma(ds, out, in_, deps=_deps(R, W) + list(deps), **kw)
    _mark(tok, R, W)
    return tok


class Ring:
    def __init__(self, K, es, name, n):
        self.b = [Buf(K, es.enter_context(K.nc.psum_tensor(K.pfx + "%s%d" % (name, i), [128, 512], F32))) for i in range(n)]
        self.i = 0

    def get(self):
        b = self.b[self.i]
        self.i = (self.i + 1) % len(self.b)
        return b


def stage_prep(K, c_ap, ada_w, ada_b, ada_scr):
    nc = K.nc
    pe, dve, act, pool, sp = K.pe, K.dve, K.act, K.pool_e, K.sp
    with ExitStack() as es:
        c_sb = es.enter_context(nc.sbuf_tensor(K.pfx + "p_c", [128, 8], F32))
        c_bf = es.enter_context(nc.sbuf_tensor(K.pfx + "p_cb", [128, 8], BF16))
        w0 = es.enter_context(nc.sbuf_tensor(K.pfx + "p_w0", [128, 8, 512], BF16))
        w1 = es.enter_context(nc.sbuf_tensor(K.pfx + "p_w1", [128, 8, 512], BF16))
        b0 = es.enter_context(nc.sbuf_tensor(K.pfx + "p_b0", [1, 512], F32))
        b1 = es.enter_context(nc.sbuf_tensor(K.pfx + "p_b1", [1, 512], F32))
        r0 = es.enter_context(nc.sbuf_tensor(K.pfx + "p_r0", [1, 512], F32))
        r1 = es.enter_context(nc.sbuf_tensor(K.pfx + "p_r1", [1, 512], F32))
        ps0 = es.enter_context(nc.psum_tensor(K.pfx + "p_ps0", [1, 512], F32))
        ps1 = es.enter_context(nc.psum_tensor(K.pfx + "p_ps1", [1, 512], F32))
        wb, bb, rb, psb = [w0, w1], [b0, b1], [r0, r1], [ps0, ps1]
        dw = [K.dsem(sw=True), K.dsem(sw=True)]
        db = [K.dsem(), K.dsem()]
        do = [K.dsem(), K.dsem()]
        dc = K.dsem()
        t = sp.dma(dc, c_sb[:], c_ap.rearrange("(p k) -> p k", k=8))
        t = act.op("activation", out=c_bf[:], in_=c_sb[:], func=AF.Silu, deps=[t])
        c_tok = t
        chunks = [(l, n) for l in range(2) for n in range(18)]
        mm_done = [None, None]
        ev_done = [None, None]
        for ci, (l, n) in enumerate(chunks):
            i = ci % 2
            wsrc = ada_w[l].rearrange("(p k) n -> p k n", k=8)[:, :, n * 512:(n + 1) * 512]
            tw = pool.dma(dw[i], wb[i][:], wsrc, deps=[mm_done[i]])
            tb = sp.dma(db[i], bb[i][:], ada_b[l:l + 1, n * 512:(n + 1) * 512], deps=[ev_done[i]])
            tm = None
            for k in range(8):
                tm = pe.op("matmul", psb[i][:], lhsT=c_bf[:, k:k + 1], rhs=wb[i][:, k, :],
                           start=(k == 0), stop=(k == 7),
                           deps=[tw, c_tok, ev_done[i]] if k == 0 else ())
            mm_done[i] = tm
            te = dve.op("tensor_tensor", out=rb[i][:], in0=psb[i][:], in1=bb[i][:], op=ALU.add,
                        deps=[tm, tb, do[i].tok() if do[i].n else None])
            ev_done[i] = te
            sp.dma(do[i], ada_scr[l:l + 1, n * 512:(n + 1) * 512], rb[i][:], deps=[te])
        return [do[0].tok(), do[1].tok()]


def load_mod_vectors(K, ada_scr, l, j, shiftT, scale1T, gate_bc, weight, dsem, lng_bc, lnb_bc, ln_g, ln_b):
    nc = K.nc
    sp, dve = K.sp, K.dve
    base = j * 3 * D
    row = ada_scr[l]
    toks = []
    toks.append(sp.dma(dsem, shiftT[:], row[base:base + D].rearrange("(k p) -> p k", p=128),
                       allow_slow_non_contiguous=True))
    toks.append(sp.dma(dsem, scale1T[:], row[base + D:base + 2 * D].rearrange("(k p) -> p k", p=128),
                       allow_slow_non_contiguous=True))
    toks.append(sp.dma(dsem, gate_bc[:], row[base + 2 * D:base + 3 * D].partition_broadcast(128)))
    toks.append(sp.dma(dsem, lng_bc[:], ln_g[l, j].partition_broadcast(128)))
    toks.append(sp.dma(dsem, lnb_bc[:], ln_b[l, j].partition_broadcast(128)))
    tall = dsem.tok()
    t1 = dve.op("tensor_scalar", out=scale1T[:], in0=scale1T[:], scalar1=1.0, scalar2=None, op0=ALU.add,
                deps=[tall])
    t2 = dve.op("tensor_scalar", out=gate_bc[:], in0=gate_bc[:], scalar1=1.0, scalar2=float(weight),
                op0=ALU.add, op1=ALU.mult, deps=[tall])
    return [t1, t2, tall]


def epilogue_ln(K, ypsA, ypsB, xe, tmp, stats, mv, gate_bc, lng_bc, lnb_bc, deps_y, deps_x, tmp_free, eps_ap):
    dve, act, pool = K.dve, K.act, K.pool_e
    ycons = []
    tl = None
    for hf, yps in enumerate((ypsA, ypsB)):
        sl = slice(hf * 512, (hf + 1) * 512)
        t = dve.op("tensor_tensor", out=tmp[:, sl], in0=yps[:], in1=gate_bc[:, sl], op=ALU.mult,
                   deps=[deps_y[hf], tmp_free])
        ycons.append(t)
        t = dve.op("scalar_tensor_tensor", out=xe[:, sl], in0=xe[:, sl], scalar=float(ALPHA), in1=tmp[:, sl],
                   op0=ALU.mult, op1=ALU.add, deps=[t, deps_x])
        t = dve.op("bn_stats", out=stats[:, hf * 6:(hf + 1) * 6], in_=xe[:, sl], deps=[t])
        tl = t
    t = dve.op("bn_aggr", out=mv[:, 0:2], in_=stats[:, 0:12], deps=[tl])
    t = act.op("activation", out=mv[:, 2:3], in_=mv[:, 1:2], func=AF.Sqrt, bias=eps_ap, scale=1.0, deps=[t])
    t = dve.op("reciprocal", out=mv[:, 2:3], in_=mv[:, 2:3], deps=[t])
    t = dve.op("scalar_tensor_tensor", out=mv[:, 3:4], in0=mv[:, 0:1], scalar=-1.0, in1=mv[:, 2:3],
               op0=ALU.mult, op1=ALU.mult, deps=[t])
    t = act.op("activation", out=xe[:], in_=xe[:], func=AF.Identity, bias=mv[:, 3:4], scale=mv[:, 2:3],
               deps=[t])
    t = pool.op("tensor_tensor", out=xe[:], in0=xe[:], in1=lng_bc[:], op=ALU.mult, deps=[t])
    t = pool.op("tensor_tensor", out=xe[:], in0=xe[:], in1=lnb_bc[:], op=ALU.add, deps=[t])
    return t, ycons


def stage_ffn(K, x_in, x_out, w1, w3, w2, ada_scr, l, j, ln_g, ln_b, ident_dram):
    nc = K.nc
    pe, dve, act, pool, sp = K.pe, K.dve, K.act, K.pool_e, K.sp
    NSUP = S // 512
    with ExitStack() as es:
        w1b = es.enter_context(nc.sbuf_tensor(K.pfx + "f_w1", [128, 8, DFF], BF16))
        w3b = es.enter_context(nc.sbuf_tensor(K.pfx + "f_w3", [128, 8, DFF], BF16))
        w2b = es.enter_context(nc.sbuf_tensor(K.pfx + "f_w2", [128, NFC, D], BF16))
        xa0 = es.enter_context(nc.sbuf_tensor(K.pfx + "f_xa0", [128, D], F32))
        xa1 = es.enter_context(nc.sbuf_tensor(K.pfx + "f_xa1", [128, D], F32))
        xe0 = es.enter_context(nc.sbuf_tensor(K.pfx + "f_xe0", [128, D], F32))
        xe1 = es.enter_context(nc.sbuf_tensor(K.pfx + "f_xe1", [128, D], F32))
        xT = es.enter_context(nc.sbuf_tensor(K.pfx + "f_xT", [128, 8, 512], BF16))
        hb = es.enter_context(nc.sbuf_tensor(K.pfx + "f_h", [128, NFC, 512], BF16))
        sl0 = es.enter_context(nc.sbuf_tensor(K.pfx + "f_sl0", [128, 512], F32))
        sl1 = es.enter_context(nc.sbuf_tensor(K.pfx + "f_sl1", [128, 512], F32))
        tmp = es.enter_context(nc.sbuf_tensor(K.pfx + "f_tmp", [128, D], F32))
        tmp1 = es.enter_context(nc.sbuf_tensor(K.pfx + "f_tmp1", [128, D], F32))
        gate_bc = es.enter_context(nc.sbuf_tensor(K.pfx + "f_gate", [128, D], F32))
        lng_bc = es.enter_context(nc.sbuf_tensor(K.pfx + "f_lng", [128, D], F32))
        lnb_bc = es.enter_context(nc.sbuf_tensor(K.pfx + "f_lnb", [128, D], F32))
        shiftT = es.enter_context(nc.sbuf_tensor(K.pfx + "f_shT", [128, 8], F32))
        scale1T = es.enter_context(nc.sbuf_tensor(K.pfx + "f_scT", [128, 8], F32))
        ident = es.enter_context(nc.sbuf_tensor(K.pfx + "f_id", [128, 128], F32))
        stats = es.enter_context(nc.sbuf_tensor(K.pfx + "f_st", [128, 12], F32))
        stats1 = es.enter_context(nc.sbuf_tensor(K.pfx + "f_st1", [128, 12], F32))
        mv = es.enter_context(nc.sbuf_tensor(K.pfx + "f_mv", [128, 4], F32))
        mv1 = es.enter_context(nc.sbuf_tensor(K.pfx + "f_mv1", [128, 4], F32))
        epsc = es.enter_context(nc.sbuf_tensor(K.pfx + "f_eps", [128, 1], F32))
        p1a = es.enter_context(nc.psum_tensor(K.pfx + "f_p1a", [128, 512], F32))
        p1b = es.enter_context(nc.psum_tensor(K.pfx + "f_p1b", [128, 512], F32))
        p3a = es.enter_context(nc.psum_tensor(K.pfx + "f_p3a", [128, 512], F32))
        p3b = es.enter_context(nc.psum_tensor(K.pfx + "f_p3b", [128, 512], F32))
        yA = es.enter_context(nc.psum_tensor(K.pfx + "f_yA", [128, 512], F32))
        yB = es.enter_context(nc.psum_tensor(K.pfx + "f_yB", [128, 512], F32))
        tp0 = es.enter_context(nc.psum_tensor(K.pfx + "f_tp0", [128, 512], F32))
        tp1 = es.enter_context(nc.psum_tensor(K.pfx + "f_tp1", [128, 512], F32))
        xa, xe, slb = [xa0, xa1], [xe0, xe1], [sl0, sl1]
        p1, p3, tp = [p1a, p1b], [p3a, p3b], [tp0, tp1]
        dmod = K.dsem()
        did = K.dsem()
        dxa = [K.dsem(), K.dsem()]
        dxe = [K.dsem(), K.dsem()]
        dst = [K.dsem(sw=True), K.dsem(sw=True)]
        t_id = sp.dma(did, ident[:], ident_dram)
        t_eps = dve.op("memset", epsc[:], float(LN_EPS))
        mod_toks = load_mod_vectors(K, ada_scr, l, j, shiftT, scale1T, gate_bc, 0.5, dmod,
                                    lng_bc, lnb_bc, ln_g, ln_b)
        w1v = w1.rearrange("(k p) n -> p k n", p=128)
        w3v = w3.rearrange("(k p) n -> p k n", p=128)
        w2v = w2.rearrange("(f p) n -> p f n", p=128)
        CB = 4
        w13_tok = {}
        for g in range(0, NFC, CB):
            hi = min(NFC, g + CB)
            cs = slice(g * 128, hi * 128)
            dg = K.dsem(sw=True)
            pool.dma(dg, w1b[:, :, cs], w1v[:, :, cs])
            tb = pool.dma(dg, w3b[:, :, cs], w3v[:, :, cs])
            for f in range(g, hi):
                w13_tok[f] = tb
        w2_tok = {}
        for g in range(0, NFC, 6):
            hi = min(NFC, g + 6)
            tb = pool.dma(K.dsem(sw=True), w2b[:, g:hi, :], w2v[:, g:hi, :])
            for f in range(g, hi):
                w2_tok[f] = tb

        xin_t = x_in.rearrange("(t p) d -> t p d", p=128)
        xout_t = x_out.rearrange("(t p) d -> t p d", p=128)

        xa_free = [None, None]
        xa_ld = [None, None]
        tp_free = [None, None]
        xT_free = None
        xT_ready = None
        p13_free = [None, None]
        h_ready = None
        h_free = None
        y_free = [None, None]
        xe_free = [None, None]
        tmps, statss, mvs = [tmp, tmp1], [stats, stats1], [mv, mv1]
        tmp_free = [None, None]
        store_toks = []

        def emit_transposes(sup):
            nonlocal xT_ready
            tl = None
            for tt in range(4):
                tile_i = sup * 4 + tt
                b = tile_i % 2
                xa_ld[b] = sp.dma(dxa[b], xa[b][:], xin_t[tile_i], deps=[xa_free[b]])
                tlast = None
                for half in range(2):
                    tm = None
                    for q in range(4):
                        kc = half * 4 + q
                        tm = pe.op("transpose", tp[half][:, q * 128:(q + 1) * 128],
                                   xa[b][:, kc * 128:(kc + 1) * 128], ident[:],
                                   deps=[xa_ld[b], t_id, tp_free[half]] if q == 0 else ())
                    te = None
                    for q in range(4):
                        kc = half * 4 + q
                        te = act.op("activation", out=xT[:, kc, tt * 128:(tt + 1) * 128],
                                    in_=tp[half][:, q * 128:(q + 1) * 128], func=AF.Identity,
                                    bias=shiftT[:, kc:kc + 1], scale=scale1T[:, kc:kc + 1],
                                    deps=[tm, xT_free] + mod_toks if q == 0 else ())
                    tp_free[half] = te
                    tlast = tm
                    tl = te
                xa_free[b] = tlast
            xT_ready = tl

        emit_transposes(0)
        for sup in range(NSUP):
            tmul = None
            for fc in range(NFC):
                i = fc % 2
                cs = slice(fc * 128, (fc + 1) * 128)
                first = [w13_tok[fc], xT_ready, p13_free[i]]
                for k in range(8):
                    pe.op("matmul", p1[i][:], lhsT=w1b[:, k, cs], rhs=xT[:, k, :], start=(k == 0), stop=(k == 7),
                          deps=first if k == 0 else ())
                tm3 = None
                for k in range(8):
                    tm3 = pe.op("matmul", p3[i][:], lhsT=w3b[:, k, cs], rhs=xT[:, k, :], start=(k == 0),
                                stop=(k == 7))
                ts = act.op("activation", out=slb[i][:], in_=p1[i][:], func=AF.Silu, deps=[tm3, p13_free[i]])
                tmul = dve.op("tensor_tensor", out=hb[:, fc, :], in0=slb[i][:], in1=p3[i][:], op=ALU.mult,
                              deps=[ts, tm3, h_free])
                p13_free[i] = tmul
            h_ready = tmul
            xT_free = tm3
            if sup + 1 < NSUP:
                emit_transposes(sup + 1)
            for tt in range(4):
                tile_i = sup * 4 + tt
                b = tile_i % 2
                txe = sp.dma(dxe[b], xe[b][:], xin_t[tile_i], deps=[xe_free[b]])
                ydone = []
                for hf, yps in enumerate((yA, yB)):
                    tm = None
                    for fc in range(NFC):
                        tm = pe.op("matmul", yps[:], lhsT=hb[:, fc, tt * 128:(tt + 1) * 128],
                                   rhs=w2b[:, fc, hf * 512:(hf + 1) * 512], start=(fc == 0), stop=(fc == NFC - 1),
                                   deps=[h_ready, y_free[hf], w2_tok[fc]] if fc == 0 else [w2_tok[fc]])
                    ydone.append(tm)
                h_free = ydone[1]
                tfin, ycons = epilogue_ln(K, yA, yB, xe[b], tmps[b], statss[b], mvs[b], gate_bc, lng_bc, lnb_bc,
                                          ydone, txe, tmp_free[b], epsc[:, 0:1])
                y_free = ycons
                tmp_free[b] = tfin
                ts = pool.dma(dst[b], xout_t[tile_i], xe[b][:], deps=[tfin])
                xe_free[b] = ts
        return [dst[0].tok(), dst[1].tok()]


class MixIO:
    def __init__(self, K, es, ring, x_in, x_out, ada_scr, l, j, ln_g, ln_b, consts, TS, weight):
        nc = K.nc
        self.K, self.ring, self.TS = K, ring, TS
        sb = lambda name, shape, dt, dma=False: Buf(K, es.enter_context(nc.sbuf_tensor(K.pfx + name, shape, dt)), dma)
        self.xa = [sb("m_xa%d" % i, [128, D], F32, True) for i in range(2)]
        self.xe = [sb("m_xe%d" % i, [128, D], F32, True) for i in range(2)]
        self.st = [K.dsem(sw=True), K.dsem(sw=True)]
        self.tmp = sb("m_tmp", [128, D], F32)
        self.gate_bc = sb("m_gate", [128, D], F32, True)
        self.lng = sb("m_lng", [128, D], F32, True)
        self.lnb = sb("m_lnb", [128, D], F32, True)
        self.shT = sb("m_shT", [128, 8], F32, True)
        self.scT = sb("m_scT", [128, 8], F32, True)
        self.cst = sb("m_cst", [128, 896], F32, True)
        self.identb = sb("m_idb", [128, 128], BF16)
        self.stats = sb("m_st", [128, 12], F32)
        self.mv = sb("m_mv", [128, 4], F32)
        self.eps = sb("m_eps", [128, 2], F32)
        self.xTs = [sb("m_xT%d" % i, [128, 8, TS], BF16) for i in range(2)]
        self.xT = self.xTs[0]
        self.xin_t = x_in.rearrange("(t p) d -> t p d", p=128)
        self.xout_t = x_out.rearrange("(t p) d -> t p d", p=128)
        sp, dve = K.sp, K.dve
        self.prefetched = set()
        for ti in range(2):
            dodma(sp, self.xa[ti][:], self.xin_t[ti], W=[self.xa[ti]])
            self.prefetched.add(ti)
        dodma(sp, self.cst[:], consts, W=[self.cst])
        do(dve, "tensor_copy", out=self.identb[:], in_=self.cst[:, 0:128], R=[self.cst], W=[self.identb])
        do(dve, "memset", self.eps[:, 0:1], float(LN_EPS), W=[self.eps])
        do(dve, "memset", self.eps[:, 1:2], 1e-6, W=[self.eps])
        base = j * 3 * D
        row = ada_scr[l]
        dodma(sp, self.shT[:], row[base:base + D].rearrange("(k p) -> p k", p=128), W=[self.shT],
              allow_slow_non_contiguous=True)
        dodma(sp, self.scT[:], row[base + D:base + 2 * D].rearrange("(k p) -> p k", p=128), W=[self.scT],
              allow_slow_non_contiguous=True)
        dodma(sp, self.gate_bc[:], row[base + 2 * D:base + 3 * D].partition_broadcast(128), W=[self.gate_bc])
        dodma(sp, self.lng[:], ln_g[l, j].partition_broadcast(128), W=[self.lng])
        dodma(sp, self.lnb[:], ln_b[l, j].partition_broadcast(128), W=[self.lnb])
        do(dve, "tensor_scalar", out=self.scT[:], in0=self.scT[:], scalar1=1.0, scalar2=None, op0=ALU.add,
           R=[self.scT], W=[self.scT])
        do(dve, "tensor_scalar", out=self.gate_bc[:], in0=self.gate_bc[:], scalar1=1.0, scalar2=float(weight),
           op0=ALU.add, op1=ALU.mult, R=[self.gate_bc], W=[self.gate_bc])
        self.ident = self.cst[:, 0:128]
        self.tri = self.cst[:, 128:256]
        self.ones = self.cst[:, 256:384]
        self.rmask = self.cst[:, 384:896]

    def load_xT(self, sup, ring=None):
        K = self.K
        ring = ring if ring is not None else self.ring
        pe, act, sp = K.pe, K.act, K.sp
        xT = self.xTs[sup % 2]
        self.xT = xT
        for tt in range(self.TS // 128):
            ti = sup * (self.TS // 128) + tt
            xa = self.xa[ti % 2]
            if ti not in self.prefetched:
                dodma(sp, xa[:], self.xin_t[ti], W=[xa])
            for half in range(2):
                bk = ring.get()
                for q in range(4):
                    kc = half * 4 + q
                    do(pe, "transpose", bk[:, q * 128:(q + 1) * 128], xa[:, kc * 128:(kc + 1) * 128], self.ident,
                       R=[xa, self.cst], W=[bk])
                for q in range(4):
                    kc = half * 4 + q
                    do(act, "activation", out=xT[:, kc, tt * 128:(tt + 1) * 128],
                       in_=bk[:, q * 128:(q + 1) * 128], func=AF.Identity, bias=self.shT[:, kc:kc + 1],
                       scale=self.scT[:, kc:kc + 1], R=[bk, self.shT, self.scT], W=[xT])
        return xT

    def out_proj(self, ti, yT, wout, tcols=None):
        K, ring = self.K, self.ring
        pe, act, dve, pool, sp = K.pe, K.act, K.dve, K.pool_e, K.sp
        if tcols is None:
            tcols = slice(0, 128)
        xe = self.xe[ti % 2]
        dodma(sp, xe[:], self.xin_t[ti], W=[xe])
        ybk = [ring.get(), ring.get()]
        for hf in range(2):
            for kc in range(8):
                do(pe, "matmul", ybk[hf][:], lhsT=yT[:, kc, tcols], rhs=wout[:, kc, hf * 512:(hf + 1) * 512],
                   start=(kc == 0), stop=(kc == 7), R=[yT, wout], W=[ybk[hf]])
        tmp, stats, mv = self.tmp, self.stats, self.mv
        for hf in range(2):
            sl = slice(hf * 512, (hf + 1) * 512)
            do(dve, "tensor_tensor", out=tmp[:, sl], in0=ybk[hf][:], in1=self.gate_bc[:, sl], op=ALU.mult,
               R=[ybk[hf], self.gate_bc], W=[tmp])
            do(dve, "scalar_tensor_tensor", out=xe[:, sl], in0=xe[:, sl], scalar=float(ALPHA), in1=tmp[:, sl],
               op0=ALU.mult, op1=ALU.add, R=[tmp, xe], W=[xe])
            do(dve, "bn_stats", out=stats[:, hf * 6:(hf + 1) * 6], in_=xe[:, sl], R=[xe], W=[stats])
        do(dve, "bn_aggr", out=mv[:, 0:2], in_=stats[:, 0:12], R=[stats], W=[mv])
        do(act, "activation", out=mv[:, 2:3], in_=mv[:, 1:2], func=AF.Sqrt, bias=self.eps[:, 0:1], scale=1.0,
           R=[mv, self.eps], W=[mv])
        do(dve, "reciprocal", out=mv[:, 2:3], in_=mv[:, 2:3], R=[mv], W=[mv])
        do(dve, "scalar_tensor_tensor", out=mv[:, 3:4], in0=mv[:, 0:1], scalar=-1.0, in1=mv[:, 2:3],
           op0=ALU.mult, op1=ALU.mult, R=[mv], W=[mv])
        do(act, "activation", out=xe[:], in_=xe[:], func=AF.Identity, bias=mv[:, 3:4], scale=mv[:, 2:3],
           R=[mv, xe], W=[xe])
        do(dve, "tensor_tensor", out=xe[:], in0=xe[:], in1=self.lng[:], op=ALU.mult, R=[xe, self.lng], W=[xe])
        do(dve, "tensor_tensor", out=xe[:], in0=xe[:], in1=self.lnb[:], op=ALU.add, R=[xe, self.lnb], W=[xe])
        dodma(pool, self.xout_t[ti], xe[:], R=[xe], ds=self.st[ti % 2])

    def final_tokens(self):
        return [self.st[0].tok(), self.st[1].tok()]


def stage_rglru(K, x_in, x_out, P, ada_scr, l, ln_g, ln_b, consts):
    nc = K.nc
    pe, dve, act, pool, sp = K.pe, K.dve, K.act, K.pool_e, K.sp
    TS = 256
    NSUP = S // TS
    with ExitStack() as es:
        sb = lambda name, shape, dt, dma=False, nparts=1: Buf(
            K, es.enter_context(nc.sbuf_tensor(K.pfx + name, shape, dt)), dma, nparts)
        ringF = Ring(K, es, "r_psF", 3)
        ringA = Ring(K, es, "r_psA", 2)
        ringB = Ring(K, es, "r_psB", 2)
        ring = ringA
        io = MixIO(K, es, ring, x_in, x_out, ada_scr, l, 1, ln_g, ln_b, consts, TS, 1.0)
        win = sb("r_win", [128, 8, 2048], BF16, "sw", nparts=4)
        wout = sb("r_wout", [128, 8, D], BF16, "sw")
        waT = sb("r_waT", [128, 8, 128], BF16, "sw")
        wxT = sb("r_wxT", [128, 8, 128], BF16, "sw")
        cw = sb("r_cw", [128, 8, 4], F32, True)
        vec = sb("r_vec", [128, 4, 8], F32, True)
        csp = sb("r_csp", [128, 3, 8], F32)
        gates = [sb("r_gate%d" % i, [128, 8, TS], F32, nparts=8) for i in range(2)]
        xbrs = [sb("r_xbr%d" % i, [128, 8, TS + 3], F32, nparts=8) for i in range(2)]
        xr = sb("r_xr", [128, 8, TS], F32, nparts=8)
        xrb = sb("r_xrb", [128, 8, TS], BF16, nparts=8)
        rg = sb("r_rg", [128, 8, TS], F32, nparts=8)
        ig = sb("r_ig", [128, 8, TS], F32, nparts=8)
        aa = sb("r_aa", [128, 8, TS], F32, nparts=8)
        a2 = sb("r_a2", [128, 8, TS], F32, nparts=8)
        hs = sb("r_hs", [128, 8, TS], F32, nparts=8)
        hprev = sb("r_hp", [128, 8], F32, nparts=8)
        one_c = sb("r_one", [128, 1], F32)
        hgT = sb("r_hgT", [128, 8, TS], BF16, nparts=8)
        w_in_v = P["rglru_w_in"].rearrange("(k p) n -> p k n", p=128)
        wtok0 = None
        for g in range(4):
            cs = slice(g * 512, (g + 1) * 512)
            t_ = dodma(pool, win[:, :, cs], w_in_v[:, :, cs], W=[win.p(g)], deps=[wtok0] if wtok0 is not None else ())
            if g == 0:
                wtok0 = t_
        dodma(pool, waT[:], P["rglru_waT"].rearrange("n j i -> j n i"), W=[waT])
        dodma(pool, wxT[:], P["rglru_wxT"].rearrange("n j i -> j n i"), W=[wxT])
        dodma(pool, wout[:], P["rglru_w_out"].rearrange("(k p) n -> p k n", p=128), W=[wout])
        for jj in range(4):
            dodma(sp, cw[:, :, jj], P["rglru_conv_w"][jj].rearrange("(n p) -> p n", p=128), W=[cw],
                  allow_slow_non_contiguous=True)
        for i, nm in enumerate(("rglru_conv_b", "rglru_ba", "rglru_bx", "rglru_lambda")):
            dodma(sp, vec[:, i, :], P[nm].rearrange("(n p) -> p n", p=128), W=[vec],
                  allow_slow_non_contiguous=True)
        cb, ba, bx, lam = (vec[:, i, :] for i in range(4))
        e_ = csp[:, 0, :]
        do(act, "activation", out=e_, in_=lam, func=AF.Exp, scale=-1.0, R=[vec], W=[csp])
        do(dve, "tensor_scalar", out=csp[:, 1, :], in0=e_, scalar1=-1.0 / 3.0, scalar2=0.5, op0=ALU.mult,
           op1=ALU.add, R=[csp], W=[csp])
        do(dve, "tensor_tensor", out=csp[:, 1, :], in0=csp[:, 1, :], in1=e_, op=ALU.mult, R=[csp], W=[csp])
        do(dve, "tensor_scalar", out=csp[:, 1, :], in0=csp[:, 1, :], scalar1=-1.0, scalar2=1.0, op0=ALU.mult,
           op1=ALU.add, R=[csp], W=[csp])
        do(dve, "tensor_tensor", out=csp[:, 1, :], in0=csp[:, 1, :], in1=e_, op=ALU.mult, R=[csp], W=[csp])
        do(dve, "tensor_scalar", out=csp[:, 2, :], in0=csp[:, 1, :], scalar1=-16.0, scalar2=None, op0=ALU.mult,
           R=[csp], W=[csp])
        do(dve, "tensor_scalar", out=csp[:, 1, :], in0=csp[:, 1, :], scalar1=-8.0, scalar2=None, op0=ALU.mult,
           R=[csp], W=[csp])
        do(pool, "memset", xbrs[0][:, :, 0:3], 0.0, W=[xbrs[0]])
        do(pool, "memset", hprev[:], 0.0, W=[hprev])
        do(pool, "memset", one_c[:], 1.0, W=[one_c])

        def front(sup):
            xT = io.load_xT(sup, ringF)
            gate, xbr = gates[sup % 2], xbrs[sup % 2]
            for n in range(8):
                bk = ringF.get()
                for kc in range(8):
                    do(pe, "matmul", bk[:, 0:TS], lhsT=win[:, kc, n * 128:(n + 1) * 128], rhs=xT[:, kc, :],
                       start=(kc == 0), stop=(kc == 7), R=[win.p(n // 4), xT], W=[bk])
                do(act, "activation", out=gate[:, n, :], in_=bk[:, 0:TS], func=AF.Gelu_apprx_tanh, R=[bk], W=[gate.p(n)])
            for n in range(8):
                bk = ringF.get()
                for kc in range(8):
                    do(pe, "matmul", bk[:, 0:TS], lhsT=win[:, kc, 1024 + n * 128:1024 + (n + 1) * 128],
                       rhs=xT[:, kc, :], start=(kc == 0), stop=(kc == 7), R=[win.p(2 + n // 4), xT], W=[bk])
                do(act, "activation", out=xbr[:, n, 3:TS + 3], in_=bk[:, 0:TS], func=AF.Identity, R=[bk], W=[xbr.p(n)])

        def back_half(sup, blocks, ring):
            gate, xbr, xbr_next = gates[sup % 2], xbrs[sup % 2], xbrs[(sup + 1) % 2]
            for n in blocks:
                do(pool, "tensor_scalar", out=xr[:, n, :], in0=xbr[:, n, 3:TS + 3], scalar1=cw[:, n, 3:4],
                   scalar2=cb[:, n:n + 1], op0=ALU.mult, op1=ALU.add, R=[xbr.p(n), cw, vec], W=[xr.p(n)])
                for jj in range(3):
                    do(dve, "scalar_tensor_tensor", out=xr[:, n, :], in0=xbr[:, n, jj:jj + TS],
                       scalar=cw[:, n, jj:jj + 1], in1=xr[:, n, :], op0=ALU.mult, op1=ALU.add,
                       R=[xbr.p(n), cw, xr.p(n)], W=[xr.p(n)])
                do(pool, "tensor_copy", out=xrb[:, n, :], in_=xr[:, n, :], R=[xr.p(n)], W=[xrb.p(n)])
            for n in blocks:
                bk = ring.get()
                do(pe, "matmul", bk[:, 0:TS], lhsT=waT[:, n, :], rhs=xrb[:, n, :], start=True, stop=True,
                   R=[waT, xrb.p(n)], W=[bk])
                do(pe, "matmul", bk[:, TS:2 * TS], lhsT=wxT[:, n, :], rhs=xrb[:, n, :], start=True, stop=True,
                   R=[wxT, xrb.p(n)], W=[bk])
                do(act, "activation", out=rg[:, n, :], in_=bk[:, 0:TS], func=AF.Sigmoid, bias=ba[:, n:n + 1],
                   R=[bk, vec], W=[rg.p(n)])
                do(act, "activation", out=ig[:, n, :], in_=bk[:, TS:2 * TS], func=AF.Sigmoid, bias=bx[:, n:n + 1],
                   R=[bk, vec], W=[ig.p(n)])
                do(pool, "tensor_tensor", out=ig[:, n, :], in0=ig[:, n, :], in1=xr[:, n, :], op=ALU.mult,
                   R=[ig.p(n), xr.p(n)], W=[ig.p(n)])
            for n in blocks:
                do(act, "activation", out=aa[:, n, :], in_=rg[:, n, :], func=AF.Exp, scale=csp[:, 1, n:n + 1],
                   R=[rg.p(n), csp], W=[aa.p(n)])
                do(act, "activation", out=a2[:, n, :], in_=rg[:, n, :], func=AF.Exp, scale=csp[:, 2, n:n + 1],
                   R=[rg.p(n), csp], W=[a2.p(n)])
            for n in blocks:
                do(act, "activation", out=a2[:, n, :], in_=a2[:, n, :], func=AF.Sqrt, scale=-1.0, bias=one_c[:, 0:1],
                   R=[a2.p(n), one_c], W=[a2.p(n)])
            for n in blocks:
                do(dve, "tensor_tensor", out=ig[:, n, :], in0=ig[:, n, :], in1=a2[:, n, :], op=ALU.mult,
                   R=[ig.p(n), a2.p(n)], W=[ig.p(n)])
                do(dve, "tensor_tensor_scan", out=hs[:, n, :], data0=aa[:, n, :], data1=ig[:, n, :],
                   initial=hprev[:, n:n + 1], op0=ALU.mult, op1=ALU.add, R=[aa.p(n), ig.p(n), hprev.p(n)],
                   W=[hs.p(n)])
                do(dve, "tensor_copy", out=hprev[:, n:n + 1], in_=hs[:, n, TS - 1:TS], R=[hs.p(n)], W=[hprev.p(n)])
                do(pool, "tensor_tensor", out=hgT[:, n, :], in0=hs[:, n, :], in1=gate[:, n, :], op=ALU.mult,
                   R=[hs.p(n), gate.p(n)], W=[hgT.p(n)])

        def tail(sup):
            xbr, xbr_next = xbrs[sup % 2], xbrs[(sup + 1) % 2]
            do(pool, "tensor_copy", out=xbr_next[:, :, 0:3], in_=xbr[:, :, TS:TS + 3], R=[xbr], W=[xbr_next])
            for tt in range(TS // 128):
                io.out_proj(sup * (TS // 128) + tt, hgT, wout, slice(tt * 128, (tt + 1) * 128))

        front(0)
        for sup in range(NSUP):
            nxt = record(front, sup + 1) if sup + 1 < NSUP else []
            cur = merge_lists(record(back_half, sup, range(0, 4), ringA), record(back_half, sup, range(4, 8), ringB))
            cur += record(tail, sup)
            emit_merged(nxt, cur)
        return io.final_tokens()


def stage_ab(K, x_in, x_out, P, ada_scr, l, ln_g, ln_b, consts):
    nc = K.nc
    pe, dve, act, pool, sp = K.pe, K.dve, K.act, K.pool_e, K.sp
    TS = 256
    NSUP = S // TS
    CQ, CF, CI, CG, CX, CV, CZ, CGT = 0, 512, 1024, 1536, 2048, 2560, 3072, 3584
    with ExitStack() as es:
        sb = lambda name, shape, dt, dma=False, nparts=1: Buf(
            K, es.enter_context(nc.sbuf_tensor(K.pfx + name, shape, dt)), dma, nparts)
        ringF = Ring(K, es, "a_psF", 2)
        ringM = Ring(K, es, "a_psM", 4)
        ringH = Ring(K, es, "a_psH", 2)
        ring = ringM
        io = MixIO(K, es, ring, x_in, x_out, ada_scr, l, 1, ln_g, ln_b, consts, TS, 1.0)
        tri, ones, rmask, identb = io.tri, io.ones, io.rmask, io.identb
        cst = io.cst
        win = sb("a_win", [128, 8, AB_IN], BF16, "sw", nparts=8)
        wout = sb("a_wout", [128, 8, D], BF16, "sw")
        wst = sb("a_wst", [128, 2, 4, 128], F32, True)
        wq_b = sb("a_wq_b", [128, 4, 128], BF16)
        wk_b = sb("a_wk_b", [128, 4, 128], BF16)
        lg = sb("a_lg", [128, 3, 4], F32, True)
        lbv = sb("a_lb", [128, 2, 4], F32)
        cw = sb("a_cw", [128, 4, 4], F32, True)
        cbv = sb("a_cb", [128, 4], F32, True)
        hg_bc = sb("a_hg", [128, 512], F32, True)
        mg_bc = sb("a_mg", [128, 512], F32, True)
        sk_bc = sb("a_sk", [128, 512], F32, True)
        gb_bc = sb("a_gb", [128, 8], F32, True)
        qs = sb("a_qs", [128, 4, TS], F32)
        acc = qs
        fg = sb("a_fg", [128, 4, TS], F32)
        lfb = sb("a_lfb", [128, 4, TS], F32)
        enb = lfb
        bb = sb("a_bb", [128, 4, TS], F32)
        eb = sb("a_eb", [128, 4, TS], F32)
        ebends = [sb("a_ebe%d" % i, [128, 4, TS // 64], F32) for i in range(2)]
        qTs = [sb("a_qT%d" % i, [128, 4, TS], BF16) for i in range(2)]
        kTs = [sb("a_kT%d" % i, [128, 4, TS], BF16) for i in range(2)]
        xbx = sb("a_xbx", [128, 4, TS + 3], F32)
        xcTs = [sb("a_xcT%d" % i, [128, 4, TS], BF16) for i in range(2)]
        qbTs = [sb("a_qbT%d" % i, [128, 4, TS], BF16) for i in range(2)]
        kbTs = [sb("a_kbT%d" % i, [128, 4, TS], BF16) for i in range(2)]
        va = sb("a_va", [64, 512], BF16)
        ga = sb("a_ga", [64, 512], F32)
        vb = sb("a_vb", [128, 4, 132], BF16)
        vhat = sb("a_vhat", [128, 4, 132], BF16)
        zb = sb("a_zb", [128, 512], F32)
        kbtm = sb("a_kbtm", [128, 512], BF16)
        xctm = sb("a_xctm", [128, 512], BF16)
        gt = sb("a_gt", [128, 8], F32)
        gw = sb("a_gw", [128, 16], F32)
        ex = sb("a_ex", [128, 16], F32)
        bcum = sb("a_bcum", [128, 8], F32)
        Sst = sb("a_S", [128, 4, 128], F32)
        Sbf = sb("a_Sbf", [128, 4, 128], BF16)
        Stmp = sb("a_Stmp", [128, 4, 128], F32)
        Cst = sb("a_C", [128, 4, 132], F32)
        Cbf = sb("a_Cbf", [128, 4, 132], BF16)
        attm = sb("a_attm", [64, 4, 64], BF16)
        kTM = sb("a_kTM", [64, 512], BF16)
        osb = sb("a_osb", [64, 4, 128], F32)
        sq = sb("a_sq", [64, 4, 128], F32)
        ssv = sb("a_ss", [64, 8], F32)
        ya = sb("a_ya", [64, 512], BF16)
        AT = sb("a_AT", [128, 4, 128], BF16)
        hsb = sb("a_hsb", [128, 4, 128], F32)
        dn = sb("a_dn", [128, 16], F32)
        hst = sb("a_hst", [128, 4, 6], F32)
        hmv = sb("a_hmv", [128, 4, 2], F32)
        hrs = sb("a_hrs", [128, 8], F32)
        yb = sb("a_yb", [128, 512], BF16)
        tmp2 = sb("a_tmp2", [128, 512], F32)
        yT = sb("a_yT", [128, 8, 128], BF16, nparts=2)

        w_in_v = P["ab_w_in"].rearrange("(k p) n -> p k n", p=128)
        wtok = {}
        for c0, after in ((CQ, None), (CF, CQ), (CX, CQ), (CI, CX), (CG, CX), (CV, CX), (CZ, CX)):
            wtok[c0] = dodma(pool, win[:, :, c0:c0 + 512], w_in_v[:, :, c0:c0 + 512], W=[win.p(c0 // 512)],
                             deps=[wtok[after]] if after is not None else ())
        dodma(pool, win[:, :, CGT:CGT + 8], w_in_v[:, :, CGT:CGT + 8], W=[win.p(CGT // 512)], deps=[wtok[CX]])
        dodma(pool, wout[:], P["ab_w_out"].rearrange("(k p) n -> p k n", p=128), W=[wout])
        dodma(sp, wst[:, 0, :, :], P["mlstm_wq_bd"].rearrange("h p f -> p h f"), W=[wst])
        dodma(sp, wst[:, 1, :, :], P["mlstm_wk_bd"].rearrange("h p f -> p h f"), W=[wst])
        do(dve, "tensor_copy", out=wq_b[:], in_=wst[:, 0, :, :], R=[wst], W=[wq_b])
        do(dve, "tensor_scalar", out=wk_b[:], in0=wst[:, 1, :, :], scalar1=float(128.0 ** -0.5),
           scalar2=None, op0=ALU.mult, R=[wst], W=[wk_b])
        for li_ in range(3):
            dodma(sp, lg[:, li_, :], P["hgrn_lb_logits"][li_].rearrange("(h p) -> p h", p=128), W=[lg],
                  allow_slow_non_contiguous=True)
        for jj in range(4):
            dodma(sp, cw[:, :, jj], P["mlstm_conv_w"][jj].rearrange("(h p) -> p h", p=128), W=[cw],
                  allow_slow_non_contiguous=True)
        dodma(sp, cbv[:], P["mlstm_conv_b"].rearrange("(h p) -> p h", p=128), W=[cbv],
              allow_slow_non_contiguous=True)
        dodma(sp, hg_bc[:], P["hgrn_norm_g"].partition_broadcast(128), W=[hg_bc])
        dodma(sp, mg_bc[:], P["mlstm_norm_g"].partition_broadcast(128), W=[mg_bc])
        dodma(sp, sk_bc[:], P["mlstm_skip"].partition_broadcast(128), W=[sk_bc])
        dodma(sp, gb_bc[:], P["mlstm_gate_b"].partition_broadcast(128), W=[gb_bc])
        do(act, "activation", out=lg[:], in_=lg[:], func=AF.Exp, R=[lg], W=[lg])
        do(dve, "tensor_tensor", out=lbv[:, 1, :], in0=lg[:, 0, :], in1=lg[:, 1, :], op=ALU.add, R=[lg], W=[lbv])
        do(dve, "tensor_tensor", out=lbv[:, 1, :], in0=lbv[:, 1, :], in1=lg[:, 2, :], op=ALU.add, R=[lg, lbv],
           W=[lbv])
        do(dve, "reciprocal", out=lbv[:, 1, :], in_=lbv[:, 1, :], R=[lbv], W=[lbv])
        do(dve, "tensor_tensor", out=lbv[:, 0, :], in0=lg[:, 0, :], in1=lbv[:, 1, :], op=ALU.mult, R=[lg, lbv],
           W=[lbv])
        do(dve, "tensor_scalar", out=lbv[:, 1, :], in0=lbv[:, 0, :], scalar1=-1.0, scalar2=1.0, op0=ALU.mult,
           op1=ALU.add, R=[lbv], W=[lbv])
        do(pool, "memset", Sst[:], 0.0, W=[Sst])
        do(pool, "memset", Sbf[:], 0.0, W=[Sbf])
        do(pool, "memset", Cst[:], 0.0, W=[Cst])
        do(pool, "memset", Cbf[:], 0.0, W=[Cbf])
        do(pool, "memset", vb[:], 1.0, W=[vb])
        do(pool, "memset", xbx[:, :, 0:3], 0.0, W=[xbx])

        def proj_fm(xT, col0, h, bk):
            for kc in range(8):
                do(pe, "matmul", bk[:, 0:TS], lhsT=win[:, kc, col0 + h * 128:col0 + (h + 1) * 128], rhs=xT[:, kc, :],
                   start=(kc == 0), stop=(kc == 7), R=[win.p(col0 // 512), xT], W=[bk])

        def proj_tm(xT, col0, ncol, tcols, bk, m=128):
            for kc in range(8):
                do(pe, "matmul", bk[0:m, 0:ncol], lhsT=xT[:, kc, tcols], rhs=win[:, kc, col0:col0 + ncol],
                   start=(kc == 0), stop=(kc == 7), R=[win.p(col0 // 512), xT], W=[bk])


        def front(sup):
            p_ = sup % 2
            qT, kT, xcT, qbT, kbT, ebe = qTs[p_], kTs[p_], xcTs[p_], qbTs[p_], kbTs[p_], ebends[p_]
            xT = io.load_xT(sup, ringF)
            for h in range(4):
                bk = ringF.get()
                proj_fm(xT, CQ, h, bk)
                do(act, "activation", out=qs[:, h, :], in_=bk[:, 0:TS], func=AF.Silu, R=[bk], W=[qs])
            for h in range(4):
                bk = ringF.get()
                proj_fm(xT, CF, h, bk)
                do(act, "activation", out=fg[:, h, :], in_=bk[:, 0:TS], func=AF.Sigmoid, R=[bk], W=[fg])
            for h in range(4):
                bk = ringF.get()
                proj_fm(xT, CX, h, bk)
                do(act, "activation", out=xbx[:, h, 3:TS + 3], in_=bk[:, 0:TS], func=AF.Identity, R=[bk], W=[xbx])
            for h in range(4):
                do(dve, "tensor_scalar", out=fg[:, h, :], in0=fg[:, h, :], scalar1=lbv[:, 1, h:h + 1],
                   scalar2=lbv[:, 0, h:h + 1], op0=ALU.mult, op1=ALU.add, R=[fg, lbv], W=[fg])
            do(act, "activation", out=lfb[:], in_=fg[:], func=AF.Ln, R=[fg], W=[lfb])
            for h in range(4):
                do(dve, "tensor_tensor_scan", out=bb[:, h, :], data0=rmask[:, 0:TS], data1=lfb[:, h, :], initial=0.0,
                   op0=ALU.mult, op1=ALU.add, R=[cst, lfb], W=[bb])
            do(act, "activation", out=eb[:], in_=bb[:], func=AF.Exp, R=[bb], W=[eb])
            do(act, "activation", out=enb[:], in_=bb[:], func=AF.Exp, scale=-1.0, R=[bb], W=[enb])
            for cq_ in range(TS // 64):
                do(dve, "tensor_copy", out=ebe[:, :, cq_], in_=eb[:, :, cq_ * 64 + 63], R=[eb], W=[ebe])
            do(dve, "tensor_tensor", out=qT[:], in0=qs[:], in1=eb[:], op=ALU.mult, R=[qs, eb], W=[qT])
            do(pool, "tensor_scalar", out=fg[:], in0=fg[:], scalar1=-1.0, scalar2=1.0, op0=ALU.mult, op1=ALU.add,
               R=[fg], W=[fg])
            do(pool, "tensor_tensor", out=kT[:], in0=fg[:], in1=enb[:], op=ALU.mult, R=[fg, enb], W=[kT])
            for h in range(4):
                do(dve, "tensor_scalar", out=acc[:, h, :], in0=xbx[:, h, 3:TS + 3], scalar1=cw[:, h, 3:4],
                   scalar2=None, op0=ALU.mult, R=[xbx, cw], W=[acc])
                for jj in range(3):
                    do(dve, "scalar_tensor_tensor", out=acc[:, h, :], in0=xbx[:, h, jj:jj + TS],
                       scalar=cw[:, h, jj:jj + 1], in1=acc[:, h, :], op0=ALU.mult, op1=ALU.add,
                       R=[xbx, cw, acc], W=[acc])
            for h in range(4):
                do(act, "activation", out=xcT[:, h, :], in_=acc[:, h, :], func=AF.Silu, bias=cbv[:, h:h + 1],
                   R=[acc, cbv], W=[xcT])
            do(pool, "tensor_copy", out=xbx[:, :, 0:3], in_=xbx[:, :, TS:TS + 3], R=[xbx], W=[xbx])
            for h in range(4):
                bk = ringF.get()
                do(pe, "matmul", bk[:, 0:TS], lhsT=wq_b[:, h, :], rhs=xcT[:, h, :], start=True, stop=True,
                   R=[wq_b, xcT], W=[bk])
                do(pe, "matmul", bk[:, TS:2 * TS], lhsT=wk_b[:, h, :], rhs=xcT[:, h, :], start=True, stop=True,
                   R=[wk_b, xcT], W=[bk])
                do(act, "activation", out=qbT[:, h, :], in_=bk[:, 0:TS], func=AF.Identity, R=[bk], W=[qbT])
                do(act, "activation", out=kbT[:, h, :], in_=bk[:, TS:2 * TS], func=AF.Identity, R=[bk], W=[kbT])


        def M(sup, tt):
            p_ = sup % 2
            qT, kT, xcT, qbT, kbT, ebe = qTs[p_], kTs[p_], xcTs[p_], qbTs[p_], kbTs[p_], ebends[p_]
            xT = io.xTs[p_]
            ti = sup * (TS // 128) + tt
            ts = slice(tt * 128, (tt + 1) * 128)
            bk = ringM.get()
            proj_tm(xT, CV, 512, ts, bk)
            do(act, "activation", out=vb[:, :, 0:128], in_=bk[:, 0:512].rearrange("p (h d) -> p h d", h=4),
               func=AF.Identity, R=[bk], W=[vb])
            bk = ringM.get()
            proj_tm(xT, CZ, 512, ts, bk)
            do(act, "activation", out=zb[:], in_=bk[:, 0:512], func=AF.Silu, R=[bk], W=[zb])
            bk = ringM.get()
            proj_tm(xT, CGT, 8, ts, bk)
            do(dve, "tensor_tensor", out=gt[:], in0=bk[:, 0:8], in1=gb_bc[:], op=ALU.add, R=[bk, gb_bc], W=[gt])
            bkb = ringM.get()
            bkb_bf = bkb[:, :].bitcast(BF16)
            for h in range(4):
                do(pe, "transpose", bkb_bf[:, h * 128:(h + 1) * 128], xcT[:, h, ts], identb[:],
                   R=[xcT, identb], W=[bkb])
            do(act, "activation", out=xctm[:], in_=bkb_bf[:, 0:512], func=AF.Identity, R=[bkb], W=[xctm])
            bk = ringM.get()
            for h in range(4):
                do(pe, "matmul", bk[:, h * 128:(h + 1) * 128], lhsT=xcT[:, h, ts], rhs=wk_b[:, h, :],
                   start=True, stop=True, R=[xcT, wk_b], W=[bk])
            do(dve, "tensor_copy", out=kbtm[:], in_=bk[:, 0:512], R=[bk], W=[kbtm])
            do(act, "activation", out=gw[:, 0:4], in_=gt[:, 4:8], func=AF.Exp, scale=-1.0, R=[gt], W=[gw])
            do(act, "activation", out=gw[:, 0:4], in_=gw[:, 0:4], func=AF.Ln, bias=io.eps[:, 1:2] if False else 1.0,
               R=[gw], W=[gw])
            do(dve, "tensor_scalar", out=gw[:, 4:8], in0=gw[:, 0:4], scalar1=-1.0, scalar2=None, op0=ALU.mult,
               R=[gw], W=[gw])
            bk = ringM.get()
            do(pe, "matmul", bk[:, 0:4], lhsT=tri, rhs=gw[:, 4:8], start=True, stop=True, R=[cst, gw], W=[bk])
            do(pe, "matmul", bk[:, 4:8], lhsT=ones, rhs=gw[:, 4:8], start=True, stop=True, R=[cst, gw], W=[bk])
            do(dve, "tensor_copy", out=bcum[:], in_=bk[:, 0:8], R=[bk], W=[bcum])
            do(dve, "tensor_copy", out=gw[:, 8:16], in_=bcum[:], R=[bcum], W=[gw])
            do(dve, "tensor_tensor", out=dn[:, 0:4], in0=gt[:, 0:4], in1=bcum[:, 0:4], op=ALU.subtract,
               R=[gt, bcum], W=[dn])
            do(dve, "tensor_tensor", out=dn[:, 4:8], in0=dn[:, 0:4], in1=bcum[:, 4:8], op=ALU.add,
               R=[dn, bcum], W=[dn])
            do(act, "activation", out=ex[:, 0:8], in_=gw[:, 8:16], func=AF.Exp, R=[gw], W=[ex])
            do(act, "activation", out=ex[:, 8:16], in_=dn[:, 0:8], func=AF.Exp, R=[dn], W=[ex])
            ebt, dec, wsc, usc = (lambda o: (lambda h: ex[:, o + h:o + h + 1]))(0), \
                (lambda h: ex[:, 4 + h:5 + h]), (lambda h: ex[:, 8 + h:9 + h]), (lambda h: ex[:, 12 + h:13 + h])
            bk = ringM.get()
            for h in range(4):
                do(pe, "matmul", bk[:, h * 128:(h + 1) * 128], lhsT=kbT[:, h, ts], rhs=qbT[:, h, ts],
                   start=True, stop=True, R=[kbT, qbT], W=[bk])
            for h in range(4):
                do(dve, "scalar_tensor_tensor", out=AT[:, h, :], in0=bk[:, h * 128:(h + 1) * 128], scalar=wsc(h),
                   in1=tri, op0=ALU.mult, op1=ALU.mult, R=[bk, ex, cst], W=[AT])
                do(dve, "tensor_scalar", out=vhat[:, h, 0:129], in0=vb[:, h, 0:129], scalar1=usc(h),
                   scalar2=None, op0=ALU.mult, R=[vb, ex], W=[vhat])
            rb = [ringM.get(), ringM.get()]
            for h in range(4):
                o = (h % 2) * 129
                do(pe, "matmul", rb[h // 2][:, o:o + 129], lhsT=AT[:, h, :], rhs=vb[:, h, 0:129], start=True,
                   stop=False, R=[AT, vb], W=[rb[h // 2]])
                do(pe, "matmul", rb[h // 2][:, o:o + 129], lhsT=qbT[:, h, ts], rhs=Cbf[:, h, 0:129], start=False,
                   stop=True, R=[qbT, Cbf], W=[rb[h // 2]])
            cbk = [ringM.get(), ringM.get()]
            for h in range(4):
                o = (h % 2) * 129
                do(pe, "matmul", cbk[h // 2][:, o:o + 129], lhsT=kbtm[:, h * 128:(h + 1) * 128],
                   rhs=vhat[:, h, 0:129], start=True, stop=True, R=[kbtm, vhat], W=[cbk[h // 2]])
            for g in range(2):
                do(dve, "tensor_tensor", out=dn[:, 8 + 2 * g:10 + 2 * g],
                   in0=rb[g][:, 0:258].rearrange("p (h d) -> p h d", h=2)[:, :, 128],
                   in1=ex[:, 2 * g:2 * g + 2], op=ALU.mult, R=[rb[g], ex], W=[dn])
            do(dve, "tensor_scalar", out=gw[:, 0:4], in0=dn[:, 8:12], scalar1=-1.0, scalar2=1.0, op0=ALU.mult,
               op1=ALU.max, R=[dn], W=[gw])
            do(dve, "tensor_tensor", out=dn[:, 8:12], in0=dn[:, 8:12], in1=gw[:, 0:4], op=ALU.max, R=[dn, gw],
               W=[dn])
            do(dve, "reciprocal", out=dn[:, 8:12], in_=dn[:, 8:12], R=[dn], W=[dn])
            do(dve, "tensor_tensor", out=dn[:, 12:16], in0=dn[:, 8:12], in1=ex[:, 0:4], op=ALU.mult, R=[dn, ex],
               W=[dn])
            for h in range(4):
                o = (h % 2) * 129
                do(act, "activation", out=hsb[:, h, :], in_=rb[h // 2][:, o:o + 128], func=AF.Copy,
                   scale=dn[:, 12 + h:13 + h], R=[rb[h // 2], dn], W=[hsb])
            for h in range(4):
                o = (h % 2) * 129
                do(dve, "scalar_tensor_tensor", out=Cst[:, h, 0:129], in0=Cst[:, h, 0:129], scalar=dec(h),
                   in1=cbk[h // 2][:, o:o + 129], op0=ALU.mult, op1=ALU.add, R=[Cst, ex, cbk[h // 2]], W=[Cst])
            do(pool, "tensor_copy", out=Cbf[:], in_=Cst[:], R=[Cst], W=[Cbf])
            for h in range(4):
                do(dve, "bn_stats", out=hst[:, h, :], in_=hsb[:, h, :], R=[hsb], W=[hst])
            for h in range(4):
                do(dve, "bn_aggr", out=hmv[:, h, :], in_=hst[:, h, :], R=[hst], W=[hmv])
            do(act, "activation", out=hrs[:, 0:4], in_=hmv[:, :, 1], func=AF.Sqrt, bias=io.eps[:, 1:2], scale=1.0,
               R=[hmv, io.eps], W=[hrs])
            do(dve, "reciprocal", out=hrs[:, 0:4], in_=hrs[:, 0:4], R=[hrs], W=[hrs])
            do(dve, "scalar_tensor_tensor", out=hrs[:, 4:8], in0=hmv[:, :, 0], scalar=-1.0, in1=hrs[:, 0:4],
               op0=ALU.mult, op1=ALU.mult, R=[hmv, hrs], W=[hrs])
            for h in range(4):
                do(act, "activation", out=hsb[:, h, :], in_=hsb[:, h, :], func=AF.Identity,
                   bias=hrs[:, 4 + h:5 + h], scale=hrs[:, h:h + 1], R=[hsb, hrs], W=[hsb])
            hflat = hsb[:, :, :].rearrange("p h d -> p (h d)")
            do(dve, "tensor_tensor", out=hflat, in0=hflat, in1=mg_bc[:], op=ALU.mult, R=[hsb, mg_bc], W=[hsb])
            do(dve, "tensor_tensor", out=tmp2[:], in0=xctm[:], in1=sk_bc[:], op=ALU.mult, R=[xctm, sk_bc],
               W=[tmp2])
            do(dve, "tensor_tensor", out=hflat, in0=hflat, in1=tmp2[:], op=ALU.add, R=[hsb, tmp2], W=[hsb])
            do(dve, "tensor_tensor", out=yb[:], in0=hflat, in1=zb[:], op=ALU.mult, R=[hsb, zb], W=[yb])
            bkb = ringM.get()
            bkb_bf = bkb[:, :].bitcast(BF16)
            for h in range(4):
                do(pe, "transpose", bkb_bf[:, h * 128:(h + 1) * 128], yb[:, h * 128:(h + 1) * 128], identb[:],
                   R=[yb, identb], W=[bkb])
            do(act, "activation", out=yT[:, 4:8, :], in_=bkb_bf[:, 0:512].rearrange("p (h d) -> p h d", h=4),
               func=AF.Identity, R=[bkb], W=[yT.p(1)])

        def H(sup, tt):
            p_ = sup % 2
            qT, kT, xcT, qbT, kbT, ebe = qTs[p_], kTs[p_], xcTs[p_], qbTs[p_], kbTs[p_], ebends[p_]
            xT = io.xTs[p_]
            ti = sup * (TS // 128) + tt
            ts = slice(tt * 128, (tt + 1) * 128)
            for c in range(2):
                cs = slice(tt * 128 + c * 64, tt * 128 + (c + 1) * 64)
                cq = tt * 2 + c
                bk = ringH.get()
                proj_tm(xT, CI, 512, cs, bk, m=64)
                do(act, "activation", out=va[:], in_=bk[0:64, 0:512], func=AF.Identity, R=[bk], W=[va])
                bk = ringH.get()
                proj_tm(xT, CG, 512, cs, bk, m=64)
                do(act, "activation", out=ga[:], in_=bk[0:64, 0:512], func=AF.Silu, R=[bk], W=[ga])
                bk = ringH.get()
                for h in range(4):
                    do(pe, "matmul", bk[0:64, h * 64:(h + 1) * 64], lhsT=kT[:, h, cs], rhs=qT[:, h, cs],
                       start=True, stop=True, R=[kT, qT], W=[bk])
                for h in range(4):
                    do(dve, "tensor_tensor", out=attm[:, h, :], in0=bk[0:64, h * 64:(h + 1) * 64],
                       in1=cst[0:64, 128:192], op=ALU.mult, R=[bk, cst], W=[attm])
                bkb = ringH.get()
                bkb_bf = bkb[:, :].bitcast(BF16)
                for h in range(4):
                    do(pe, "transpose", bkb_bf[0:64, h * 128:(h + 1) * 128], kT[:, h, cs], identb[:],
                       R=[kT, identb], W=[bkb])
                do(act, "activation", out=kTM[:], in_=bkb_bf[0:64, 0:512], func=AF.Identity, R=[bkb], W=[kTM])
                obk = ringH.get()
                for h in range(4):
                    do(pe, "matmul", obk[0:64, h * 128:(h + 1) * 128], lhsT=attm[:, h, :],
                       rhs=va[:, h * 128:(h + 1) * 128], start=True, stop=False, R=[attm, va], W=[obk])
                    do(pe, "matmul", obk[0:64, h * 128:(h + 1) * 128], lhsT=qT[:, h, cs], rhs=Sbf[:, h, :],
                       start=False, stop=True, R=[qT, Sbf], W=[obk])
                sbk = ringH.get()
                for h in range(4):
                    do(pe, "matmul", sbk[:, h * 128:(h + 1) * 128], lhsT=kTM[:, h * 128:(h + 1) * 128],
                       rhs=va[:, h * 128:(h + 1) * 128], start=True, stop=True, R=[kTM, va], W=[sbk])
                do(dve, "tensor_tensor", out=Stmp[:], in0=Sst[:], in1=sbk[:, 0:512].rearrange("p (h d) -> p h d", h=4),
                   op=ALU.add, R=[Sst, sbk], W=[Stmp])
                for h in range(4):
                    do(dve, "tensor_scalar", out=Sst[:, h, :], in0=Stmp[:, h, :], scalar1=ebe[:, h, cq:cq + 1],
                       scalar2=None, op0=ALU.mult, R=[Stmp, ebe], W=[Sst])
                do(pool, "tensor_copy", out=Sbf[:], in_=Sst[:], R=[Sst], W=[Sbf])
                do(act, "activation", out=osb[:], in_=obk[0:64, 0:512].rearrange("p (h d) -> p h d", h=4),
                   func=AF.Identity, R=[obk], W=[osb])
                do(dve, "tensor_tensor", out=sq[:], in0=osb[:], in1=osb[:], op=ALU.mult, R=[osb], W=[sq])
                for h in range(4):
                    do(dve, "tensor_reduce", out=ssv[:, h:h + 1], in_=sq[:, h, :], axis=mybir.AxisListType.X,
                       op=ALU.add, R=[sq], W=[ssv])
                do(act, "activation", out=ssv[:, 4:8], in_=ssv[:, 0:4], func=AF.Sqrt, bias=io.eps[0:64, 1:2],
                   scale=1.0 / 128.0, R=[ssv, io.eps], W=[ssv])
                do(dve, "reciprocal", out=ssv[:, 4:8], in_=ssv[:, 4:8], R=[ssv], W=[ssv])
                for h in range(4):
                    do(dve, "tensor_scalar", out=osb[:, h, :], in0=osb[:, h, :], scalar1=ssv[:, 4 + h:5 + h],
                       scalar2=None, op0=ALU.mult, R=[osb, ssv], W=[osb])
                oflat = osb[:, :, :].rearrange("p h d -> p (h d)")
                do(dve, "tensor_tensor", out=oflat, in0=oflat, in1=hg_bc[0:64, :], op=ALU.mult, R=[osb, hg_bc],
                   W=[osb])
                do(dve, "tensor_tensor", out=ya[:], in0=oflat, in1=ga[:], op=ALU.mult, R=[osb, ga], W=[ya])
                bkb = ringH.get()
                bkb_bf = bkb[:, :].bitcast(BF16)
                for h in range(4):
                    do(pe, "transpose", bkb_bf[:, h * 64:(h + 1) * 64], ya[:, h * 128:(h + 1) * 128],
                       identb[0:64, 0:64], R=[ya, identb], W=[bkb])
                do(act, "activation", out=yT[:, 0:4, c * 64:(c + 1) * 64],
                   in_=bkb_bf[:, 0:256].rearrange("p (h d) -> p h d", h=4), func=AF.Identity, R=[bkb], W=[yT.p(0)])

        def O(sup, tt):
            io.out_proj(sup * (TS // 128) + tt, yT, wout)

        front(0)
        for sup in range(NSUP):
            nxt = record(front, sup + 1) if sup + 1 < NSUP else []
            cur = []
            for tt in range(TS // 128):
                cur += merge_lists(record(M, sup, tt), record(H, sup, tt))
                cur += record(O, sup, tt)
            emit_merged(nxt, cur)
        return io.final_tokens()


ALL_STAGES = ("prep", "ffn00", "ab", "ffn02", "ffn10", "rglru", "ffn12")

PARAM_SHAPES = {
    "ada_w": [2, D, 9 * D], "ada_b": [2, 9 * D], "ln_g": [2, 3, D], "ln_b": [2, 3, D],
    "ffn_w1": [2, 2, D, DFF], "ffn_w3": [2, 2, D, DFF], "ffn_w2": [2, 2, DFF, D],
    "hgrn_lb_logits": [3, 512], "ab_w_in": [D, AB_IN], "ab_w_out": [D, D], "hgrn_norm_g": [512],
    "mlstm_conv_w": [4, 512], "mlstm_conv_b": [512], "mlstm_wq_bd": [4, 128, 128], "mlstm_wk_bd": [4, 128, 128],
    "mlstm_gate_b": [8], "mlstm_skip": [512], "mlstm_norm_g": [512],
    "rglru_w_in": [D, 2 * D], "rglru_conv_w": [4, D], "rglru_conv_b": [D], "rglru_waT": [8, 128, 128],
    "rglru_ba": [D], "rglru_wxT": [8, 128, 128], "rglru_bx": [D], "rglru_lambda": [D], "rglru_w_out": [D, D],
    "consts": [128, 896],
}


def build_program(stages=ALL_STAGES):
    nc = bass.Bass("TRN2", target_bir_lowering=False)
    dt = lambda name, shape, kind="ExternalInput": nc.dram_tensor(name, list(shape), F32, kind=kind).ap()
    x = dt("x", [S, D])
    c = dt("c", [D])
    P = {k: dt(k, shp) for k, shp in PARAM_SHAPES.items()}
    out = dt("out", [S, D], kind="ExternalOutput")
    ada_scr = dt("ada_scr", [2, 9 * D], kind="Internal")
    xs = [dt("xs%d" % i, [S, D], kind="Internal") for i in range(2)]
    ident = P["consts"][:, 0:128]

    K = Kern(nc, 0)
    cur = x
    nxt = 0
    for si, st in enumerate(stages):
        last = si == len(stages) - 1
        K.pfx = "s%d_" % si
        if st == "prep":
            toks = stage_prep(K, c, P["ada_w"], P["ada_b"], ada_scr)
        else:
            dst = out if last else xs[nxt]
            if st.startswith("ffn"):
                l, j = int(st[3]), int(st[4])
                f = 0 if j == 0 else 1
                toks = stage_ffn(K, cur, dst, P["ffn_w1"][l, f], P["ffn_w3"][l, f], P["ffn_w2"][l, f], ada_scr,
                                 l, j, P["ln_g"], P["ln_b"], ident)
            elif st == "ab":
                toks = stage_ab(K, cur, dst, P, ada_scr, 0, P["ln_g"], P["ln_b"], P["consts"])
            elif st == "rglru":
                toks = stage_rglru(K, cur, dst, P, ada_scr, 1, P["ln_g"], P["ln_b"], P["consts"])
            cur = dst
            nxt ^= 1
        K.barrier(toks)
    return nc


def make_consts():
    cst = np.zeros((128, 896), np.float32)
    cst[:, 0:128] = np.eye(128, dtype=np.float32)
    cst[:, 128:256] = np.triu(np.ones((128, 128), np.float32))
    cst[:, 256:384] = 1.0
    rm = np.ones(512, np.float32)
    rm[0::64] = 0.0
    cst[:, 384:896] = rm[None, :]
    return cst


def block_diag_layout(w):
    o = np.zeros((4, 128, 128), np.float32)
    for h in range(4):
        for n in range(32):
            o[h, 4 * n:4 * n + 4, 4 * n:4 * n + 4] = w[h * 32 + n].T
    return o


def shared_inputs(inputs):
    g = lambda k: np.ascontiguousarray(np.asarray(inputs[k], dtype=np.float32))
    m = {k: g(k) for k in ("ada_w", "ada_b", "ln_g", "ln_b", "ffn_w1", "ffn_w3", "ffn_w2", "hgrn_lb_logits")}
    for k in ("ab_w_in", "ab_w_out", "hgrn_norm_g", "mlstm_conv_w", "mlstm_conv_b", "mlstm_gate_b", "mlstm_skip",
              "mlstm_norm_g", "rglru_w_in", "rglru_conv_w", "rglru_conv_b", "rglru_ba", "rglru_bx", "rglru_lambda",
              "rglru_w_out"):
        m[k] = np.ascontiguousarray(g(k)[0])
    m["mlstm_wq_bd"] = block_diag_layout(g("mlstm_wq")[0])
    m["mlstm_wk_bd"] = block_diag_layout(g("mlstm_wk")[0])
    m["rglru_waT"] = np.ascontiguousarray(g("rglru_wa")[0].transpose(0, 2, 1))
    m["rglru_wxT"] = np.ascontiguousarray(g("rglru_wx")[0].transpose(0, 2, 1))
    m["consts"] = make_consts()
    return m


def core_inputs(inputs, b, shared=None, x_override=None):
    m = dict(shared if shared is not None else shared_inputs(inputs))
    m["x"] = np.ascontiguousarray(inputs["x"][b] if x_override is None else x_override, dtype=np.float32)
    m["c"] = np.ascontiguousarray(inputs["c"][b], dtype=np.float32)
    return m


def kernel(**inputs):
    nc = build_program()
    shared = shared_inputs(inputs)
    in_maps = [core_inputs(inputs, b, shared) for b in range(NB)]
    res = run_bass_kernel_spmd(nc, in_maps, core_ids=list(range(NB)))
    return np.stack([r["out"] for r in res.results], axis=0).astype(np.float32)
```

```python
from contextlib import ExitStack
import numpy as np
import concourse.bass as bass
import concourse.mybir as mybir
from concourse.bass_utils import run_bass_kernel_spmd

F32, BF16 = mybir.dt.float32, mybir.dt.bfloat16
AF = mybir.ActivationFunctionType
ALU = mybir.AluOpType

D = 1024
S = 4096
NB = 8
DFF = 2816
NFC = DFF // 128
NT = S // 128
ALPHA = 4.0 ** 0.25
AB_IN = 3592
LN_EPS = 1e-5


class Tok:
    __slots__ = ("sem", "val", "epoch")

    def __init__(self, sem, val, epoch):
        self.sem, self.val, self.epoch = sem, val, epoch


class DSem:
    def __init__(self, K, sem):
        self.K = K
        self.sem = sem
        self.n = 0

    def tok(self):
        return Tok(self.sem, self.n, self.K.epoch)


class Eng:
    def __init__(self, K, name):
        self.K, self.name = K, name
        self.eng = getattr(K.nc, name)
        self.sem = None
        self.n = 0
        self.waited = {}

    def wait(self, t, force=False):
        if t is None:
            return
        if isinstance(t, (list, tuple)):
            for u in t:
                self.wait(u, force)
            return
        if t.epoch < self.K.epoch and not force:
            return
        if t.val <= 0:
            return
        key = id(t.sem)
        if t.sem is self.sem and not force:
            pass
        if self.waited.get(key, 0) >= t.val:
            return
        self.waited[key] = t.val
        self.eng.wait_ge(t.sem, t.val)

    def last(self):
        return Tok(self.sem, self.n, self.K.epoch)

    def op(self, meth, *a, deps=(), **kw):
        self.wait(deps)
        ins = getattr(self.eng, meth)(*a, **kw)
        self.n += 1
        ins.then_inc(self.sem, 1)
        return Tok(self.sem, self.n, self.K.epoch)

    def dma(self, dsem, out, in_, deps=(), **kw):
        self.wait(deps)
        self.eng.dma_start(out=out, in_=in_, **kw).then_inc(dsem.sem, 16)
        dsem.n += 16
        return dsem.tok()


class Kern:
    def __init__(self, nc, nsems):
        self.nc = nc
        self.nsem = 0
        self.pfx = "s0_"
        self.dsems = {False: [], True: []}
        self.dcur = {False: 0, True: 0}
        self.epoch = 0
        self.engs = {n: Eng(self, n) for n in ("tensor", "vector", "scalar", "gpsimd", "sync")}
        for e in self.engs.values():
            e.sem = self.new_sem()
        self.pe, self.dve, self.act, self.pool_e, self.sp = (self.engs[n] for n in
                                                             ("tensor", "vector", "scalar", "gpsimd", "sync"))

    def new_sem(self):
        self.nsem += 1
        return self.nc.alloc_semaphore("s%d" % self.nsem)

    def dsem(self, sw=False):
        lst = self.dsems[sw]
        if self.dcur[sw] == len(lst):
            lst.append(DSem(self, self.new_sem()))
        d = lst[self.dcur[sw]]
        self.dcur[sw] += 1
        return d

    def barrier(self, extra=()):
        toks = [e.last() for e in self.engs.values() if e.n > 0] + list(extra)
        self.epoch += 1
        self.dcur = {False: 0, True: 0}
        for e in self.engs.values():
            e.sem = self.new_sem()
            e.n = 0
            e.waited = {}
        for e in self.engs.values():
            for t in toks:
                e.wait(t, force=True)


INORDER_NO_SELF_WAIT = ("tensor",)


class _Part:
    __slots__ = ("w", "r", "ds")

    def __init__(self):
        self.w = None
        self.r = {}
        self.ds = None


class _Ref:
    __slots__ = ("parts",)

    def __init__(self, parts):
        self.parts = parts

    @property
    def ds(self):
        return self.parts[0].ds


class Buf:
    def __init__(self, K, t, dma=False, nparts=1):
        self.t = t
        self.parts = [_Part() for _ in range(nparts)]
        if dma:
            for pt in self.parts:
                pt.ds = K.dsem(sw=(dma == "sw"))
        self.ds = self.parts[0].ds

    def p(self, i):
        return _Ref([self.parts[i]])

    def __getitem__(self, k):
        return self.t[k]


def _deps(R, W, skip_sem=None):
    d = []
    for b in R:
        for pt in b.parts:
            if pt.w is not None:
                d.append(pt.w)
    for b in W:
        for pt in b.parts:
            if pt.w is not None:
                d.append(pt.w)
            d.extend(pt.r.values())
    if skip_sem is not None:
        d = [t for t in d if t.sem is not skip_sem]
    return d


def _mark(tok, R, W):
    for b in R:
        for pt in b.parts:
            pt.r[id(tok.sem)] = tok
    for b in W:
        for pt in b.parts:
            pt.w = tok
            pt.r = {}


_REC = [None]


def record(fn, *args):
    prev = _REC[0]
    _REC[0] = lst = []
    try:
        fn(*args)
    finally:
        _REC[0] = prev
    return lst


def emit_merged(a, b):
    na, nb = len(a), len(b)
    ia = ib = 0
    while ia < na or ib < nb:
        if ib >= nb or (ia < na and ia * nb <= ib * na):
            a[ia]()
            ia += 1
        else:
            b[ib]()
            ib += 1


def merge_lists(a, b):
    out, na, nb, ia, ib = [], len(a), len(b), 0, 0
    while ia < na or ib < nb:
        if ib >= nb or (ia < na and ia * nb <= ib * na):
            out.append(a[ia])
            ia += 1
        else:
            out.append(b[ib])
            ib += 1
    return out


def do(e, meth, *a, R=(), W=(), deps=(), **kw):
    if _REC[0] is not None:
        th = lambda: _do_now(e, meth, *a, R=R, W=W, deps=deps, **kw)
        if meth == "matmul" and kw.get("start") is False and _REC[0]:
            prev = _REC[0][-1]
            _REC[0][-1] = lambda prev=prev, th=th: (prev(), th())
        else:
            _REC[0].append(th)
        return None
    return _do_now(e, meth, *a, R=R, W=W, deps=deps, **kw)


def _do_now(e, meth, *a, R=(), W=(), deps=(), **kw):
    skip = e.sem if e.name in INORDER_NO_SELF_WAIT else None
    tok = e.op(meth, *a, deps=_deps(R, W, skip) + list(deps), **kw)
    _mark(tok, R, W)
    return tok


def dodma(e, out, in_, R=(), W=(), ds=None, deps=(), **kw):
    if _REC[0] is not None:
        _REC[0].append(lambda: _dodma_now(e, out, in_, R=R, W=W, ds=ds, deps=deps, **kw))
        return None
    return _dodma_now(e, out, in_, R=R, W=W, ds=ds, deps=deps, **kw)


def _dodma_now(e, out, in_, R=(), W=(), ds=None, deps=(), **kw):
    if ds is None:
        ds = W[0].ds
    tok = e.dma(ds, out, in_, deps=_deps(R, W) + list(deps), **kw)
    _mark(tok, R, W)
    return tok


class Ring:
    def __init__(self, K, es, name, n):
        self.b = [Buf(K, es.enter_context(K.nc.psum_tensor(K.pfx + "%s%d" % (name, i), [128, 512], F32))) for i in range(n)]
        self.i = 0

    def get(self):
        b = self.b[self.i]
        self.i = (self.i + 1) % len(self.b)
        return b


def stage_prep(K, c_ap, ada_w, ada_b, ada_scr):
    nc = K.nc
    pe, dve, act, pool, sp = K.pe, K.dve, K.act, K.pool_e, K.sp
    with ExitStack() as es:
        c_sb = es.enter_context(nc.sbuf_tensor(K.pfx + "p_c", [128, 8], F32))
        c_bf = es.enter_context(nc.sbuf_tensor(K.pfx + "p_cb", [128, 8], BF16))
        w0 = es.enter_context(nc.sbuf_tensor(K.pfx + "p_w0", [128, 8, 512], BF16))
        w1 = es.enter_context(nc.sbuf_tensor(K.pfx + "p_w1", [128, 8, 512], BF16))
        b0 = es.enter_context(nc.sbuf_tensor(K.pfx + "p_b0", [1, 512], F32))
        b1 = es.enter_context(nc.sbuf_tensor(K.pfx + "p_b1", [1, 512], F32))
        r0 = es.enter_context(nc.sbuf_tensor(K.pfx + "p_r0", [1, 512], F32))
        r1 = es.enter_context(nc.sbuf_tensor(K.pfx + "p_r1", [1, 512], F32))
        ps0 = es.enter_context(nc.psum_tensor(K.pfx + "p_ps0", [1, 512], F32))
        ps1 = es.enter_context(nc.psum_tensor(K.pfx + "p_ps1", [1, 512], F32))
        wb, bb, rb, psb = [w0, w1], [b0, b1], [r0, r1], [ps0, ps1]
        dw = [K.dsem(sw=True), K.dsem(sw=True)]
        db = [K.dsem(), K.dsem()]
        do = [K.dsem(), K.dsem()]
        dc = K.dsem()
        t = sp.dma(dc, c_sb[:], c_ap.rearrange("(p k) -> p k", k=8))
        t = act.op("activation", out=c_bf[:], in_=c_sb[:], func=AF.Silu, deps=[t])
        c_tok = t
        chunks = [(l, n) for l in range(2) for n in range(18)]
        mm_done = [None, None]
        ev_done = [None, None]
        for ci, (l, n) in enumerate(chunks):
            i = ci % 2
            wsrc = ada_w[l].rearrange("(p k) n -> p k n", k=8)[:, :, n * 512:(n + 1) * 512]
            tw = pool.dma(dw[i], wb[i][:], wsrc, deps=[mm_done[i]])
            tb = sp.dma(db[i], bb[i][:], ada_b[l:l + 1, n * 512:(n + 1) * 512], deps=[ev_done[i]])
            tm = None
            for k in range(8):
                tm = pe.op("matmul", psb[i][:], lhsT=c_bf[:, k:k + 1], rhs=wb[i][:, k, :],
                           start=(k == 0), stop=(k == 7),
                           deps=[tw, c_tok, ev_done[i]] if k == 0 else ())
            mm_done[i] = tm
            te = dve.op("tensor_tensor", out=rb[i][:], in0=psb[i][:], in1=bb[i][:], op=ALU.add,
                        deps=[tm, tb, do[i].tok() if do[i].n else None])
            ev_done[i] = te
            sp.dma(do[i], ada_scr[l:l + 1, n * 512:(n + 1) * 512], rb[i][:], deps=[te])
        return [do[0].tok(), do[1].tok()]


def load_mod_vectors(K, ada_scr, l, j, shiftT, scale1T, gate_bc, weight, dsem, lng_bc, lnb_bc, ln_g, ln_b):
    nc = K.nc
    sp, dve = K.sp, K.dve
    base = j * 3 * D
    row = ada_scr[l]
    toks = []
    toks.append(sp.dma(dsem, shiftT[:], row[base:base + D].rearrange("(k p) -> p k", p=128),
                       allow_slow_non_contiguous=True))
    toks.append(sp.dma(dsem, scale1T[:], row[base + D:base + 2 * D].rearrange("(k p) -> p k", p=128),
                       allow_slow_non_contiguous=True))
    toks.append(sp.dma(dsem, gate_bc[:], row[base + 2 * D:base + 3 * D].partition_broadcast(128)))
    toks.append(sp.dma(dsem, lng_bc[:], ln_g[l, j].partition_broadcast(128)))
    toks.append(sp.dma(dsem, lnb_bc[:], ln_b[l, j].partition_broadcast(128)))
    tall = dsem.tok()
    t1 = dve.op("tensor_scalar", out=scale1T[:], in0=scale1T[:], scalar1=1.0, scalar2=None, op0=ALU.add,
                deps=[tall])
    t2 = dve.op("tensor_scalar", out=gate_bc[:], in0=gate_bc[:], scalar1=1.0, scalar2=float(weight),
                op0=ALU.add, op1=ALU.mult, deps=[tall])
    return [t1, t2, tall]


def epilogue_ln(K, ypsA, ypsB, xe, tmp, stats, mv, gate_bc, lng_bc, lnb_bc, deps_y, deps_x, tmp_free, eps_ap):
    dve, act, pool = K.dve, K.act, K.pool_e
    ycons = []
    tl = None
    for hf, yps in enumerate((ypsA, ypsB)):
        sl = slice(hf * 512, (hf + 1) * 512)
        t = dve.op("tensor_tensor", out=tmp[:, sl], in0=yps[:], in1=gate_bc[:, sl], op=ALU.mult,
                   deps=[deps_y[hf], tmp_free])
        ycons.append(t)
        t = dve.op("scalar_tensor_tensor", out=xe[:, sl], in0=xe[:, sl], scalar=float(ALPHA), in1=tmp[:, sl],
                   op0=ALU.mult, op1=ALU.add, deps=[t, deps_x])
        t = dve.op("bn_stats", out=stats[:, hf * 6:(hf + 1) * 6], in_=xe[:, sl], deps=[t])
        tl = t
    t = dve.op("bn_aggr", out=mv[:, 0:2], in_=stats[:, 0:12], deps=[tl])
    t = act.op("activation", out=mv[:, 2:3], in_=mv[:, 1:2], func=AF.Sqrt, bias=eps_ap, scale=1.0, deps=[t])
    t = dve.op("reciprocal", out=mv[:, 2:3], in_=mv[:, 2:3], deps=[t])
    t = dve.op("scalar_tensor_tensor", out=mv[:, 3:4], in0=mv[:, 0:1], scalar=-1.0, in1=mv[:, 2:3],
               op0=ALU.mult, op1=ALU.mult, deps=[t])
    t = act.op("activation", out=xe[:], in_=xe[:], func=AF.Identity, bias=mv[:, 3:4], scale=mv[:, 2:3],
               deps=[t])
    t = pool.op("tensor_tensor", out=xe[:], in0=xe[:], in1=lng_bc[:], op=ALU.mult, deps=[t])
    t = pool.op("tensor_tensor", out=xe[:], in0=xe[:], in1=lnb_bc[:], op=ALU.add, deps=[t])
    return t, ycons


def stage_ffn(K, x_in, x_out, w1, w3, w2, ada_scr, l, j, ln_g, ln_b, ident_dram):
    nc = K.nc
    pe, dve, act, pool, sp = K.pe, K.dve, K.act, K.pool_e, K.sp
    NSUP = S // 512
    with ExitStack() as es:
        w1b = es.enter_context(nc.sbuf_tensor(K.pfx + "f_w1", [128, 8, DFF], BF16))
        w3b = es.enter_context(nc.sbuf_tensor(K.pfx + "f_w3", [128, 8, DFF], BF16))
        w2b = es.enter_context(nc.sbuf_tensor(K.pfx + "f_w2", [128, NFC, D], BF16))
        xa0 = es.enter_context(nc.sbuf_tensor(K.pfx + "f_xa0", [128, D], F32))
        xa1 = es.enter_context(nc.sbuf_tensor(K.pfx + "f_xa1", [128, D], F32))
        xe0 = es.enter_context(nc.sbuf_tensor(K.pfx + "f_xe0", [128, D], F32))
        xe1 = es.enter_context(nc.sbuf_tensor(K.pfx + "f_xe1", [128, D], F32))
        xT = es.enter_context(nc.sbuf_tensor(K.pfx + "f_xT", [128, 8, 512], BF16))
        hb = es.enter_context(nc.sbuf_tensor(K.pfx + "f_h", [128, NFC, 512], BF16))
        sl0 = es.enter_context(nc.sbuf_tensor(K.pfx + "f_sl0", [128, 512], F32))
        sl1 = es.enter_context(nc.sbuf_tensor(K.pfx + "f_sl1", [128, 512], F32))
        tmp = es.enter_context(nc.sbuf_tensor(K.pfx + "f_tmp", [128, D], F32))
        tmp1 = es.enter_context(nc.sbuf_tensor(K.pfx + "f_tmp1", [128, D], F32))
        gate_bc = es.enter_context(nc.sbuf_tensor(K.pfx + "f_gate", [128, D], F32))
        lng_bc = es.enter_context(nc.sbuf_tensor(K.pfx + "f_lng", [128, D], F32))
        lnb_bc = es.enter_context(nc.sbuf_tensor(K.pfx + "f_lnb", [128, D], F32))
        shiftT = es.enter_context(nc.sbuf_tensor(K.pfx + "f_shT", [128, 8], F32))
        scale1T = es.enter_context(nc.sbuf_tensor(K.pfx + "f_scT", [128, 8], F32))
        ident = es.enter_context(nc.sbuf_tensor(K.pfx + "f_id", [128, 128], F32))
        stats = es.enter_context(nc.sbuf_tensor(K.pfx + "f_st", [128, 12], F32))
        stats1 = es.enter_context(nc.sbuf_tensor(K.pfx + "f_st1", [128, 12], F32))
        mv = es.enter_context(nc.sbuf_tensor(K.pfx + "f_mv", [128, 4], F32))
        mv1 = es.enter_context(nc.sbuf_tensor(K.pfx + "f_mv1", [128, 4], F32))
        epsc = es.enter_context(nc.sbuf_tensor(K.pfx + "f_eps", [128, 1], F32))
        p1a = es.enter_context(nc.psum_tensor(K.pfx + "f_p1a", [128, 512], F32))
        p1b = es.enter_context(nc.psum_tensor(K.pfx + "f_p1b", [128, 512], F32))
        p3a = es.enter_context(nc.psum_tensor(K.pfx + "f_p3a", [128, 512], F32))
        p3b = es.enter_context(nc.psum_tensor(K.pfx + "f_p3b", [128, 512], F32))
        yA = es.enter_context(nc.psum_tensor(K.pfx + "f_yA", [128, 512], F32))
        yB = es.enter_context(nc.psum_tensor(K.pfx + "f_yB", [128, 512], F32))
        tp0 = es.enter_context(nc.psum_tensor(K.pfx + "f_tp0", [128, 512], F32))
        tp1 = es.enter_context(nc.psum_tensor(K.pfx + "f_tp1", [128, 512], F32))
        xa, xe, slb = [xa0, xa1], [xe0, xe1], [sl0, sl1]
        p1, p3, tp = [p1a, p1b], [p3a, p3b], [tp0, tp1]
        dmod = K.dsem()
        did = K.dsem()
        dxa = [K.dsem(), K.dsem()]
        dxe = [K.dsem(), K.dsem()]
        dst = [K.dsem(sw=True), K.dsem(sw=True)]
        t_id = sp.dma(did, ident[:], ident_dram)
        t_eps = dve.op("memset", epsc[:], float(LN_EPS))
        mod_toks = load_mod_vectors(K, ada_scr, l, j, shiftT, scale1T, gate_bc, 0.5, dmod,
                                    lng_bc, lnb_bc, ln_g, ln_b)
        w1v = w1.rearrange("(k p) n -> p k n", p=128)
        w3v = w3.rearrange("(k p) n -> p k n", p=128)
        w2v = w2.rearrange("(f p) n -> p f n", p=128)
        CB = 4
        w13_tok = {}
        for g in range(0, NFC, CB):
            hi = min(NFC, g + CB)
            cs = slice(g * 128, hi * 128)
            dg = K.dsem(sw=True)
            pool.dma(dg, w1b[:, :, cs], w1v[:, :, cs])
            tb = pool.dma(dg, w3b[:, :, cs], w3v[:, :, cs])
            for f in range(g, hi):
                w13_tok[f] = tb
        w2_tok = {}
        for g in range(0, NFC, 6):
            hi = min(NFC, g + 6)
            tb = pool.dma(K.dsem(sw=True), w2b[:, g:hi, :], w2v[:, g:hi, :])
            for f in range(g, hi):
                w2_tok[f] = tb

        xin_t = x_in.rearrange("(t p) d -> t p d", p=128)
        xout_t = x_out.rearrange("(t p) d -> t p d", p=128)

        xa_free = [None, None]
        xa_ld = [None, None]
        tp_free = [None, None]
        xT_free = None
        xT_ready = None
        p13_free = [None, None]
        h_ready = None
        h_free = None
        y_free = [None, None]
        xe_free = [None, None]
        tmps, statss, mvs = [tmp, tmp1], [stats, stats1], [mv, mv1]
        tmp_free = [None, None]
        store_toks = []

        def emit_transposes(sup):
            nonlocal xT_ready
            tl = None
            for tt in range(4):
                tile_i = sup * 4 + tt
                b = tile_i % 2
                xa_ld[b] = sp.dma(dxa[b], xa[b][:], xin_t[tile_i], deps=[xa_free[b]])
                tlast = None
                for half in range(2):
                    tm = None
                    for q in range(4):
                        kc = half * 4 + q
                        tm = pe.op("transpose", tp[half][:, q * 128:(q + 1) * 128],
                                   xa[b][:, kc * 128:(kc + 1) * 128], ident[:],
                                   deps=[xa_ld[b], t_id, tp_free[half]] if q == 0 else ())
                    te = None
                    for q in range(4):
                        kc = half * 4 + q
                        te = act.op("activation", out=xT[:, kc, tt * 128:(tt + 1) * 128],
                                    in_=tp[half][:, q * 128:(q + 1) * 128], func=AF.Identity,
                                    bias=shiftT[:, kc:kc + 1], scale=scale1T[:, kc:kc + 1],
                                    deps=[tm, xT_free] + mod_toks if q == 0 else ())
                    tp_free[half] = te
                    tlast = tm
                    tl = te
                xa_free[b] = tlast
            xT_ready = tl

        emit_transposes(0)
        for sup in range(NSUP):
            tmul = None
            for fc in range(NFC):
                i = fc % 2
                cs = slice(fc * 128, (fc + 1) * 128)
                first = [w13_tok[fc], xT_ready, p13_free[i]]
                for k in range(8):
                    pe.op("matmul", p1[i][:], lhsT=w1b[:, k, cs], rhs=xT[:, k, :], start=(k == 0), stop=(k == 7),
                          deps=first if k == 0 else ())
                tm3 = None
                for k in range(8):
                    tm3 = pe.op("matmul", p3[i][:], lhsT=w3b[:, k, cs], rhs=xT[:, k, :], start=(k == 0),
                                stop=(k == 7))
                ts = act.op("activation", out=slb[i][:], in_=p1[i][:], func=AF.Silu, deps=[tm3, p13_free[i]])
                tmul = dve.op("tensor_tensor", out=hb[:, fc, :], in0=slb[i][:], in1=p3[i][:], op=ALU.mult,
                              deps=[ts, tm3, h_free])
                p13_free[i] = tmul
            h_ready = tmul
            xT_free = tm3
            if sup + 1 < NSUP:
                emit_transposes(sup + 1)
            for tt in range(4):
                tile_i = sup * 4 + tt
                b = tile_i % 2
                txe = sp.dma(dxe[b], xe[b][:], xin_t[tile_i], deps=[xe_free[b]])
                ydone = []
                for hf, yps in enumerate((yA, yB)):
                    tm = None
                    for fc in range(NFC):
                        tm = pe.op("matmul", yps[:], lhsT=hb[:, fc, tt * 128:(tt + 1) * 128],
                                   rhs=w2b[:, fc, hf * 512:(hf + 1) * 512], start=(fc == 0), stop=(fc == NFC - 1),
                                   deps=[h_ready, y_free[hf], w2_tok[fc]] if fc == 0 else [w2_tok[fc]])
                    ydone.append(tm)
                h_free = ydone[1]
                tfin, ycons = epilogue_ln(K, yA, yB, xe[b], tmps[b], statss[b], mvs[b], gate_bc, lng_bc, lnb_bc,
                                          ydone, txe, tmp_free[b], epsc[:, 0:1])
                y_free = ycons
                tmp_free[b] = tfin
                ts = pool.dma(dst[b], xout_t[tile_i], xe[b][:], deps=[tfin])
                xe_free[b] = ts
        return [dst[0].tok(), dst[1].tok()]


class MixIO:
    def __init__(self, K, es, ring, x_in, x_out, ada_scr, l, j, ln_g, ln_b, consts, TS, weight):
        nc = K.nc
        self.K, self.ring, self.TS = K, ring, TS
        sb = lambda name, shape, dt, dma=False: Buf(K, es.enter_context(nc.sbuf_tensor(K.pfx + name, shape, dt)), dma)
        self.xa = [sb("m_xa%d" % i, [128, D], F32, True) for i in range(2)]
        self.xe = [sb("m_xe%d" % i, [128, D], F32, True) for i in range(2)]
        self.st = [K.dsem(sw=True), K.dsem(sw=True)]
        self.tmp = sb("m_tmp", [128, D], F32)
        self.gate_bc = sb("m_gate", [128, D], F32, True)
        self.lng = sb("m_lng", [128, D], F32, True)
        self.lnb = sb("m_lnb", [128, D], F32, True)
        self.shT = sb("m_shT", [128, 8], F32, True)
        self.scT = sb("m_scT", [128, 8], F32, True)
        self.cst = sb("m_cst", [128, 896], F32, True)
        self.identb = sb("m_idb", [128, 128], BF16)
        self.stats = sb("m_st", [128, 12], F32)
        self.mv = sb("m_mv", [128, 4], F32)
        self.eps = sb("m_eps", [128, 2], F32)
        self.xTs = [sb("m_xT%d" % i, [128, 8, TS], BF16) for i in range(2)]
        self.xT = self.xTs[0]
        self.xin_t = x_in.rearrange("(t p) d -> t p d", p=128)
        self.xout_t = x_out.rearrange("(t p) d -> t p d", p=128)
        sp, dve = K.sp, K.dve
        self.prefetched = set()
        for ti in range(2):
            dodma(sp, self.xa[ti][:], self.xin_t[ti], W=[self.xa[ti]])
            self.prefetched.add(ti)
        dodma(sp, self.cst[:], consts, W=[self.cst])
        do(dve, "tensor_copy", out=self.identb[:], in_=self.cst[:, 0:128], R=[self.cst], W=[self.identb])
        do(dve, "memset", self.eps[:, 0:1], float(LN_EPS), W=[self.eps])
        do(dve, "memset", self.eps[:, 1:2], 1e-6, W=[self.eps])
        base = j * 3 * D
        row = ada_scr[l]
        dodma(sp, self.shT[:], row[base:base + D].rearrange("(k p) -> p k", p=128), W=[self.shT],
              allow_slow_non_contiguous=True)
        dodma(sp, self.scT[:], row[base + D:base + 2 * D].rearrange("(k p) -> p k", p=128), W=[self.scT],
              allow_slow_non_contiguous=True)
        dodma(sp, self.gate_bc[:], row[base + 2 * D:base + 3 * D].partition_broadcast(128), W=[self.gate_bc])
        dodma(sp, self.lng[:], ln_g[l, j].partition_broadcast(128), W=[self.lng])
        dodma(sp, self.lnb[:], ln_b[l, j].partition_broadcast(128), W=[self.lnb])
        do(dve, "tensor_scalar", out=self.scT[:], in0=self.scT[:], scalar1=1.0, scalar2=None, op0=ALU.add,
           R=[self.scT], W=[self.scT])
        do(dve, "tensor_scalar", out=self.gate_bc[:], in0=self.gate_bc[:], scalar1=1.0, scalar2=float(weight),
           op0=ALU.add, op1=ALU.mult, R=[self.gate_bc], W=[self.gate_bc])
        self.ident = self.cst[:, 0:128]
        self.tri = self.cst[:, 128:256]
        self.ones = self.cst[:, 256:384]
        self.rmask = self.cst[:, 384:896]

    def load_xT(self, sup, ring=None):
        K = self.K
        ring = ring if ring is not None else self.ring
        pe, act, sp = K.pe, K.act, K.sp
        xT = self.xTs[sup % 2]
        self.xT = xT
        for tt in range(self.TS // 128):
            ti = sup * (self.TS // 128) + tt
            xa = self.xa[ti % 2]
            if ti not in self.prefetched:
                dodma(sp, xa[:], self.xin_t[ti], W=[xa])
            for half in range(2):
                bk = ring.get()
                for q in range(4):
                    kc = half * 4 + q
                    do(pe, "transpose", bk[:, q * 128:(q + 1) * 128], xa[:, kc * 128:(kc + 1) * 128], self.ident,
                       R=[xa, self.cst], W=[bk])
                for q in range(4):
                    kc = half * 4 + q
                    do(act, "activation", out=xT[:, kc, tt * 128:(tt + 1) * 128],
                       in_=bk[:, q * 128:(q + 1) * 128], func=AF.Identity, bias=self.shT[:, kc:kc + 1],
                       scale=self.scT[:, kc:kc + 1], R=[bk, self.shT, self.scT], W=[xT])
        return xT

    def out_proj(self, ti, yT, wout, tcols=None):
        K, ring = self.K, self.ring
        pe, act, dve, pool, sp = K.pe, K.act, K.dve, K.pool_e, K.sp
        if tcols is None:
            tcols = slice(0, 128)
        xe = self.xe[ti % 2]
        dodma(sp, xe[:], self.xin_t[ti], W=[xe])
        ybk = [ring.get(), ring.get()]
        for hf in range(2):
            for kc in range(8):
                do(pe, "matmul", ybk[hf][:], lhsT=yT[:, kc, tcols], rhs=wout[:, kc, hf * 512:(hf + 1) * 512],
                   start=(kc == 0), stop=(kc == 7), R=[yT, wout], W=[ybk[hf]])
        tmp, stats, mv = self.tmp, self.stats, self.mv
        for hf in range(2):
            sl = slice(hf * 512, (hf + 1) * 512)
            do(dve, "tensor_tensor", out=tmp[:, sl], in0=ybk[hf][:], in1=self.gate_bc[:, sl], op=ALU.mult,
               R=[ybk[hf], self.gate_bc], W=[tmp])
            do(dve, "scalar_tensor_tensor", out=xe[:, sl], in0=xe[:, sl], scalar=float(ALPHA), in1=tmp[:, sl],
               op0=ALU.mult, op1=ALU.add, R=[tmp, xe], W=[xe])
            do(dve, "bn_stats", out=stats[:, hf * 6:(hf + 1) * 6], in_=xe[:, sl], R=[xe], W=[stats])
        do(dve, "bn_aggr", out=mv[:, 0:2], in_=stats[:, 0:12], R=[stats], W=[mv])
        do(act, "activation", out=mv[:, 2:3], in_=mv[:, 1:2], func=AF.Sqrt, bias=self.eps[:, 0:1], scale=1.0,
           R=[mv, self.eps], W=[mv])
        do(dve, "reciprocal", out=mv[:, 2:3], in_=mv[:, 2:3], R=[mv], W=[mv])
        do(dve, "scalar_tensor_tensor", out=mv[:, 3:4], in0=mv[:, 0:1], scalar=-1.0, in1=mv[:, 2:3],
           op0=ALU.mult, op1=ALU.mult, R=[mv], W=[mv])
        do(act, "activation", out=xe[:], in_=xe[:], func=AF.Identity, bias=mv[:, 3:4], scale=mv[:, 2:3],
           R=[mv, xe], W=[xe])
        do(dve, "tensor_tensor", out=xe[:], in0=xe[:], in1=self.lng[:], op=ALU.mult, R=[xe, self.lng], W=[xe])
        do(dve, "tensor_tensor", out=xe[:], in0=xe[:], in1=self.lnb[:], op=ALU.add, R=[xe, self.lnb], W=[xe])
        dodma(pool, self.xout_t[ti], xe[:], R=[xe], ds=self.st[ti % 2])

    def final_tokens(self):
        return [self.st[0].tok(), self.st[1].tok()]


def stage_rglru(K, x_in, x_out, P, ada_scr, l, ln_g, ln_b, consts):
    nc = K.nc
    pe, dve, act, pool, sp = K.pe, K.dve, K.act, K.pool_e, K.sp
    TS = 256
    NSUP = S // TS
    with ExitStack() as es:
        sb = lambda name, shape, dt, dma=False, nparts=1: Buf(
            K, es.enter_context(nc.sbuf_tensor(K.pfx + name, shape, dt)), dma, nparts)
        ringF = Ring(K, es, "r_psF", 3)
        ringA = Ring(K, es, "r_psA", 2)
        ringB = Ring(K, es, "r_psB", 2)
        ring = ringA
        io = MixIO(K, es, ring, x_in, x_out, ada_scr, l, 1, ln_g, ln_b, consts, TS, 1.0)
        win = sb("r_win", [128, 8, 2048], BF16, "sw", nparts=4)
        wout = sb("r_wout", [128, 8, D], BF16, "sw")
        waT = sb("r_waT", [128, 8, 128], BF16, "sw")
        wxT = sb("r_wxT", [128, 8, 128], BF16, "sw")
        cw = sb("r_cw", [128, 8, 4], F32, True)
        vec = sb("r_vec", [128, 4, 8], F32, True)
        csp = sb("r_csp", [128, 3, 8], F32)
        gates = [sb("r_gate%d" % i, [128, 8, TS], F32, nparts=8) for i in range(2)]
        xbrs = [sb("r_xbr%d" % i, [128, 8, TS + 3], F32, nparts=8) for i in range(2)]
        xr = sb("r_xr", [128, 8, TS], F32, nparts=8)
        xrb = sb("r_xrb", [128, 8, TS], BF16, nparts=8)
        rg = sb("r_rg", [128, 8, TS], F32, nparts=8)
        ig = sb("r_ig", [128, 8, TS], F32, nparts=8)
        aa = sb("r_aa", [128, 8, TS], F32, nparts=8)
        a2 = sb("r_a2", [128, 8, TS], F32, nparts=8)
        hs = sb("r_hs", [128, 8, TS], F32, nparts=8)
        hprev = sb("r_hp", [128, 8], F32, nparts=8)
        one_c = sb("r_one", [128, 1], F32)
        hgT = sb("r_hgT", [128, 8, TS], BF16, nparts=8)
        w_in_v = P["rglru_w_in"].rearrange("(k p) n -> p k n", p=128)
        wtok0 = None
        for g in range(4):
            cs = slice(g * 512, (g + 1) * 512)
            t_ = dodma(pool, win[:, :, cs], w_in_v[:, :, cs], W=[win.p(g)], deps=[wtok0] if wtok0 is not None else ())
            if g == 0:
                wtok0 = t_
        dodma(pool, waT[:], P["rglru_waT"].rearrange("n j i -> j n i"), W=[waT])
        dodma(pool, wxT[:], P["rglru_wxT"].rearrange("n j i -> j n i"), W=[wxT])
        dodma(pool, wout[:], P["rglru_w_out"].rearrange("(k p) n -> p k n", p=128), W=[wout])
        for jj in range(4):
            dodma(sp, cw[:, :, jj], P["rglru_conv_w"][jj].rearrange("(n p) -> p n", p=128), W=[cw],
                  allow_slow_non_contiguous=True)
        for i, nm in enumerate(("rglru_conv_b", "rglru_ba", "rglru_bx", "rglru_lambda")):
            dodma(sp, vec[:, i, :], P[nm].rearrange("(n p) -> p n", p=128), W=[vec],
                  allow_slow_non_contiguous=True)
        cb, ba, bx, lam = (vec[:, i, :] for i in range(4))
        e_ = csp[:, 0, :]
        do(act, "activation", out=e_, in_=lam, func=AF.Exp, scale=-1.0, R=[vec], W=[csp])
        do(dve, "tensor_scalar", out=csp[:, 1, :], in0=e_, scalar1=-1.0 / 3.0, scalar2=0.5, op0=ALU.mult,
           op1=ALU.add, R=[csp], W=[csp])
        do(dve, "tensor_tensor", out=csp[:, 1, :], in0=csp[:, 1, :], in1=e_, op=ALU.mult, R=[csp], W=[csp])
        do(dve, "tensor_scalar", out=csp[:, 1, :], in0=csp[:, 1, :], scalar1=-1.0, scalar2=1.0, op0=ALU.mult,
           op1=ALU.add, R=[csp], W=[csp])
        do(dve, "tensor_tensor", out=csp[:, 1, :], in0=csp[:, 1, :], in1=e_, op=ALU.mult, R=[csp], W=[csp])
        do(dve, "tensor_scalar", out=csp[:, 2, :], in0=csp[:, 1, :], scalar1=-16.0, scalar2=None, op0=ALU.mult,
           R=[csp], W=[csp])
        do(dve, "tensor_scalar", out=csp[:, 1, :], in0=csp[:, 1, :], scalar1=-8.0, scalar2=None, op0=ALU.mult,
           R=[csp], W=[csp])
        do(pool, "memset", xbrs[0][:, :, 0:3], 0.0, W=[xbrs[0]])
        do(pool, "memset", hprev[:], 0.0, W=[hprev])
        do(pool, "memset", one_c[:], 1.0, W=[one_c])

        def front(sup):
            xT = io.load_xT(sup, ringF)
            gate, xbr = gates[sup % 2], xbrs[sup % 2]
            for n in range(8):
                bk = ringF.get()
                for kc in range(8):
                    do(pe, "matmul", bk[:, 0:TS], lhsT=win[:, kc, n * 128:(n + 1) * 128], rhs=xT[:, kc, :],
                       start=(kc == 0), stop=(kc == 7), R=[win.p(n // 4), xT], W=[bk])
                do(act, "activation", out=gate[:, n, :], in_=bk[:, 0:TS], func=AF.Gelu_apprx_tanh, R=[bk], W=[gate.p(n)])
            for n in range(8):
                bk = ringF.get()
                for kc in range(8):
                    do(pe, "matmul", bk[:, 0:TS], lhsT=win[:, kc, 1024 + n * 128:1024 + (n + 1) * 128],
                       rhs=xT[:, kc, :], start=(kc == 0), stop=(kc == 7), R=[win.p(2 + n // 4), xT], W=[bk])
                do(act, "activation", out=xbr[:, n, 3:TS + 3], in_=bk[:, 0:TS], func=AF.Identity, R=[bk], W=[xbr.p(n)])

        def back_half(sup, blocks, ring):
            gate, xbr, xbr_next = gates[sup % 2], xbrs[sup % 2], xbrs[(sup + 1) % 2]
            for n in blocks:
                do(pool, "tensor_scalar", out=xr[:, n, :], in0=xbr[:, n, 3:TS + 3], scalar1=cw[:, n, 3:4],
                   scalar2=cb[:, n:n + 1], op0=ALU.mult, op1=ALU.add, R=[xbr.p(n), cw, vec], W=[xr.p(n)])
                for jj in range(3):
                    do(dve, "scalar_tensor_tensor", out=xr[:, n, :], in0=xbr[:, n, jj:jj + TS],
                       scalar=cw[:, n, jj:jj + 1], in1=xr[:, n, :], op0=ALU.mult, op1=ALU.add,
                       R=[xbr.p(n), cw, xr.p(n)], W=[xr.p(n)])
                do(pool, "tensor_copy", out=xrb[:, n, :], in_=xr[:, n, :], R=[xr.p(n)], W=[xrb.p(n)])
            for n in blocks:
                bk = ring.get()
                do(pe, "matmul", bk[:, 0:TS], lhsT=waT[:, n, :], rhs=xrb[:, n, :], start=True, stop=True,
                   R=[waT, xrb.p(n)], W=[bk])
                do(pe, "matmul", bk[:, TS:2 * TS], lhsT=wxT[:, n, :], rhs=xrb[:, n, :], start=True, stop=True,
                   R=[wxT, xrb.p(n)], W=[bk])
                do(act, "activation", out=rg[:, n, :], in_=bk[:, 0:TS], func=AF.Sigmoid, bias=ba[:, n:n + 1],
                   R=[bk, vec], W=[rg.p(n)])
                do(act, "activation", out=ig[:, n, :], in_=bk[:, TS:2 * TS], func=AF.Sigmoid, bias=bx[:, n:n + 1],
                   R=[bk, vec], W=[ig.p(n)])
                do(pool, "tensor_tensor", out=ig[:, n, :], in0=ig[:, n, :], in1=xr[:, n, :], op=ALU.mult,
                   R=[ig.p(n), xr.p(n)], W=[ig.p(n)])
            for n in blocks:
                do(act, "activation", out=aa[:, n, :], in_=rg[:, n, :], func=AF.Exp, scale=csp[:, 1, n:n + 1],
                   R=[rg.p(n), csp], W=[aa.p(n)])
                do(act, "activation", out=a2[:, n, :], in_=rg[:, n, :], func=AF.Exp, scale=csp[:, 2, n:n + 1],
                   R=[rg.p(n), csp], W=[a2.p(n)])
            for n in blocks:
                do(act, "activation", out=a2[:, n, :], in_=a2[:, n, :], func=AF.Sqrt, scale=-1.0, bias=one_c[:, 0:1],
                   R=[a2.p(n), one_c], W=[a2.p(n)])
            for n in blocks:
                do(dve, "tensor_tensor", out=ig[:, n, :], in0=ig[:, n, :], in1=a2[:, n, :], op=ALU.mult,
                   R=[ig.p(n), a2.p(n)], W=[ig.p(n)])
                do(dve, "tensor_tensor_scan", out=hs[:, n, :], data0=aa[:, n, :], data1=ig[:, n, :],
                   initial=hprev[:, n:n + 1], op0=ALU.mult, op1=ALU.add, R=[aa.p(n), ig.p(n), hprev.p(n)],
                   W=[hs.p(n)])
                do(dve, "tensor_copy", out=hprev[:, n:n + 1], in_=hs[:, n, TS - 1:TS], R=[hs.p(n)], W=[hprev.p(n)])
                do(pool, "tensor_tensor", out=hgT[:, n, :], in0=hs[:, n, :], in1=gate[:, n, :], op=ALU.mult,
                   R=[hs.p(n), gate.p(n)], W=[hgT.p(n)])

        def tail(sup):
            xbr, xbr_next = xbrs[sup % 2], xbrs[(sup + 1) % 2]
            do(pool, "tensor_copy", out=xbr_next[:, :, 0:3], in_=xbr[:, :, TS:TS + 3], R=[xbr], W=[xbr_next])
            for tt in range(TS // 128):
                io.out_proj(sup * (TS // 128) + tt, hgT, wout, slice(tt * 128, (tt + 1) * 128))

        front(0)
        for sup in range(NSUP):
            nxt = record(front, sup + 1) if sup + 1 < NSUP else []
            cur = merge_lists(record(back_half, sup, range(0, 4), ringA), record(back_half, sup, range(4, 8), ringB))
            cur += record(tail, sup)
            emit_merged(nxt, cur)
        return io.final_tokens()


def stage_ab(K, x_in, x_out, P, ada_scr, l, ln_g, ln_b, consts):
    nc = K.nc
    pe, dve, act, pool, sp = K.pe, K.dve, K.act, K.pool_e, K.sp
    TS = 256
    NSUP = S // TS
    CQ, CF, CI, CG, CX, CV, CZ, CGT = 0, 512, 1024, 1536, 2048, 2560, 3072, 3584
    with ExitStack() as es:
        sb = lambda name, shape, dt, dma=False, nparts=1: Buf(
            K, es.enter_context(nc.sbuf_tensor(K.pfx + name, shape, dt)), dma, nparts)
        ringF = Ring(K, es, "a_psF", 2)
        ringM = Ring(K, es, "a_psM", 4)
        ringH = Ring(K, es, "a_psH", 2)
        ring = ringM
        io = MixIO(K, es, ring, x_in, x_out, ada_scr, l, 1, ln_g, ln_b, consts, TS, 1.0)
        tri, ones, rmask, identb = io.tri, io.ones, io.rmask, io.identb
        cst = io.cst
        win = sb("a_win", [128, 8, AB_IN], BF16, "sw", nparts=8)
        wout = sb("a_wout", [128, 8, D], BF16, "sw")
        wst = sb("a_wst", [128, 2, 4, 128], F32, True)
        wq_b = sb("a_wq_b", [128, 4, 128], BF16)
        wk_b = sb("a_wk_b", [128, 4, 128], BF16)
        lg = sb("a_lg", [128, 3, 4], F32, True)
        lbv = sb("a_lb", [128, 2, 4], F32)
        cw = sb("a_cw", [128, 4, 4], F32, True)
        cbv = sb("a_cb", [128, 4], F32, True)
        hg_bc = sb("a_hg", [128, 512], F32, True)
        mg_bc = sb("a_mg", [128, 512], F32, True)
        sk_bc = sb("a_sk", [128, 512], F32, True)
        gb_bc = sb("a_gb", [128, 8], F32, True)
        qs = sb("a_qs", [128, 4, TS], F32)
        acc = qs
        fg = sb("a_fg", [128, 4, TS], F32)
        lfb = sb("a_lfb", [128, 4, TS], F32)
        enb = lfb
        bb = sb("a_bb", [128, 4, TS], F32)
        eb = sb("a_eb", [128, 4, TS], F32)
        ebends = [sb("a_ebe%d" % i, [128, 4, TS // 64], F32) for i in range(2)]
        qTs = [sb("a_qT%d" % i, [128, 4, TS], BF16) for i in range(2)]
        kTs = [sb("a_kT%d" % i, [128, 4, TS], BF16) for i in range(2)]
        xbx = sb("a_xbx", [128, 4, TS + 3], F32)
        xcTs = [sb("a_xcT%d" % i, [128, 4, TS], BF16) for i in range(2)]
        qbTs = [sb("a_qbT%d" % i, [128, 4, TS], BF16) for i in range(2)]
        kbTs = [sb("a_kbT%d" % i, [128, 4, TS], BF16) for i in range(2)]
        va = sb("a_va", [64, 512], BF16)
        ga = sb("a_ga", [64, 512], F32)
        vb = sb("a_vb", [128, 4, 132], BF16)
        vhat = sb("a_vhat", [128, 4, 132], BF16)
        zb = sb("a_zb", [128, 512], F32)
        kbtm = sb("a_kbtm", [128, 512], BF16)
        xctm = sb("a_xctm", [128, 512], BF16)
        gt = sb("a_gt", [128, 8], F32)
        gw = sb("a_gw", [128, 16], F32)
        ex = sb("a_ex", [128, 16], F32)
        bcum = sb("a_bcum", [128, 8], F32)
        Sst = sb("a_S", [128, 4, 128], F32)
        Sbf = sb("a_Sbf", [128, 4, 128], BF16)
        Stmp = sb("a_Stmp", [128, 4, 128], F32)
        Cst = sb("a_C", [128, 4, 132], F32)
        Cbf = sb("a_Cbf", [128, 4, 132], BF16)
        attm = sb("a_attm", [64, 4, 64], BF16)
        kTM = sb("a_kTM", [64, 512], BF16)
        osb = sb("a_osb", [64, 4, 128], F32)
        sq = sb("a_sq", [64, 4, 128], F32)
        ssv = sb("a_ss", [64, 8], F32)
        ya = sb("a_ya", [64, 512], BF16)
        AT = sb("a_AT", [128, 4, 128], BF16)
        hsb = sb("a_hsb", [128, 4, 128], F32)
        dn = sb("a_dn", [128, 16], F32)
        hst = sb("a_hst", [128, 4, 6], F32)
        hmv = sb("a_hmv", [128, 4, 2], F32)
        hrs = sb("a_hrs", [128, 8], F32)
        yb = sb("a_yb", [128, 512], BF16)
        tmp2 = sb("a_tmp2", [128, 512], F32)
        yT = sb("a_yT", [128, 8, 128], BF16, nparts=2)

        w_in_v = P["ab_w_in"].rearrange("(k p) n -> p k n", p=128)
        wtok = {}
        for c0, after in ((CQ, None), (CF, CQ), (CX, CQ), (CI, CX), (CG, CX), (CV, CX), (CZ, CX)):
            wtok[c0] = dodma(pool, win[:, :, c0:c0 + 512], w_in_v[:, :, c0:c0 + 512], W=[win.p(c0 // 512)],
                             deps=[wtok[after]] if after is not None else ())
        dodma(pool, win[:, :, CGT:CGT + 8], w_in_v[:, :, CGT:CGT + 8], W=[win.p(CGT // 512)], deps=[wtok[CX]])
        dodma(pool, wout[:], P["ab_w_out"].rearrange("(k p) n -> p k n", p=128), W=[wout])
        dodma(sp, wst[:, 0, :, :], P["mlstm_wq_bd"].rearrange("h p f -> p h f"), W=[wst])
        dodma(sp, wst[:, 1, :, :], P["mlstm_wk_bd"].rearrange("h p f -> p h f"), W=[wst])
        do(dve, "tensor_copy", out=wq_b[:], in_=wst[:, 0, :, :], R=[wst], W=[wq_b])
        do(dve, "tensor_scalar", out=wk_b[:], in0=wst[:, 1, :, :], scalar1=float(128.0 ** -0.5),
           scalar2=None, op0=ALU.mult, R=[wst], W=[wk_b])
        for li_ in range(3):
            dodma(sp, lg[:, li_, :], P["hgrn_lb_logits"][li_].rearrange("(h p) -> p h", p=128), W=[lg],
                  allow_slow_non_contiguous=True)
        for jj in range(4):
            dodma(sp, cw[:, :, jj], P["mlstm_conv_w"][jj].rearrange("(h p) -> p h", p=128), W=[cw],
                  allow_slow_non_contiguous=True)
        dodma(sp, cbv[:], P["mlstm_conv_b"].rearrange("(h p) -> p h", p=128), W=[cbv],
              allow_slow_non_contiguous=True)
        dodma(sp, hg_bc[:], P["hgrn_norm_g"].partition_broadcast(128), W=[hg_bc])
        dodma(sp, mg_bc[:], P["mlstm_norm_g"].partition_broadcast(128), W=[mg_bc])
        dodma(sp, sk_bc[:], P["mlstm_skip"].partition_broadcast(128), W=[sk_bc])
        dodma(sp, gb_bc[:], P["mlstm_gate_b"].partition_broadcast(128), W=[gb_bc])
        do(act, "activation", out=lg[:], in_=lg[:], func=AF.Exp, R=[lg], W=[lg])
        do(dve, "tensor_tensor", out=lbv[:, 1, :], in0=lg[:, 0, :], in1=lg[:, 1, :], op=ALU.add, R=[lg], W=[lbv])
        do(dve, "tensor_tensor", out=lbv[:, 1, :], in0=lbv[:, 1, :], in1=lg[:, 2, :], op=ALU.add, R=[lg, lbv],
           W=[lbv])
        do(dve, "reciprocal", out=lbv[:, 1, :], in_=lbv[:, 1, :], R=[lbv], W=[lbv])
        do(dve, "tensor_tensor", out=lbv[:, 0, :], in0=lg[:, 0, :], in1=lbv[:, 1, :], op=ALU.mult, R=[lg, lbv],
           W=[lbv])
        do(dve, "tensor_scalar", out=lbv[:, 1, :], in0=lbv[:, 0, :], scalar1=-1.0, scalar2=1.0, op0=ALU.mult,
           op1=ALU.add, R=[lbv], W=[lbv])
        do(pool, "memset", Sst[:], 0.0, W=[Sst])
        do(pool, "memset", Sbf[:], 0.0, W=[Sbf])
        do(pool, "memset", Cst[:], 0.0, W=[Cst])
        do(pool, "memset", Cbf[:], 0.0, W=[Cbf])
        do(pool, "memset", vb[:], 1.0, W=[vb])
        do(pool, "memset", xbx[:, :, 0:3], 0.0, W=[xbx])

        def proj_fm(xT, col0, h, bk):
            for kc in range(8):
                do(pe, "matmul", bk[:, 0:TS], lhsT=win[:, kc, col0 + h * 128:col0 + (h + 1) * 128], rhs=xT[:, kc, :],
                   start=(kc == 0), stop=(kc == 7), R=[win.p(col0 // 512), xT], W=[bk])

        def proj_tm(xT, col0, ncol, tcols, bk, m=128):
            for kc in range(8):
                do(pe, "matmul", bk[0:m, 0:ncol], lhsT=xT[:, kc, tcols], rhs=win[:, kc, col0:col0 + ncol],
                   start=(kc == 0), stop=(kc == 7), R=[win.p(col0 // 512), xT], W=[bk])


        def front(sup):
            p_ = sup % 2
            qT, kT, xcT, qbT, kbT, ebe = qTs[p_], kTs[p_], xcTs[p_], qbTs[p_], kbTs[p_], ebends[p_]
            xT = io.load_xT(sup, ringF)
            for h in range(4):
                bk = ringF.get()
                proj_fm(xT, CQ, h, bk)
                do(act, "activation", out=qs[:, h, :], in_=bk[:, 0:TS], func=AF.Silu, R=[bk], W=[qs])
            for h in range(4):
                bk = ringF.get()
                proj_fm(xT, CF, h, bk)
                do(act, "activation", out=fg[:, h, :], in_=bk[:, 0:TS], func=AF.Sigmoid, R=[bk], W=[fg])
            for h in range(4):
                bk = ringF.get()
                proj_fm(xT, CX, h, bk)
                do(act, "activation", out=xbx[:, h, 3:TS + 3], in_=bk[:, 0:TS], func=AF.Identity, R=[bk], W=[xbx])
            for h in range(4):
                do(dve, "tensor_scalar", out=fg[:, h, :], in0=fg[:, h, :], scalar1=lbv[:, 1, h:h + 1],
                   scalar2=lbv[:, 0, h:h + 1], op0=ALU.mult, op1=ALU.add, R=[fg, lbv], W=[fg])
            do(act, "activation", out=lfb[:], in_=fg[:], func=AF.Ln, R=[fg], W=[lfb])
            for h in range(4):
                do(dve, "tensor_tensor_scan", out=bb[:, h, :], data0=rmask[:, 0:TS], data1=lfb[:, h, :], initial=0.0,
                   op0=ALU.mult, op1=ALU.add, R=[cst, lfb], W=[bb])
            do(act, "activation", out=eb[:], in_=bb[:], func=AF.Exp, R=[bb], W=[eb])
            do(act, "activation", out=enb[:], in_=bb[:], func=AF.Exp, scale=-1.0, R=[bb], W=[enb])
            for cq_ in range(TS // 64):
                do(dve, "tensor_copy", out=ebe[:, :, cq_], in_=eb[:, :, cq_ * 64 + 63], R=[eb], W=[ebe])
            do(dve, "tensor_tensor", out=qT[:], in0=qs[:], in1=eb[:], op=ALU.mult, R=[qs, eb], W=[qT])
            do(pool, "tensor_scalar", out=fg[:], in0=fg[:], scalar1=-1.0, scalar2=1.0, op0=ALU.mult, op1=ALU.add,
               R=[fg], W=[fg])
            do(pool, "tensor_tensor", out=kT[:], in0=fg[:], in1=enb[:], op=ALU.mult, R=[fg, enb], W=[kT])
            for h in range(4):
                do(dve, "tensor_scalar", out=acc[:, h, :], in0=xbx[:, h, 3:TS + 3], scalar1=cw[:, h, 3:4],
                   scalar2=None, op0=ALU.mult, R=[xbx, cw], W=[acc])
                for jj in range(3):
                    do(dve, "scalar_tensor_tensor", out=acc[:, h, :], in0=xbx[:, h, jj:jj + TS],
                       scalar=cw[:, h, jj:jj + 1], in1=acc[:, h, :], op0=ALU.mult, op1=ALU.add,
                       R=[xbx, cw, acc], W=[acc])
            for h in range(4):
                do(act, "activation", out=xcT[:, h, :], in_=acc[:, h, :], func=AF.Silu, bias=cbv[:, h:h + 1],
                   R=[acc, cbv], W=[xcT])
            do(pool, "tensor_copy", out=xbx[:, :, 0:3], in_=xbx[:, :, TS:TS + 3], R=[xbx], W=[xbx])
            for h in range(4):
                bk = ringF.get()
                do(pe, "matmul", bk[:, 0:TS], lhsT=wq_b[:, h, :], rhs=xcT[:, h, :], start=True, stop=True,
                   R=[wq_b, xcT], W=[bk])
                do(pe, "matmul", bk[:, TS:2 * TS], lhsT=wk_b[:, h, :], rhs=xcT[:, h, :], start=True, stop=True,
                   R=[wk_b, xcT], W=[bk])
                do(act, "activation", out=qbT[:, h, :], in_=bk[:, 0:TS], func=AF.Identity, R=[bk], W=[qbT])
                do(act, "activation", out=kbT[:, h, :], in_=bk[:, TS:2 * TS], func=AF.Identity, R=[bk], W=[kbT])


        def M(sup, tt):
            p_ = sup % 2
            qT, kT, xcT, qbT, kbT, ebe = qTs[p_], kTs[p_], xcTs[p_], qbTs[p_], kbTs[p_], ebends[p_]
            xT = io.xTs[p_]
            ti = sup * (TS // 128) + tt
            ts = slice(tt * 128, (tt + 1) * 128)
            bk = ringM.get()
            proj_tm(xT, CV, 512, ts, bk)
            do(act, "activation", out=vb[:, :, 0:128], in_=bk[:, 0:512].rearrange("p (h d) -> p h d", h=4),
               func=AF.Identity, R=[bk], W=[vb])
            bk = ringM.get()
            proj_tm(xT, CZ, 512, ts, bk)
            do(act, "activation", out=zb[:], in_=bk[:, 0:512], func=AF.Silu, R=[bk], W=[zb])
            bk = ringM.get()
            proj_tm(xT, CGT, 8, ts, bk)
            do(dve, "tensor_tensor", out=gt[:], in0=bk[:, 0:8], in1=gb_bc[:], op=ALU.add, R=[bk, gb_bc], W=[gt])
            bkb = ringM.get()
            bkb_bf = bkb[:, :].bitcast(BF16)
            for h in range(4):
                do(pe, "transpose", bkb_bf[:, h * 128:(h + 1) * 128], xcT[:, h, ts], identb[:],
                   R=[xcT, identb], W=[bkb])
            do(act, "activation", out=xctm[:], in_=bkb_bf[:, 0:512], func=AF.Identity, R=[bkb], W=[xctm])
            bk = ringM.get()
            for h in range(4):
                do(pe, "matmul", bk[:, h * 128:(h + 1) * 128], lhsT=xcT[:, h, ts], rhs=wk_b[:, h, :],
                   start=True, stop=True, R=[xcT, wk_b], W=[bk])
            do(dve, "tensor_copy", out=kbtm[:], in_=bk[:, 0:512], R=[bk], W=[kbtm])
            do(act, "activation", out=gw[:, 0:4], in_=gt[:, 4:8], func=AF.Exp, scale=-1.0, R=[gt], W=[gw])
            do(act, "activation", out=gw[:, 0:4], in_=gw[:, 0:4], func=AF.Ln, bias=io.eps[:, 1:2] if False else 1.0,
               R=[gw], W=[gw])
            do(dve, "tensor_scalar", out=gw[:, 4:8], in0=gw[:, 0:4], scalar1=-1.0, scalar2=None, op0=ALU.mult,
               R=[gw], W=[gw])
            bk = ringM.get()
            do(pe, "matmul", bk[:, 0:4], lhsT=tri, rhs=gw[:, 4:8], start=True, stop=True, R=[cst, gw], W=[bk])
            do(pe, "matmul", bk[:, 4:8], lhsT=ones, rhs=gw[:, 4:8], start=True, stop=True, R=[cst, gw], W=[bk])
            do(dve, "tensor_copy", out=bcum[:], in_=bk[:, 0:8], R=[bk], W=[bcum])
            do(dve, "tensor_copy", out=gw[:, 8:16], in_=bcum[:], R=[bcum], W=[gw])
            do(dve, "tensor_tensor", out=dn[:, 0:4], in0=gt[:, 0:4], in1=bcum[:, 0:4], op=ALU.subtract,
               R=[gt, bcum], W=[dn])
            do(dve, "tensor_tensor", out=dn[:, 4:8], in0=dn[:, 0:4], in1=bcum[:, 4:8], op=ALU.add,
               R=[dn, bcum], W=[dn])
            do(act, "activation", out=ex[:, 0:8], in_=gw[:, 8:16], func=AF.Exp, R=[gw], W=[ex])
            do(act, "activation", out=ex[:, 8:16], in_=dn[:, 0:8], func=AF.Exp, R=[dn], W=[ex])
            ebt, dec, wsc, usc = (lambda o: (lambda h: ex[:, o + h:o + h + 1]))(0), \
                (lambda h: ex[:, 4 + h:5 + h]), (lambda h: ex[:, 8 + h:9 + h]), (lambda h: ex[:, 12 + h:13 + h])
            bk = ringM.get()
            for h in range(4):
                do(pe, "matmul", bk[:, h * 128:(h + 1) * 128], lhsT=kbT[:, h, ts], rhs=qbT[:, h, ts],
                   start=True, stop=True, R=[kbT, qbT], W=[bk])
            for h in range(4):
                do(dve, "scalar_tensor_tensor", out=AT[:, h, :], in0=bk[:, h * 128:(h + 1) * 128], scalar=wsc(h),
                   in1=tri, op0=ALU.mult, op1=ALU.mult, R=[bk, ex, cst], W=[AT])
                do(dve, "tensor_scalar", out=vhat[:, h, 0:129], in0=vb[:, h, 0:129], scalar1=usc(h),
                   scalar2=None, op0=ALU.mult, R=[vb, ex], W=[vhat])
            rb = [ringM.get(), ringM.get()]
            for h in range(4):
                o = (h % 2) * 129
                do(pe, "matmul", rb[h // 2][:, o:o + 129], lhsT=AT[:, h, :], rhs=vb[:, h, 0:129], start=True,
                   stop=False, R=[AT, vb], W=[rb[h // 2]])
                do(pe, "matmul", rb[h // 2][:, o:o + 129], lhsT=qbT[:, h, ts], rhs=Cbf[:, h, 0:129], start=False,
                   stop=True, R=[qbT, Cbf], W=[rb[h // 2]])
            cbk = [ringM.get(), ringM.get()]
            for h in range(4):
                o = (h % 2) * 129
                do(pe, "matmul", cbk[h // 2][:, o:o + 129], lhsT=kbtm[:, h * 128:(h + 1) * 128],
                   rhs=vhat[:, h, 0:129], start=True, stop=True, R=[kbtm, vhat], W=[cbk[h // 2]])
            for g in range(2):
                do(dve, "tensor_tensor", out=dn[:, 8 + 2 * g:10 + 2 * g],
                   in0=rb[g][:, 0:258].rearrange("p (h d) -> p h d", h=2)[:, :, 128],
                   in1=ex[:, 2 * g:2 * g + 2], op=ALU.mult, R=[rb[g], ex], W=[dn])
            do(dve, "tensor_scalar", out=gw[:, 0:4], in0=dn[:, 8:12], scalar1=-1.0, scalar2=1.0, op0=ALU.mult,
               op1=ALU.max, R=[dn], W=[gw])
            do(dve, "tensor_tensor", out=dn[:, 8:12], in0=dn[:, 8:12], in1=gw[:, 0:4], op=ALU.max, R=[dn, gw],
               W=[dn])
            do(dve, "reciprocal", out=dn[:, 8:12], in_=dn[:, 8:12], R=[dn], W=[dn])
            do(dve, "tensor_tensor", out=dn[:, 12:16], in0=dn[:, 8:12], in1=ex[:, 0:4], op=ALU.mult, R=[dn, ex],
               W=[dn])
            for h in range(4):
                o = (h % 2) * 129
                do(act, "activation", out=hsb[:, h, :], in_=rb[h // 2][:, o:o + 128], func=AF.Copy,
                   scale=dn[:, 12 + h:13 + h], R=[rb[h // 2], dn], W=[hsb])
            for h in range(4):
                o = (h % 2) * 129
                do(dve, "scalar_tensor_tensor", out=Cst[:, h, 0:129], in0=Cst[:, h, 0:129], scalar=dec(h),
                   in1=cbk[h // 2][:, o:o + 129], op0=ALU.mult, op1=ALU.add, R=[Cst, ex, cbk[h // 2]], W=[Cst])
            do(pool, "tensor_copy", out=Cbf[:], in_=Cst[:], R=[Cst], W=[Cbf])
            for h in range(4):
                do(dve, "bn_stats", out=hst[:, h, :], in_=hsb[:, h, :], R=[hsb], W=[hst])
            for h in range(4):
                do(dve, "bn_aggr", out=hmv[:, h, :], in_=hst[:, h, :], R=[hst], W=[hmv])
            do(act, "activation", out=hrs[:, 0:4], in_=hmv[:, :, 1], func=AF.Sqrt, bias=io.eps[:, 1:2], scale=1.0,
               R=[hmv, io.eps], W=[hrs])
            do(dve, "reciprocal", out=hrs[:, 0:4], in_=hrs[:, 0:4], R=[hrs], W=[hrs])
            do(dve, "scalar_tensor_tensor", out=hrs[:, 4:8], in0=hmv[:, :, 0], scalar=-1.0, in1=hrs[:, 0:4],
               op0=ALU.mult, op1=ALU.mult, R=[hmv, hrs], W=[hrs])
            for h in range(4):
                do(act, "activation", out=hsb[:, h, :], in_=hsb[:, h, :], func=AF.Identity,
                   bias=hrs[:, 4 + h:5 + h], scale=hrs[:, h:h + 1], R=[hsb, hrs], W=[hsb])
            hflat = hsb[:, :, :].rearrange("p h d -> p (h d)")
            do(dve, "tensor_tensor", out=hflat, in0=hflat, in1=mg_bc[:], op=ALU.mult, R=[hsb, mg_bc], W=[hsb])
            do(dve, "tensor_tensor", out=tmp2[:], in0=xctm[:], in1=sk_bc[:], op=ALU.mult, R=[xctm, sk_bc],
               W=[tmp2])
            do(dve, "tensor_tensor", out=hflat, in0=hflat, in1=tmp2[:], op=ALU.add, R=[hsb, tmp2], W=[hsb])
            do(dve, "tensor_tensor", out=yb[:], in0=hflat, in1=zb[:], op=ALU.mult, R=[hsb, zb], W=[yb])
            bkb = ringM.get()
            bkb_bf = bkb[:, :].bitcast(BF16)
            for h in range(4):
                do(pe, "transpose", bkb_bf[:, h * 128:(h + 1) * 128], yb[:, h * 128:(h + 1) * 128], identb[:],
                   R=[yb, identb], W=[bkb])
            do(act, "activation", out=yT[:, 4:8, :], in_=bkb_bf[:, 0:512].rearrange("p (h d) -> p h d", h=4),
               func=AF.Identity, R=[bkb], W=[yT.p(1)])

        def H(sup, tt):
            p_ = sup % 2
            qT, kT, xcT, qbT, kbT, ebe = qTs[p_], kTs[p_], xcTs[p_], qbTs[p_], kbTs[p_], ebends[p_]
            xT = io.xTs[p_]
            ti = sup * (TS // 128) + tt
            ts = slice(tt * 128, (tt + 1) * 128)
            for c in range(2):
                cs = slice(tt * 128 + c * 64, tt * 128 + (c + 1) * 64)
                cq = tt * 2 + c
                bk = ringH.get()
                proj_tm(xT, CI, 512, cs, bk, m=64)
                do(act, "activation", out=va[:], in_=bk[0:64, 0:512], func=AF.Identity, R=[bk], W=[va])
                bk = ringH.get()
                proj_tm(xT, CG, 512, cs, bk, m=64)
                do(act, "activation", out=ga[:], in_=bk[0:64, 0:512], func=AF.Silu, R=[bk], W=[ga])
                bk = ringH.get()
                for h in range(4):
                    do(pe, "matmul", bk[0:64, h * 64:(h + 1) * 64], lhsT=kT[:, h, cs], rhs=qT[:, h, cs],
                       start=True, stop=True, R=[kT, qT], W=[bk])
                do(dve, "tensor_tensor", out=attm[:], in0=bk[0:64, 0:256].rearrange("p (h d) -> p h d", h=4),
                   in1=cst[0:64, 128:192].unsqueeze(1).to_broadcast([64, 4, 64]), op=ALU.mult, R=[bk, cst], W=[attm])
                bkb = ringH.get()
                bkb_bf = bkb[:, :].bitcast(BF16)
                for h in range(4):
                    do(pe, "transpose", bkb_bf[0:64, h * 128:(h + 1) * 128], kT[:, h, cs], identb[:],
                       R=[kT, identb], W=[bkb])
                do(act, "activation", out=kTM[:], in_=bkb_bf[0:64, 0:512], func=AF.Identity, R=[bkb], W=[kTM])
                obk = ringH.get()
                for h in range(4):
                    do(pe, "matmul", obk[0:64, h * 128:(h + 1) * 128], lhsT=attm[:, h, :],
                       rhs=va[:, h * 128:(h + 1) * 128], start=True, stop=False, R=[attm, va], W=[obk])
                    do(pe, "matmul", obk[0:64, h * 128:(h + 1) * 128], lhsT=qT[:, h, cs], rhs=Sbf[:, h, :],
                       start=False, stop=True, R=[qT, Sbf], W=[obk])
                sbk = ringH.get()
                for h in range(4):
                    do(pe, "matmul", sbk[:, h * 128:(h + 1) * 128], lhsT=kTM[:, h * 128:(h + 1) * 128],
                       rhs=va[:, h * 128:(h + 1) * 128], start=True, stop=True, R=[kTM, va], W=[sbk])
                do(dve, "tensor_tensor", out=Stmp[:], in0=Sst[:], in1=sbk[:, 0:512].rearrange("p (h d) -> p h d", h=4),
                   op=ALU.add, R=[Sst, sbk], W=[Stmp])
                do(dve, "tensor_tensor", out=Sst[:], in0=Stmp[:], in1=ebe[:, :, cq:cq + 1].to_broadcast([128, 4, 128]),
                   op=ALU.mult, R=[Stmp, ebe], W=[Sst])
                do(pool, "tensor_copy", out=Sbf[:], in_=Sst[:], R=[Sst], W=[Sbf])
                do(act, "activation", out=osb[:], in_=obk[0:64, 0:512].rearrange("p (h d) -> p h d", h=4),
                   func=AF.Identity, R=[obk], W=[osb])
                do(dve, "tensor_tensor", out=sq[:], in0=osb[:], in1=osb[:], op=ALU.mult, R=[osb], W=[sq])
                do(dve, "tensor_reduce", out=ssv[:, 0:4], in_=sq[:], axis=mybir.AxisListType.X, op=ALU.add,
                   R=[sq], W=[ssv])
                do(act, "activation", out=ssv[:, 4:8], in_=ssv[:, 0:4], func=AF.Sqrt, bias=io.eps[0:64, 1:2],
                   scale=1.0 / 128.0, R=[ssv, io.eps], W=[ssv])
                do(dve, "reciprocal", out=ssv[:, 4:8], in_=ssv[:, 4:8], R=[ssv], W=[ssv])
                do(dve, "tensor_tensor", out=osb[:], in0=osb[:], in1=ssv[:, 4:8].unsqueeze(2).to_broadcast([64, 4, 128]),
                   op=ALU.mult, R=[osb, ssv], W=[osb])
                oflat = osb[:, :, :].rearrange("p h d -> p (h d)")
                do(dve, "tensor_tensor", out=oflat, in0=oflat, in1=hg_bc[0:64, :], op=ALU.mult, R=[osb, hg_bc],
                   W=[osb])
                do(dve, "tensor_tensor", out=ya[:], in0=oflat, in1=ga[:], op=ALU.mult, R=[osb, ga], W=[ya])
                bkb = ringH.get()
                bkb_bf = bkb[:, :].bitcast(BF16)
                for h in range(4):
                    do(pe, "transpose", bkb_bf[:, h * 64:(h + 1) * 64], ya[:, h * 128:(h + 1) * 128],
                       identb[0:64, 0:64], R=[ya, identb], W=[bkb])
                do(act, "activation", out=yT[:, 0:4, c * 64:(c + 1) * 64],
                   in_=bkb_bf[:, 0:256].rearrange("p (h d) -> p h d", h=4), func=AF.Identity, R=[bkb], W=[yT.p(0)])

        def O(sup, tt):
            io.out_proj(sup * (TS // 128) + tt, yT, wout)

        front(0)
        for sup in range(NSUP):
            nxt = record(front, sup + 1) if sup + 1 < NSUP else []
            cur = []
            for tt in range(TS // 128):
                cur += merge_lists(record(M, sup, tt), record(H, sup, tt))
                cur += record(O, sup, tt)
            emit_merged(nxt, cur)
        return io.final_tokens()


ALL_STAGES = ("prep", "ffn00", "ab", "ffn02", "ffn10", "rglru", "ffn12")

PARAM_SHAPES = {
    "ada_w": [2, D, 9 * D], "ada_b": [2, 9 * D], "ln_g": [2, 3, D], "ln_b": [2, 3, D],
    "ffn_w1": [2, 2, D, DFF], "ffn_w3": [2, 2, D, DFF], "ffn_w2": [2, 2, DFF, D],
    "hgrn_lb_logits": [3, 512], "ab_w_in": [D, AB_IN], "ab_w_out": [D, D], "hgrn_norm_g": [512],
    "mlstm_conv_w": [4, 512], "mlstm_conv_b": [512], "mlstm_wq_bd": [4, 128, 128], "mlstm_wk_bd": [4, 128, 128],
    "mlstm_gate_b": [8], "mlstm_skip": [512], "mlstm_norm_g": [512],
    "rglru_w_in": [D, 2 * D], "rglru_conv_w": [4, D], "rglru_conv_b": [D], "rglru_waT": [8, 128, 128],
    "rglru_ba": [D], "rglru_wxT": [8, 128, 128], "rglru_bx": [D], "rglru_lambda": [D], "rglru_w_out": [D, D],
    "consts": [128, 896],
}


def build_program(stages=ALL_STAGES):
    nc = bass.Bass("TRN2", target_bir_lowering=False)
    dt = lambda name, shape, kind="ExternalInput": nc.dram_tensor(name, list(shape), F32, kind=kind).ap()
    x = dt("x", [S, D])
    c = dt("c", [D])
    P = {k: dt(k, shp) for k, shp in PARAM_SHAPES.items()}
    out = dt("out", [S, D], kind="ExternalOutput")
    ada_scr = dt("ada_scr", [2, 9 * D], kind="Internal")
    xs = [dt("xs%d" % i, [S, D], kind="Internal") for i in range(2)]
    ident = P["consts"][:, 0:128]

    K = Kern(nc, 0)
    cur = x
    nxt = 0
    for si, st in enumerate(stages):
        last = si == len(stages) - 1
        K.pfx = "s%d_" % si
        if st == "prep":
            toks = stage_prep(K, c, P["ada_w"], P["ada_b"], ada_scr)
        else:
            dst = out if last else xs[nxt]
            if st.startswith("ffn"):
                l, j = int(st[3]), int(st[4])
                f = 0 if j == 0 else 1
                toks = stage_ffn(K, cur, dst, P["ffn_w1"][l, f], P["ffn_w3"][l, f], P["ffn_w2"][l, f], ada_scr,
                                 l, j, P["ln_g"], P["ln_b"], ident)
            elif st == "ab":
                toks = stage_ab(K, cur, dst, P, ada_scr, 0, P["ln_g"], P["ln_b"], P["consts"])
            elif st == "rglru":
                toks = stage_rglru(K, cur, dst, P, ada_scr, 1, P["ln_g"], P["ln_b"], P["consts"])
            cur = dst
            nxt ^= 1
        K.barrier(toks)
    return nc


def make_consts():
    cst = np.zeros((128, 896), np.float32)
    cst[:, 0:128] = np.eye(128, dtype=np.float32)
    cst[:, 128:256] = np.triu(np.ones((128, 128), np.float32))
    cst[:, 256:384] = 1.0
    rm = np.ones(512, np.float32)
    rm[0::64] = 0.0
    cst[:, 384:896] = rm[None, :]
    return cst


def block_diag_layout(w):
    o = np.zeros((4, 128, 128), np.float32)
    for h in range(4):
        for n in range(32):
            o[h, 4 * n:4 * n + 4, 4 * n:4 * n + 4] = w[h * 32 + n].T
    return o


def shared_inputs(inputs):
    g = lambda k: np.ascontiguousarray(np.asarray(inputs[k], dtype=np.float32))
    m = {k: g(k) for k in ("ada_w", "ada_b", "ln_g", "ln_b", "ffn_w1", "ffn_w3", "ffn_w2", "hgrn_lb_logits")}
    for k in ("ab_w_in", "ab_w_out", "hgrn_norm_g", "mlstm_conv_w", "mlstm_conv_b", "mlstm_gate_b", "mlstm_skip",
              "mlstm_norm_g", "rglru_w_in", "rglru_conv_w", "rglru_conv_b", "rglru_ba", "rglru_bx", "rglru_lambda",
              "rglru_w_out"):
        m[k] = np.ascontiguousarray(g(k)[0])
    m["mlstm_wq_bd"] = block_diag_layout(g("mlstm_wq")[0])
    m["mlstm_wk_bd"] = block_diag_layout(g("mlstm_wk")[0])
    m["rglru_waT"] = np.ascontiguousarray(g("rglru_wa")[0].transpose(0, 2, 1))
    m["rglru_wxT"] = np.ascontiguousarray(g("rglru_wx")[0].transpose(0, 2, 1))
    m["consts"] = make_consts()
    return m


def core_inputs(inputs, b, shared=None, x_override=None):
    m = dict(shared if shared is not None else shared_inputs(inputs))
    m["x"] = np.ascontiguousarray(inputs["x"][b] if x_override is None else x_override, dtype=np.float32)
    m["c"] = np.ascontiguousarray(inputs["c"][b], dtype=np.float32)
    return m


def kernel(**inputs):
    nc = build_program()
    shared = shared_inputs(inputs)
    in_maps = [core_inputs(inputs, b, shared) for b in range(NB)]
    res = run_bass_kernel_spmd(nc, in_maps, core_ids=list(range(NB)))
    return np.stack([r["out"] for r in res.results], axis=0).astype(np.float32)
```
